# Optimizing a Trainium2 kernel written in Bass

```python
import math
import jax, jax.numpy as jnp
from jax import lax
import numpy as np

D_MODEL = 1024
BATCH = 4
SEQ = 4096
DEPTH = 2
DEC_BATCH = 32
DEC_SEQ = 32
PAST_LEN = 1024

CHUNK = 64
N_MIXERS = 2
N_SSM_LAYERS = (DEPTH + 1) // 2
N_RET_LAYERS = DEPTH // 2
SSM_GROUP = 16
SSM_GROUPS = D_MODEL // SSM_GROUP
SSM_STATE = 64
SSM_DT_MIN = 1e-3
SSM_DT_MAX = 1e-1
RET_HEADS = 4
RET_DK = D_MODEL // RET_HEADS
RET_DV = 2 * D_MODEL // RET_HEADS
RET_QK = RET_HEADS * RET_DK
RET_V = RET_HEADS * RET_DV
ROPE_BASE = 10000.0
N_MEM = 256
MEM_HEADS = 4
MEM_HD = D_MODEL // MEM_HEADS
D_FF = 2816
CONV_W = 3
EPS = 1e-6
GN_EPS = 1e-5

kernel_name = 'hybrid_s5_retention_stream_step'

F32 = jnp.float32


def rms_norm(x, g):
    xf = x.astype(F32)
    y = xf * lax.rsqrt(jnp.mean(xf * xf, axis=-1, keepdims=True) + EPS)
    return (y * g.astype(F32)).astype(x.dtype)


def to_chunks(a, c):
    b, l = a.shape[0], a.shape[1]
    return jnp.moveaxis(a.reshape((b, l // c, c) + a.shape[2:]), 1, 0)


def from_chunks(a):
    a = jnp.moveaxis(a, 0, 1)
    return a.reshape((a.shape[0], a.shape[1] * a.shape[2]) + a.shape[3:])


def s5_discretise(a_re, a_im, log_dt, b_re, b_im, c_re, c_im):
    lam = lax.complex(a_re.astype(F32), a_im.astype(F32))
    dt = jnp.exp(log_dt.astype(F32))[:, None]
    a_bar = jnp.exp(lam * dt)
    b = lax.complex(b_re.astype(F32), b_im.astype(F32))
    b_bar = ((a_bar - 1.0) / lam)[..., None] * b
    c = lax.complex(c_re.astype(F32), c_im.astype(F32))
    return a_bar, b_bar, c


def _linear_combine(e1, e2):
    a1, b1 = e1
    a2, b2 = e2
    return a2 * a1, a2 * b1 + b2


def s5_block(state, u, a_bar, b_bar, c):
    bu = jnp.einsum('gpc,blgc->blgp', b_bar, u.astype(jnp.complex64))
    bu = bu.at[:, 0].add(a_bar * state)
    a = jnp.broadcast_to(a_bar, bu.shape)
    _, xs = lax.associative_scan(_linear_combine, (a, bu), axis=1)
    y = jnp.real(jnp.einsum('gcp,blgp->blgc', c, xs))
    return xs[:, -1], y


def s5_mixer(h, st_re, st_im, chunk, a_re, a_im, log_dt, b_re, b_im, c_re, c_im, d, w_glu, b_glu):
    bsz, l, _ = h.shape
    a_bar, b_bar, c = s5_discretise(a_re, a_im, log_dt, b_re, b_im, c_re, c_im)
    hf = h.astype(F32)
    u = hf.reshape(bsz, l, SSM_GROUPS, SSM_GROUP)
    state0 = lax.complex(st_re.astype(F32), st_im.astype(F32))
    step = lambda s, uc: s5_block(s, uc, a_bar, b_bar, c)
    state, y = lax.scan(step, state0, to_chunks(u, chunk))
    y = from_chunks(y).reshape(bsz, l, D_MODEL) + d.astype(F32) * hf
    g = jax.nn.gelu(y)
    out = g * jax.nn.sigmoid(g @ w_glu.astype(F32) + b_glu.astype(F32))
    return out.astype(h.dtype), jnp.real(state), jnp.imag(state)


def rotary(x, pos):
    half = x.shape[-1] // 2
    freqs = ROPE_BASE ** (-jnp.arange(half, dtype=F32) / half)
    ang = pos.astype(F32)[:, None] * freqs[None, :]
    cos = jnp.cos(ang)[None, :, None, :]
    sin = jnp.sin(ang)[None, :, None, :]
    x1, x2 = x[..., :half], x[..., half:]
    return jnp.concatenate([x1 * cos - x2 * sin, x2 * cos + x1 * sin], axis=-1)


def ret_log_decay():
    return jnp.log(1.0 - 2.0 ** (-5.0 - jnp.arange(RET_HEADS, dtype=F32)))


def ret_block(s, qkv, log_g):
    q, k, v = qkv
    c = q.shape[1]
    idx = jnp.arange(c, dtype=F32)
    diff = idx[:, None] - idx[None, :]
    decay = jnp.where(diff >= 0, jnp.exp(jnp.maximum(diff, 0.0)[None] * log_g[:, None, None]), 0.0)
    scores = jnp.einsum('blhd,bmhd->bhlm', q, k) * decay[None]
    intra = jnp.einsum('bhlm,bmhe->blhe', scores, v)
    inner = jnp.exp((idx[:, None] + 1.0) * log_g[None, :])
    cross = jnp.einsum('blhd,bhde->blhe', q, s) * inner[None, :, :, None]
    tail = jnp.exp((c - 1.0 - idx)[:, None] * log_g[None, :])
    s_new = jnp.exp(c * log_g)[None, :, None, None] * s + jnp.einsum('blhd,blhe->bhde', k * tail[None, :, :, None], v)
    return s_new, intra + cross


def retention_mixer(h, state, pos0, chunk, w_qkvg, w_o):
    bsz, l, _ = h.shape
    p = h.astype(F32) @ w_qkvg.astype(F32)
    q, k, v, g = jnp.split(p, [RET_QK, 2 * RET_QK, 2 * RET_QK + RET_V], axis=-1)
    pos = pos0 + jnp.arange(l)
    q = rotary(q.reshape(bsz, l, RET_HEADS, RET_DK), pos)
    k = rotary(k.reshape(bsz, l, RET_HEADS, RET_DK), pos) * (RET_DK ** -0.5)
    v = v.reshape(bsz, l, RET_HEADS, RET_DV)
    log_g = ret_log_decay()
    step = lambda s, qkv: ret_block(s, qkv, log_g)
    state, o = lax.scan(step, state.astype(F32), (to_chunks(q, chunk), to_chunks(k, chunk), to_chunks(v, chunk)))
    o = from_chunks(o)
    mu = jnp.mean(o, axis=-1, keepdims=True)
    var = jnp.mean(jnp.square(o - mu), axis=-1, keepdims=True)
    o = (o - mu) * lax.rsqrt(var + GN_EPS)
    y = jax.nn.silu(g) * o.reshape(bsz, l, RET_V)
    return (y @ w_o.astype(F32)).astype(h.dtype), state


def memory_kv(mem, g_mem, w_kv):
    b, n, _ = mem.shape
    m = rms_norm(mem, g_mem).astype(F32) @ w_kv.astype(F32)
    k, v = jnp.split(m, 2, axis=-1)
    return k.reshape(b, n, MEM_HEADS, MEM_HD), v.reshape(b, n, MEM_HEADS, MEM_HD)


def memory_attn(h, mk, mv, w_q, w_o):
    bsz, l, _ = h.shape
    q = (h.astype(F32) @ w_q.astype(F32)).reshape(bsz, l, MEM_HEADS, MEM_HD)
    s = jnp.einsum('blhd,bmhd->bhlm', q, mk.astype(F32)) * (MEM_HD ** -0.5)
    p = jax.nn.softmax(s, axis=-1)
    o = jnp.einsum('bhlm,bmhd->blhd', p, mv.astype(F32)).reshape(bsz, l, D_MODEL)
    return (o @ w_o.astype(F32)).astype(h.dtype)


def conv_ffn(h, conv_state, w_up, conv_w, conv_b, w_down):
    l = h.shape[1]
    u = h.astype(F32) @ w_up.astype(F32)
    a, g = jnp.split(u, 2, axis=-1)
    ext = jnp.concatenate([conv_state.astype(F32), a], axis=1)
    cw = conv_w.astype(F32)
    conv = cw[0] * ext[:, 0:l] + cw[1] * ext[:, 1:l + 1] + cw[2] * ext[:, 2:l + 2] + conv_b.astype(F32)
    y = (jax.nn.gelu(conv) * g) @ w_down.astype(F32)
    return y.astype(h.dtype), ext[:, l:]


def setup_inputs(seed: int = 0) -> dict:
    key = jax.random.key(seed)
    ks = iter(jax.random.split(key, 48))

    def nrm(shape, scale):
        return jax.random.normal(next(ks), shape, F32) * scale

    def gain(shape):
        return 1.0 + nrm(shape, 0.01)

    n_idx = jnp.arange(SSM_STATE, dtype=F32)
    return {
        'x_prompt': nrm((BATCH, SEQ, D_MODEL), 1.0),
        'x_sample': nrm((DEC_BATCH, DEC_SEQ, D_MODEL), 1.0),
        'mem_prompt': nrm((BATCH, N_MEM, D_MODEL), 1.0),
        'state_ssm_re': nrm((N_SSM_LAYERS, DEC_BATCH, SSM_GROUPS, SSM_STATE), 0.1),
        'state_ssm_im': nrm((N_SSM_LAYERS, DEC_BATCH, SSM_GROUPS, SSM_STATE), 0.1),
        'state_ret': nrm((N_RET_LAYERS, DEC_BATCH, RET_HEADS, RET_DK, RET_DV), 0.1),
        'cache_mem_k': nrm((DEPTH, DEC_BATCH, N_MEM, MEM_HEADS, MEM_HD), 1.0),
        'cache_mem_v': nrm((DEPTH, DEC_BATCH, N_MEM, MEM_HEADS, MEM_HD), 1.0),
        'cache_conv': nrm((DEPTH, DEC_BATCH, CONV_W - 1, D_FF), 1.0),
        'norm_mix': gain((DEPTH, D_MODEL)),
        'norm_mem_q': gain((DEPTH, D_MODEL)),
        'norm_mem_kv': gain((DEPTH, D_MODEL)),
        'norm_ffn': gain((DEPTH, D_MODEL)),
        'norm_final': gain((D_MODEL,)),
        'ssm_a_re': -0.5 + nrm((N_SSM_LAYERS, SSM_GROUPS, SSM_STATE), 0.01),
        'ssm_a_im': math.pi * n_idx + nrm((N_SSM_LAYERS, SSM_GROUPS, SSM_STATE), 0.01),
        'ssm_log_dt': jax.random.uniform(next(ks), (N_SSM_LAYERS, SSM_GROUPS), F32, math.log(SSM_DT_MIN), math.log(SSM_DT_MAX)),
        'ssm_b_re': nrm((N_SSM_LAYERS, SSM_GROUPS, SSM_STATE, SSM_GROUP), (2 * SSM_GROUP) ** -0.5),
        'ssm_b_im': nrm((N_SSM_LAYERS, SSM_GROUPS, SSM_STATE, SSM_GROUP), (2 * SSM_GROUP) ** -0.5),
        'ssm_c_re': nrm((N_SSM_LAYERS, SSM_GROUPS, SSM_GROUP, SSM_STATE), (2 * SSM_STATE) ** -0.5),
        'ssm_c_im': nrm((N_SSM_LAYERS, SSM_GROUPS, SSM_GROUP, SSM_STATE), (2 * SSM_STATE) ** -0.5),
        'ssm_d': nrm((N_SSM_LAYERS, D_MODEL), 1.0),
        'ssm_w_glu': nrm((N_SSM_LAYERS, D_MODEL, D_MODEL), D_MODEL ** -0.5),
        'ssm_b_glu': nrm((N_SSM_LAYERS, D_MODEL), 0.01),
        'ret_w_qkvg': nrm((N_RET_LAYERS, D_MODEL, 2 * RET_QK + 2 * RET_V), D_MODEL ** -0.5),
        'ret_w_o': nrm((N_RET_LAYERS, RET_V, D_MODEL), RET_V ** -0.5),
        'mem_w_q': nrm((DEPTH, D_MODEL, D_MODEL), D_MODEL ** -0.5),
        'mem_w_kv': nrm((DEPTH, D_MODEL, 2 * D_MODEL), D_MODEL ** -0.5),
        'mem_w_o': nrm((DEPTH, D_MODEL, D_MODEL), D_MODEL ** -0.5),
        'ffn_w_up': nrm((DEPTH, D_MODEL, 2 * D_FF), D_MODEL ** -0.5),
        'ffn_conv_w': nrm((DEPTH, CONV_W, D_FF), CONV_W ** -0.5),
        'ffn_conv_b': nrm((DEPTH, D_FF), 0.01),
        'ffn_w_down': nrm((DEPTH, D_FF, D_MODEL), D_FF ** -0.5),
    }


def reference(x_prompt, x_sample, mem_prompt, state_ssm_re, state_ssm_im, state_ret, cache_mem_k, cache_mem_v, cache_conv,
              norm_mix, norm_mem_q, norm_mem_kv, norm_ffn, norm_final,
              ssm_a_re, ssm_a_im, ssm_log_dt, ssm_b_re, ssm_b_im, ssm_c_re, ssm_c_im, ssm_d, ssm_w_glu, ssm_b_glu,
              ret_w_qkvg, ret_w_o,
              mem_w_q, mem_w_kv, mem_w_o,
              ffn_w_up, ffn_conv_w, ffn_conv_b, ffn_w_down):
    xp, xs = x_prompt, x_sample
    bp = xp.shape[0]
    ls = xs.shape[1]
    ssm_re_p, ssm_im_p, ssm_re_s, ssm_im_s = [], [], [], []
    ret_p, ret_s = [], []
    mk_p, mv_p = [], []
    conv_p, conv_s = [], []
    for i in range(DEPTH):
        j = i // N_MIXERS
        hp = rms_norm(xp, norm_mix[i])
        hs = rms_norm(xs, norm_mix[i])
        if i % N_MIXERS == 0:
            w = (ssm_a_re[j], ssm_a_im[j], ssm_log_dt[j], ssm_b_re[j], ssm_b_im[j], ssm_c_re[j], ssm_c_im[j],
                 ssm_d[j], ssm_w_glu[j], ssm_b_glu[j])
            zeros = jnp.zeros((bp, SSM_GROUPS, SSM_STATE), F32)
            yp, re_p, im_p = s5_mixer(hp, zeros, zeros, CHUNK, *w)
            ys, re_s, im_s = s5_mixer(hs, state_ssm_re[j], state_ssm_im[j], ls, *w)
            ssm_re_p.append(re_p)
            ssm_im_p.append(im_p)
            ssm_re_s.append(re_s)
            ssm_im_s.append(im_s)
        else:
            s0 = jnp.zeros((bp, RET_HEADS, RET_DK, RET_DV), F32)
            yp, sp = retention_mixer(hp, s0, 0, CHUNK, ret_w_qkvg[j], ret_w_o[j])
            ys, ss = retention_mixer(hs, state_ret[j], PAST_LEN, ls, ret_w_qkvg[j], ret_w_o[j])
            ret_p.append(sp)
            ret_s.append(ss)
        xp = xp + yp
        xs = xs + ys
        mkp, mvp = memory_kv(mem_prompt, norm_mem_kv[i], mem_w_kv[i])
        mk_p.append(mkp)
        mv_p.append(mvp)
        xp = xp + memory_attn(rms_norm(xp, norm_mem_q[i]), mkp, mvp, mem_w_q[i], mem_w_o[i])
        xs = xs + memory_attn(rms_norm(xs, norm_mem_q[i]), cache_mem_k[i], cache_mem_v[i], mem_w_q[i], mem_w_o[i])
        cz = jnp.zeros((bp, CONV_W - 1, D_FF), F32)
        fp, cp = conv_ffn(rms_norm(xp, norm_ffn[i]), cz, ffn_w_up[i], ffn_conv_w[i], ffn_conv_b[i], ffn_w_down[i])
        fs, cs = conv_ffn(rms_norm(xs, norm_ffn[i]), cache_conv[i], ffn_w_up[i], ffn_conv_w[i], ffn_conv_b[i], ffn_w_down[i])
        xp = xp + fp
        xs = xs + fs
        conv_p.append(cp)
        conv_s.append(cs)
    y_prompt = rms_norm(xp, norm_final)
    y_sample = rms_norm(xs, norm_final)
    return (y_prompt, y_sample,
            jnp.stack(ssm_re_p), jnp.stack(ssm_im_p), jnp.stack(ssm_re_s), jnp.stack(ssm_im_s),
            jnp.stack(ret_p), jnp.stack(ret_s),
            jnp.stack(mk_p), jnp.stack(mv_p),
            jnp.stack(conv_p), jnp.stack(conv_s))
```

```python
import math
from contextlib import ExitStack, contextmanager

import numpy as np
import concourse.bass as bass
import concourse.mybir as mybir
from concourse.bass_utils import run_bass_kernel_spmd

F32 = mybir.dt.float32
BF16 = mybir.dt.bfloat16
I32 = mybir.dt.int32
AF = mybir.ActivationFunctionType
ALU = mybir.AluOpType
AX = mybir.AxisListType

D = 1024
SEQ = 4096
TB = 512
NBLK = SEQ // TB
NS = 4
LS = 32
PAST = 1024
DFF = 2816
NFT = DFF // 128
EPS = 1e-6
GN_EPS = 1e-5
LG = [math.log(1.0 - 2.0 ** (-5.0 - h)) for h in range(4)]
GELU_C = 1.5957691216057308


class Sem:
    def __init__(self, h, name):
        self.h = h
        self.name = name


class Eng:
    def __init__(self, name, h, sem):
        self.name = name
        self.h = h
        self.sem = sem
        self.cnt = 0
        self.seen = {}


class Buf:
    def __init__(self, name, t, kind="sb"):
        self.name = name
        self.t = t
        self.kind = kind
        self.w = {}
        self.r = {}
        self.dsem = None
        self.dcnt = 0

    def __getitem__(self, k):
        return self.t[k]


class KB:
    def __init__(self, nc):
        self.nc = nc
        self.es = ExitStack()
        self.nsem = 0
        self.pe = Eng("pe", nc.tensor, self.newsem("pe"))
        self.act = Eng("act", nc.scalar, self.newsem("act"))
        self.dve = Eng("dve", nc.vector, self.newsem("dve"))
        self.pool = Eng("pool", nc.gpsimd, self.newsem("pool"))
        self.sp = Eng("sp", nc.sync, self.newsem("sp"))
        self.engs = [self.pe, self.act, self.dve, self.pool, self.sp]
        self.dbufs = []
        self.uid = 0

    def newsem(self, name):
        self.nsem += 1
        return Sem(self.es.enter_context(self.nc.semaphore(f"s{self.nsem}_{name}")), name)

    def sb(self, name, shape, dt, st=None):
        self.uid += 1
        t = (st or self.es).enter_context(self.nc.sbuf_tensor(f"{name}_{self.uid}", list(shape), dt))
        b = Buf(name, t)
        b.scoped = st is not None
        return b

    def psum(self, name, shape, dt):
        t = self.es.enter_context(self.nc.psum_tensor(name, list(shape), dt))
        return Buf(name, t, "ps")

    def dram(self, name, shape, dt, kind):
        h = self.nc.dram_tensor(name, list(shape), dt, kind=kind)
        return Buf(name, h.ap(), "dram")

    def _wait(self, e, sem, v):
        if v > e.seen.get(sem, 0):
            if sem is e.sem:
                assert v <= e.cnt, f"same-engine wait on pending inc {e.name}"
            e.h.wait_ge(sem.h, v)
            e.seen[sem] = v

    def op(self, e, fn, rd=(), wr=(), inc=True):
        if any(b.kind == "ps" for b in rd):
            wr = list(wr) + [b for b in rd if b.kind == "ps"]
            rd = [b for b in rd if b.kind != "ps"]
        need = {}
        for b in rd:
            for sem, v in b.w.items():
                if v > need.get(sem, 0):
                    need[sem] = v
        skip_self = e is self.pe
        for b in wr:
            for sem, v in b.r.items():
                if not (skip_self and sem is e.sem) and v > need.get(sem, 0):
                    need[sem] = v
            for sem, v in b.w.items():
                if not (skip_self and sem is e.sem) and v > need.get(sem, 0):
                    need[sem] = v
        for sem, v in need.items():
            self._wait(e, sem, v)
        ins = fn()
        val = e.cnt + 1
        if inc:
            ins.then_inc(e.sem.h, 1)
            e.cnt = val
        for b in rd:
            if b.r.get(e.sem, 0) < val:
                b.r[e.sem] = val
        for b in wr:
            if b.w.get(e.sem, 0) < val:
                b.w[e.sem] = val
        return ins

    def dma(self, q, out_ap, in_ap, dst, src, **kw):
        need = {}
        for sem, v in src.w.items():
            if v > need.get(sem, 0):
                need[sem] = v
        if dst.kind != "dram":
            for sem, v in dst.r.items():
                if v > need.get(sem, 0):
                    need[sem] = v
            for sem, v in dst.w.items():
                if v > need.get(sem, 0):
                    need[sem] = v
        owner = dst if dst.kind == "sb" else (src if src.kind == "sb" else dst)
        if owner.dsem is None:
            owner.dsem = self.newsem("d_" + owner.name)
            self.dbufs.append(owner)
        if owner.dcnt > 0 and owner.kind != "dram":
            need[owner.dsem] = owner.dcnt
        for sem, v in need.items():
            self._wait(q, sem, v)
        ins = q.h.dma_start(out=out_ap, in_=in_ap, **kw)
        ins.then_inc(owner.dsem.h, 16)
        owner.dcnt += 16
        dst.w[owner.dsem] = owner.dcnt
        src.r[owner.dsem] = owner.dcnt

    def barrier(self, final=False):
        sp = self.sp
        for e in self.engs:
            if e is not sp and (final or e.cnt != getattr(e, "bar_cnt", -1)):
                self._wait(sp, e.sem, e.cnt)
            e.bar_cnt = e.cnt
        for b in self.dbufs:
            if final or (b.kind != "dram" and getattr(b, "scoped", False)):
                self._wait(sp, b.dsem, b.dcnt)
        ins = sp.h.nop()
        ins.then_inc(sp.sem.h, 1)
        sp.cnt += 1
        for e in self.engs:
            if e is not sp:
                self._wait(e, sp.sem, sp.cnt)

    @contextmanager
    def scope(self):
        st = ExitStack()
        try:
            yield st
        finally:
            self.barrier()
            st.close()

    def mm(self, out, lhsT, rhs, rd, wr, start=True, stop=True, inc=True, **kw):
        nc = self.nc
        return self.op(self.pe, lambda: nc.tensor.matmul(out, lhsT, rhs, start=start, stop=stop, **kw), rd, wr, inc)

    def tr(self, out, in_, ident, rd, wr, inc=True):
        nc = self.nc
        return self.op(self.pe, lambda: nc.tensor.transpose(out, in_, ident), rd, wr, inc)

    def actf(self, out, in_, func, rd, wr, scale=None, bias=None, accum_out=None):
        nc = self.nc
        kw = {}
        if scale is not None:
            kw["scale"] = scale
        if bias is not None:
            kw["bias"] = bias
        if accum_out is not None:
            kw["accum_out"] = accum_out
        return self.op(self.act, lambda: nc.scalar.activation(out=out, in_=in_, func=func, **kw), rd, wr)

    def tt(self, e, out, in0, in1, op, rd, wr):
        return self.op(e, lambda: e.h.tensor_tensor(out=out, in0=in0, in1=in1, op=op), rd, wr)

    def ts(self, e, out, in0, s1, op0, rd, wr, s2=None, op1=None):
        if op1 is None:
            return self.op(e, lambda: e.h.tensor_scalar(out=out, in0=in0, scalar1=s1, scalar2=None, op0=op0), rd, wr)
        return self.op(e, lambda: e.h.tensor_scalar(out=out, in0=in0, scalar1=s1, scalar2=s2, op0=op0, op1=op1), rd, wr)

    def stt(self, out, in0, scalar, in1, op0, op1, rd, wr):
        nc = self.nc
        return self.op(self.dve, lambda: nc.vector.scalar_tensor_tensor(out=out, in0=in0, scalar=scalar, in1=in1, op0=op0, op1=op1), rd, wr)

    def cp(self, e, out, in_, rd, wr):
        if e is self.act:
            return self.actf(out, in_, AF.Copy, rd, wr)
        return self.op(e, lambda: e.h.tensor_copy(out=out, in_=in_), rd, wr)

    def memset(self, e, ap, val, wr):
        return self.op(e, lambda: e.h.memset(ap, val), (), wr)

    def recip(self, out, in_, rd, wr):
        nc = self.nc
        return self.op(self.dve, lambda: nc.vector.reciprocal(out=out, in_=in_), rd, wr)


class WStream:
    def __init__(self, kb, nslots, slot_elems):
        self.kb = kb
        self.n = nslots
        self.slots = [kb.sb(f"wslot{i}", [128, slot_elems], BF16) for i in range(nslots)]
        self.sched = []
        self.pos = 0
        self.issued = 0

    def plan(self, lst):
        self.sched.extend(lst)

    def _issue(self, i):
        scr, c, ne = self.sched[i]
        slot = self.slots[i % self.n]
        scr.load(self.kb, slot, c, ne)

    def get(self, scr, c):
        key = self.sched[self.pos]
        assert key[0] is scr and key[1] == c, f"wstream mismatch at {self.pos}: want {scr.name},{c} sched {key[0].name},{key[1]}"
        lim = min(len(self.sched), self.pos + self.n)
        while self.issued < lim:
            self._issue(self.issued)
            self.issued += 1
        slot = self.slots[self.pos % self.n]
        self.pos += 1
        return slot


def build(nc, nblk=NBLK, do_sample=True, dbg=None, dbg_blk=("p", 0), stages=99, stop=99):
    kb = KB(nc)
    pe, act, dve, pool, sp = kb.pe, kb.act, kb.dve, kb.pool, kb.sp

    def din(name, shape):
        return kb.dram(name, shape, F32, "ExternalInput")

    def dout(name, shape):
        return kb.dram(name, shape, F32, "ExternalOutput")

    xp = din("xp", [SEQ, D]); xs = din("xs", [NS * LS, D]); mem = din("mem", [256, D])
    sre = din("sre", [NS, 32, 128]); sim = din("sim", [NS, 32, 128])
    sret = din("sret", [NS, 4, 256, 512])
    cmk = din("cmk", [2, NS, 256, D]); cmv = din("cmv", [2, NS, 256, D])
    cconv = din("cconv", [2, NS, 2, DFF])
    norm_mix = din("norm_mix", [2, D]); norm_mem_q = din("norm_mem_q", [2, D])
    norm_mem_kv = din("norm_mem_kv", [2, D]); norm_ffn = din("norm_ffn", [2, D])
    norm_final = din("norm_final", [D])
    a_re = din("ssm_a_re", [32, 128]); a_im = din("ssm_a_im", [32, 128])
    log_dt = din("ssm_log_dt", [64])
    b_re = din("ssm_b_re", [64, 64, 16]); b_im = din("ssm_b_im", [64, 64, 16])
    c_re = din("ssm_c_re", [64, 16, 64]); c_im = din("ssm_c_im", [64, 16, 64])
    ssm_d = din("ssm_d", [D]); w_glu = din("ssm_w_glu", [D, D]); b_glu = din("ssm_b_glu", [D])
    w_qkvg = din("ret_w_qkvg", [D, 6144]); w_ro = din("ret_w_o", [2048, D])
    w_mq = din("mem_w_q", [2, D, D]); w_mkv = din("mem_w_kv", [2, D, 2048]); w_mo = din("mem_w_o", [2, D, D])
    w_up = din("ffn_w_up", [2, D, 2 * DFF]); conv_w = din("ffn_conv_w", [2, 3, DFF]); conv_b = din("ffn_conv_b", [2, DFF])
    w_down = din("ffn_w_down", [2, DFF, D])

    yp = dout("yp", [SEQ, D]); ys = dout("ys", [NS * LS, D])
    o_sre_p = dout("o_sre_p", [32, 128]); o_sim_p = dout("o_sim_p", [32, 128])
    o_sre_s = dout("o_sre_s", [NS, 32, 128]); o_sim_s = dout("o_sim_s", [NS, 32, 128])
    o_ret_p = dout("o_ret_p", [4, 256, 512]); o_ret_s = dout("o_ret_s", [NS, 4, 256, 512])
    o_mk = dout("o_mk", [2, 256, D]); o_mv = dout("o_mv", [2, 256, D])
    o_conv_p = dout("o_conv_p", [2, 2, DFF]); o_conv_s = dout("o_conv_s", [2, NS, 2, DFF])
    dbg_out = dout("dbg", [128, 8, TB]) if dbg is not None else None

    def wscr(name, nch, kt, oc):
        b = kb.dram("scr_" + name, [nch, 128, kt * oc], BF16, "Internal")
        b.kt = kt; b.oc = oc; b.nch = nch
        b.load = lambda kb_, slot, c, ne, b=b: kb_.dma(kb_.sp, slot.t[:, 0:ne], b.t[c][:, 0:ne], slot, b)
        return b

    def wpm(name, kt, ocols, oc, kind="cols"):
        b = kb.dram("scr_" + name, [128, kt, ocols], BF16, "Internal")
        b.kt = kt; b.oc = oc; b.nch = ocols // oc

        def load(kb_, slot, c, ne, b=b, kind=kind, kt=kt, oc=oc):
            if kind == "cols":
                kb_.dma(kb_.sp, slot.t[:, 0:kt * oc].rearrange("p (k o) -> p k o", o=oc), b.t[:, :, c * oc:(c + 1) * oc], slot, b)
            elif kind == "up":
                sv = slot.t[:, 0:kt * 512].rearrange("p (k o) -> p k o", o=512)
                kb_.dma(kb_.sp, sv[:, :, 0:256], b.t[:, :, c * 256:(c + 1) * 256], slot, b)
                kb_.dma(kb_.sp, sv[:, :, 256:512], b.t[:, :, DFF + c * 256:DFF + (c + 1) * 256], slot, b)
            else:
                nk = ne // 1024
                kb_.dma(kb_.sp, slot.t[:, 0:ne].rearrange("p (k o) -> p k o", o=1024), b.t[:, 4 * c:4 * c + nk, :], slot, b)
        b.load = load
        return b

    W = {}
    W["glu"] = wpm("glu", 8, D, 512)
    for i in range(2):
        W[f"mq{i}"] = wpm(f"mq{i}", 8, D, 512)
        W[f"mo{i}"] = wpm(f"mo{i}", 8, D, 512)
        W[f"mkv{i}"] = wpm(f"mkv{i}", 8, 2048, 512)
        W[f"up{i}"] = wpm(f"up{i}", 8, 2 * DFF, 512, "up")
        W[f"down{i}"] = wpm(f"down{i}", NFT, D, 1024, "rows")
    W["rq"] = wpm("rq", 8, 1024, 256); W["rk"] = wpm("rk", 8, 1024, 256)
    W["rv"] = wpm("rv", 8, 2048, 512); W["rg"] = wpm("rg", 8, 2048, 512)
    W["ro"] = wpm("ro", 16, D, 256)
    W["s5f"] = wscr("s5f", 8, 1, 2048)
    W["s5g"] = wscr("s5g", 8, 1, 2048)
    W["s5k"] = wscr("s5k", 8, 1, 1024)

    wsrc = Buf("wsrc", None, "dram")

    def cast_w(scr, src2d, col0=0):
        ncols = scr.nch * scr.oc
        kb.dma(pool, scr.t[:, :, :], src2d[:, col0:col0 + ncols].rearrange("(kt p) o -> p kt o", p=128), scr, wsrc)

    def cast_up(i):
        cast_w(W[f"up{i}"], w_up.t[i])

    def cast_down(i):
        scr = W[f"down{i}"]
        kb.dma(pool, scr.t[:, :, :], w_down.t[i].rearrange("(kt p) o -> p kt o", p=128), scr, wsrc)

    def cast_group(g):
        if g == 0:
            for i in range(2):
                cast_w(W[f"mkv{i}"], w_mkv.t[i])
        elif g == 1:
            cast_w(W["glu"], w_glu.t)
            cast_w(W["mq0"], w_mq.t[0]); cast_w(W["mo0"], w_mo.t[0])
            cast_up(0)
            cast_down(0)
        else:
            cast_w(W["rq"], w_qkvg.t, 0); cast_w(W["rk"], w_qkvg.t, 1024)
            cast_w(W["rv"], w_qkvg.t, 2048); cast_w(W["rg"], w_qkvg.t, 4096)
            cast_w(W["ro"], w_ro.t)
            cast_w(W["mq1"], w_mq.t[1]); cast_w(W["mo1"], w_mo.t[1])
            cast_up(1)
            cast_down(1)

    if stop == 0:
        kb.barrier(); return kb
    P = [kb.psum(f"pb{i}", [128, 512], F32) for i in range(8)]
    gen_i = [0]

    def gps():
        b = P[gen_i[0] % 4]
        gen_i[0] += 1
        return b

    identf = kb.sb("identf", [128, 128], F32)
    onesb = kb.sb("onesb", [128, 128], BF16)
    kb.memset(pool, identf.t[:], 1.0, [identf])
    kb.op(pool, lambda: nc.gpsimd.affine_select(out=identf.t[:], in_=identf.t[:], pattern=[[-1, 128]],
                                                compare_op=ALU.is_equal, fill=0.0, base=0, channel_multiplier=1),
          [identf], [identf])
    kb.memset(dve, onesb.t[:], 1.0, [onesb])
    onesf = kb.sb("onesf", [128, 128], F32)
    kb.memset(dve, onesf.t[:], 1.0, [onesf])

    def alias(base, name, ap):
        v = Buf(name, ap)
        v.w = base.w; v.r = base.r
        return v

    vec_nat = kb.sb("vec_nat", [88, 128], F32)
    vecT = kb.sb("vecT", [128, 88], F32)
    vsrcs = [norm_mix.t[0], norm_mix.t[1], norm_mem_q.t[0], norm_mem_q.t[1], norm_mem_kv.t[0], norm_mem_kv.t[1],
             norm_ffn.t[0], norm_ffn.t[1], norm_final.t, ssm_d.t, b_glu.t]
    for k, ap in enumerate(vsrcs):
        kb.dma(sp, vec_nat.t[8 * k:8 * k + 8, :], ap.rearrange("(t p) -> t p", p=128), vec_nat, wsrc)
    vviews = [alias(vecT, f"vec{k}", vecT.t[:, 8 * k:8 * k + 8]) for k in range(11)]
    g_mix = vviews[0:2]; g_mq = vviews[2:4]; g_mkv = vviews[4:6]; g_ffn = vviews[6:8]
    g_fin = vviews[8]; v_d = vviews[9]; v_bglu = vviews[10]
    cw = []; cb = []; cnat = []; cT = []
    for i in range(2):
        cn = kb.sb(f"cnat{i}", [88, 128], F32)
        ct_ = kb.sb(f"cT{i}", [128, 88], F32)
        kb.dma(sp, cn.t[0:66, :], conv_w.t[i].rearrange("r (ft p) -> (r ft) p", p=128), cn, wsrc)
        kb.dma(sp, cn.t[66:88, :], conv_b.t[i].rearrange("(ft p) -> ft p", p=128), cn, wsrc)
        cnat.append(cn); cT.append(ct_)
        cw.append([alias(ct_, f"cw{i}_{r}", ct_.t[:, 22 * r:22 * r + 22]) for r in range(3)])
        cb.append(alias(ct_, f"cb{i}", ct_.t[:, 66:88]))

    Apow = {lv: (kb.sb(f"A{lv}re", [128, 32], F32), kb.sb(f"A{lv}im", [128, 32], F32)) for lv in (8, 16, 32, 64)}
    C0 = kb.sb("C0", [128, TB], F32); S0 = kb.sb("S0", [128, TB], F32)
    dC = kb.sb("dC", [128, 16], F32); dS = kb.sb("dS", [128, 16], F32)
    M01 = kb.sb("M01", [128, TB], F32)
    M01s = kb.sb("M01s", [128, 128], F32)
    innerp = kb.sb("innerp", [128, 4, TB], F32)
    inners = kb.sb("inners", [128, 4, 128], F32)
    tailp = kb.sb("tailp", [128, 4, 4], F32)
    tinvp = kb.sb("tinvp", [128, 4, 4], F32)
    tails = kb.sb("tails", [128, 4], F32)
    tinvs = kb.sb("tinvs", [128, 4], F32)

    def tr_f32(dst_ap, dst_buf, src_ap, src_buf, n, e=None):
        pb = gps()
        kb.tr(pb.t[:, 0:n], src_ap, identf.t[0:n, 0:n], [src_buf, identf], [pb])
        kb.cp(e or dve, dst_ap, pb.t[:, 0:n], [pb], [dst_buf])

    tr_f32(vecT.t[:], vecT, vec_nat.t[:], vec_nat, 88)
    for i in range(2):
        tr_f32(cT[i].t[:], cT[i], cnat[i].t[:], cnat[i], 88)
    if stop == 1:
        kb.barrier(); return kb
    with kb.scope() as st:
        ii = kb.sb("ii", [128, TB], I32, st)
        ff = kb.sb("ff", [128, TB], F32, st)

        def iota_f(dst_ap, dst_buf, pattern, base, cm):
            nfree = 1
            for _, n in pattern:
                nfree *= n
            kb.op(pool, lambda: nc.gpsimd.iota(ii.t[:, 0:nfree], pattern=pattern, base=base, channel_multiplier=cm), (), [ii])
            kb.cp(dve, dst_ap, ii.t[:, 0:nfree], [ii], [dst_buf])

        fj = kb.sb("fj", [128, 1], F32, st)
        iota_f(fj.t[:], fj, [[0, 1]], 0, 1)
        kb.actf(fj.t[:], fj.t[:], AF.Exp, [fj], [fj], scale=-math.log(10000.0) / 128.0)
        hpi = kb.sb("hpi", [128, 1], F32, st)
        kb.memset(dve, hpi.t[:], math.pi / 2, [hpi])
        er = kb.sb("er", [128, 16], F32, st); ei = kb.sb("ei", [128, 16], F32, st)
        kb.actf(er.t[:, 0:1], fj.t[:], AF.Sin, [fj, hpi], [er], scale=-1.0, bias=hpi.t[:, 0:1])
        kb.actf(ei.t[:, 0:1], fj.t[:], AF.Sin, [fj], [ei])
        t1 = kb.sb("t1", [128, TB], F32, st); t2 = kb.sb("t2", [128, TB], F32, st)
        for k in range(10):
            kb.tt(dve, t1.t[:, 0:1], er.t[:, k:k + 1], er.t[:, k:k + 1], ALU.mult, [er], [t1])
            kb.tt(dve, t2.t[:, 0:1], ei.t[:, k:k + 1], ei.t[:, k:k + 1], ALU.mult, [ei], [t2])
            kb.tt(dve, er.t[:, k + 1:k + 2], t1.t[:, 0:1], t2.t[:, 0:1], ALU.subtract, [t1, t2], [er])
            kb.tt(dve, t1.t[:, 0:1], er.t[:, k:k + 1], ei.t[:, k:k + 1], ALU.mult, [er, ei], [t1])
            kb.ts(dve, ei.t[:, k + 1:k + 2], t1.t[:, 0:1], 2.0, ALU.mult, [t1], [ei])
        kb.memset(dve, C0.t[:, 0:1], 1.0, [C0]); kb.memset(dve, S0.t[:, 0:1], 0.0, [S0])
        for k in range(9):
            n = 1 << k
            cr = er.t[:, k:k + 1]; ci = ei.t[:, k:k + 1]
            kb.ts(dve, t1.t[:, 0:n], S0.t[:, 0:n], ci, ALU.mult, [S0, ei], [t1])
            kb.stt(C0.t[:, n:2 * n], C0.t[:, 0:n], cr, t1.t[:, 0:n], ALU.mult, ALU.subtract, [C0, er, t1], [C0])
            kb.ts(dve, t2.t[:, 0:n], C0.t[:, 0:n], ci, ALU.mult, [C0, ei], [t2])
            kb.stt(S0.t[:, n:2 * n], S0.t[:, 0:n], cr, t2.t[:, 0:n], ALU.mult, ALU.add, [S0, er, t2], [S0])
        kb.memset(dve, dC.t[:, 0:1], 1.0, [dC]); kb.memset(dve, dS.t[:, 0:1], 0.0, [dS])
        for b in range(1, 8):
            cr = er.t[:, 9:10]; ci = ei.t[:, 9:10]
            kb.ts(dve, t1.t[:, 0:1], dS.t[:, b - 1:b], ci, ALU.mult, [dS, ei], [t1])
            kb.stt(dC.t[:, b:b + 1], dC.t[:, b - 1:b], cr, t1.t[:, 0:1], ALU.mult, ALU.subtract, [dC, er, t1], [dC])
            kb.ts(dve, t2.t[:, 0:1], dC.t[:, b - 1:b], ci, ALU.mult, [dC, ei], [t2])
            kb.stt(dS.t[:, b:b + 1], dS.t[:, b - 1:b], cr, t2.t[:, 0:1], ALU.mult, ALU.add, [dS, er, t2], [dS])
        kb.cp(dve, dC.t[:, 8:9], er.t[:, 10:11], [er], [dC])
        kb.cp(dve, dS.t[:, 8:9], ei.t[:, 10:11], [ei], [dS])

        kb.memset(pool, M01.t[:], 1.0, [M01])
        kb.op(pool, lambda: nc.gpsimd.affine_select(out=M01.t[:], in_=M01.t[:], pattern=[[1, TB]], compare_op=ALU.is_ge,
                                                    fill=0.0, base=0, channel_multiplier=-1), [M01], [M01])
        kb.memset(pool, M01s.t[:], 1.0, [M01s])
        for s_ in range(NS):
            blk = M01s.t[:, 32 * s_:32 * s_ + 32]
            kb.op(pool, lambda blk=blk, s_=s_: nc.gpsimd.affine_select(out=blk, in_=blk, pattern=[[1, 32]], compare_op=ALU.is_ge,
                                                                      fill=0.0, base=32 * s_, channel_multiplier=-1), [M01s], [M01s])
            kb.op(pool, lambda blk=blk, s_=s_: nc.gpsimd.affine_select(out=blk, in_=blk, pattern=[[0, 32]], compare_op=ALU.is_ge,
                                                                      fill=0.0, base=-32 * s_, channel_multiplier=1), [M01s], [M01s])
        rt = kb.sb("rt", [128, 128], F32, st)
        for h in range(4):
            iota_f(ff.t[:, 0:TB], ff, [[1, TB]], 1, 0)
            kb.actf(innerp.t[:, h, :], ff.t[:, 0:TB], AF.Exp, [ff], [innerp], scale=LG[h])
            iota_f(ff.t[:, 0:128], ff, [[0, 4], [1, 32]], 1, 0)
            kb.actf(inners.t[:, h, :], ff.t[:, 0:128], AF.Exp, [ff], [inners], scale=LG[h])
            iota_f(ff.t[:, 0:4], ff, [[-128, 4]], TB - 1, -1)
            kb.actf(tailp.t[:, h, :], ff.t[:, 0:4], AF.Exp, [ff], [tailp], scale=LG[h])
            iota_f(ff.t[:, 0:4], ff, [[128, 4]], 1, 1)
            kb.actf(tinvp.t[:, h, :], ff.t[:, 0:4], AF.Exp, [ff], [tinvp], scale=-LG[h])
            iota_f(ff.t[:, 0:128], ff, [[0, 4], [-1, 32]], 31, 0)
            kb.actf(rt.t[:], ff.t[:, 0:128], AF.Exp, [ff], [rt], scale=LG[h])
            tr_f32(tails.t[:, h:h + 1], tails, rt.t[0:1, :], rt, 1)
            kb.recip(rt.t[:], inners.t[:, h, :], [inners], [rt])
            tr_f32(tinvs.t[:, h:h + 1], tinvs, rt.t[0:1, :], rt, 1)

    cast_group(0)
    cast_group(1)
    if stop == 2:
        kb.barrier(); return kb
    with kb.scope() as st:
        ewi = [0]

        def ew():
            ewi[0] += 1
            return dve if ewi[0] % 2 else pool

        nat = kb.sb("nat", [32, 3, 128], F32, st)
        kb.dma(sp, nat.t[:, 0, :], a_re.t[:, :], nat, wsrc)
        kb.dma(sp, nat.t[:, 1, :], a_im.t[:, :], nat, wsrc)
        ld2 = kb.sb("ld2", [32, 2], F32, st)
        kb.dma(sp, ld2.t[:], log_dt.t.rearrange("(gp gpar) -> gp gpar", gpar=2), ld2, wsrc)
        kb.cp(dve, nat.t[:, 2, :].rearrange("g (a p) -> g a p", p=64), ld2.t[:].unsqueeze(2).to_broadcast([32, 2, 64]), [ld2], [nat])
        q_are = kb.sb("q_are", [128, 32], F32, st); q_aim = kb.sb("q_aim", [128, 32], F32, st); q_dt = kb.sb("q_dt", [128, 32], F32, st)
        tr_f32(q_are.t[:], q_are, nat.t[:, 0, :], nat, 32)
        tr_f32(q_aim.t[:], q_aim, nat.t[:, 1, :], nat, 32)
        tr_f32(q_dt.t[:], q_dt, nat.t[:, 2, :], nat, 32)
        kb.actf(q_dt.t[:], q_dt.t[:], AF.Exp, [q_dt], [q_dt])
        lr = kb.sb("lr", [128, 32], F32, st); li = kb.sb("li", [128, 32], F32, st)
        kb.tt(dve, lr.t[:], q_are.t[:], q_dt.t[:], ALU.mult, [q_are, q_dt], [lr])
        kb.tt(dve, li.t[:], q_aim.t[:], q_dt.t[:], ALU.mult, [q_aim, q_dt], [li])
        hpi = kb.sb("hpi2", [128, 1], F32, st)
        kb.memset(dve, hpi.t[:], math.pi / 2, [hpi])
        e8 = kb.sb("e8", [128, 32], F32, st); ar = kb.sb("ar", [128, 32], F32, st); ai = kb.sb("ai", [128, 32], F32, st)
        u1 = kb.sb("u1", [128, 32], F32, st); u2 = kb.sb("u2", [128, 32], F32, st)
        kb.actf(e8.t[:], lr.t[:], AF.Exp, [lr], [e8], scale=0.125)
        kb.actf(u1.t[:], li.t[:], AF.Sin, [li, hpi], [u1], scale=-0.125, bias=hpi.t[:, 0:1])
        kb.actf(u2.t[:], li.t[:], AF.Sin, [li], [u2], scale=0.125)
        kb.tt(dve, ar.t[:], e8.t[:], u1.t[:], ALU.mult, [e8, u1], [ar])
        kb.tt(dve, ai.t[:], e8.t[:], u2.t[:], ALU.mult, [e8, u2], [ai])

        def csq(r, i):
            kb.tt(dve, u1.t[:], r.t[:], r.t[:], ALU.mult, [r], [u1])
            kb.tt(dve, u2.t[:], i.t[:], i.t[:], ALU.mult, [i], [u2])
            kb.tt(dve, u2.t[:], u1.t[:], u2.t[:], ALU.subtract, [u1, u2], [u2])
            kb.tt(dve, u1.t[:], r.t[:], i.t[:], ALU.mult, [r, i], [u1])
            kb.ts(dve, i.t[:], u1.t[:], 2.0, ALU.mult, [u1], [i])
            kb.cp(dve, r.t[:], u2.t[:], [u2], [r])

        for _ in range(3):
            csq(ar, ai)
        Pre = kb.sb("Pre", [128, 9, 32], F32, st); Pim = kb.sb("Pim", [128, 9, 32], F32, st)
        kb.memset(dve, Pre.t[:, 0, :], 1.0, [Pre]); kb.memset(dve, Pim.t[:, 0, :], 0.0, [Pim])
        for m in range(8):
            kb.tt(dve, u1.t[:], Pre.t[:, m, :], ar.t[:], ALU.mult, [Pre, ar], [u1])
            kb.tt(dve, u2.t[:], Pim.t[:, m, :], ai.t[:], ALU.mult, [Pim, ai], [u2])
            kb.tt(dve, Pre.t[:, m + 1, :], u1.t[:], u2.t[:], ALU.subtract, [u1, u2], [Pre])
            kb.tt(dve, u1.t[:], Pre.t[:, m, :], ai.t[:], ALU.mult, [Pre, ai], [u1])
            kb.tt(dve, u2.t[:], Pim.t[:, m, :], ar.t[:], ALU.mult, [Pim, ar], [u2])
            kb.tt(dve, Pim.t[:, m + 1, :], u1.t[:], u2.t[:], ALU.add, [u1, u2], [Pim])
        kb.cp(dve, Apow[8][0].t[:], Pre.t[:, 8, :], [Pre], [Apow[8][0]])
        kb.cp(dve, Apow[8][1].t[:], Pim.t[:, 8, :], [Pim], [Apow[8][1]])
        for lv in (16, 32, 64):
            kb.cp(dve, Apow[lv][0].t[:], Apow[lv // 2][0].t[:], [Apow[lv // 2][0]], [Apow[lv][0]])
            kb.cp(dve, Apow[lv][1].t[:], Apow[lv // 2][1].t[:], [Apow[lv // 2][1]], [Apow[lv][1]])
            csq(Apow[lv][0], Apow[lv][1])
        cre = kb.sb("cre", [128, 32], F32, st); cim = kb.sb("cim", [128, 32], F32, st)
        xm1 = kb.sb("xm1", [128, 32], F32, st); rden = kb.sb("rden", [128, 32], F32, st)
        kb.ts(dve, xm1.t[:], ar.t[:], -1.0, ALU.add, [ar], [xm1])
        kb.tt(dve, u1.t[:], q_are.t[:], q_are.t[:], ALU.mult, [q_are], [u1])
        kb.tt(dve, u2.t[:], q_aim.t[:], q_aim.t[:], ALU.mult, [q_aim], [u2])
        kb.tt(dve, u1.t[:], u1.t[:], u2.t[:], ALU.add, [u1, u2], [u1])
        kb.recip(rden.t[:], u1.t[:], [u1], [rden])
        kb.tt(dve, u1.t[:], xm1.t[:], q_are.t[:], ALU.mult, [xm1, q_are], [u1])
        kb.tt(dve, u2.t[:], ai.t[:], q_aim.t[:], ALU.mult, [ai, q_aim], [u2])
        kb.tt(dve, u1.t[:], u1.t[:], u2.t[:], ALU.add, [u1, u2], [u1])
        kb.tt(dve, cre.t[:], u1.t[:], rden.t[:], ALU.mult, [u1, rden], [cre])
        kb.tt(dve, u1.t[:], ai.t[:], q_are.t[:], ALU.mult, [ai, q_are], [u1])
        kb.tt(dve, u2.t[:], xm1.t[:], q_aim.t[:], ALU.mult, [xm1, q_aim], [u2])
        kb.tt(dve, u1.t[:], u1.t[:], u2.t[:], ALU.subtract, [u1, u2], [u1])
        kb.tt(dve, cim.t[:], u1.t[:], rden.t[:], ALU.mult, [u1, rden], [cim])

        Bre = kb.sb("Bre", [128, 32, 16], F32, st); Bim = kb.sb("Bim", [128, 32, 16], F32, st)
        for gpar in range(2):
            kb.dma(sp, Bre.t[64 * gpar:64 * gpar + 64, :, :],
                   b_re.t.rearrange("(gp gpar) p c -> gpar p gp c", gpar=2)[gpar], Bre, wsrc)
            kb.dma(sp, Bim.t[64 * gpar:64 * gpar + 64, :, :],
                   b_im.t.rearrange("(gp gpar) p c -> gpar p gp c", gpar=2)[gpar], Bim, wsrc)
        Bbre = kb.sb("Bbre", [128, 32, 16], F32, st); Bbim = kb.sb("Bbim", [128, 32, 16], F32, st)
        w1 = kb.sb("w1", [128, 32, 32], F32, st); w2 = kb.sb("w2", [128, 32, 32], F32, st)

        def bc(buf_ap, n):
            return buf_ap.unsqueeze(2).to_broadcast([128, 32, n])

        kb.tt(dve, w1.t[:, :, 0:16], Bre.t[:], bc(cre.t[:], 16), ALU.mult, [Bre, cre], [w1])
        kb.tt(dve, w2.t[:, :, 0:16], Bim.t[:], bc(cim.t[:], 16), ALU.mult, [Bim, cim], [w2])
        kb.tt(dve, Bbre.t[:], w1.t[:, :, 0:16], w2.t[:, :, 0:16], ALU.subtract, [w1, w2], [Bbre])
        kb.tt(dve, w1.t[:, :, 0:16], Bim.t[:], bc(cre.t[:], 16), ALU.mult, [Bim, cre], [w1])
        kb.tt(dve, w2.t[:, :, 0:16], Bre.t[:], bc(cim.t[:], 16), ALU.mult, [Bre, cim], [w2])
        kb.tt(dve, Bbim.t[:], w1.t[:, :, 0:16], w2.t[:, :, 0:16], ALU.add, [w1, w2], [Bbim])

        Qbd = kb.sb("Qbd", [128, 8, 8, 2, 128], F32, st)
        kb.memset(dve, Qbd.t[:], 0.0, [Qbd])
        for m in range(8):
            for plane in range(2):
                pa, pb_ = (Pre, Pim) if plane == 0 else (Pim, Pre)
                kb.tt(dve, w1.t[:, :, 0:16], Bbre.t[:], bc(pa.t[:, m, :], 16), ALU.mult, [Bbre, pa], [w1])
                kb.tt(dve, w2.t[:, :, 0:16], Bbim.t[:], bc(pb_.t[:, m, :], 16), ALU.mult, [Bbim, pb_], [w2])
                for gpar in range(2):
                    ps_ = slice(64 * gpar, 64 * gpar + 64)
                    dst = Qbd.t[ps_, :, m, plane, :].rearrange("p ct (j c) -> p ct j c", c=32)[:, :, :, 16 * gpar:16 * gpar + 16]
                    i0 = w1.t[ps_, :, 0:16].rearrange("p (ct j) c -> p ct j c", j=4)
                    i1 = w2.t[ps_, :, 0:16].rearrange("p (ct j) c -> p ct j c", j=4)
                    kb.tt(dve, dst, i0, i1, ALU.subtract if plane == 0 else ALU.add, [w1, w2], [Qbd])

        Cn = [kb.sb("Cnre", [128, 8, 64], F32, st), kb.sb("Cnim", [128, 8, 64], F32, st)]
        kb.dma(sp, Cn[0].t[:], c_re.t.rearrange("(ct gl) co p -> (gl co) ct p", gl=8), Cn[0], wsrc)
        kb.dma(sp, Cn[1].t[:], c_im.t.rearrange("(ct gl) co p -> (gl co) ct p", gl=8), Cn[1], wsrc)
        mk = kb.sb("mk", [128, 2], F32, st)
        idv = identf.t[:].rearrange("p (a b c) -> p a b c", b=2, c=16)
        for par in range(2):
            kb.op(dve, lambda par=par: nc.vector.tensor_reduce(out=mk.t[:, par:par + 1], in_=idv[:, :, par, :], axis=AX.XY, op=ALU.add),
                  [identf], [mk])
        Cpad = kb.sb("Cpad", [128, 8, 128], F32, st)
        CT = [kb.sb("CTre", [128, 32, 32], F32, st), kb.sb("CTimN", [128, 32, 32], F32, st)]
        for pl in range(2):
            for par in range(2):
                kb.ts(dve, Cpad.t[:, :, 64 * par:64 * par + 64], Cn[pl].t[:], mk.t[:, par:par + 1], ALU.mult, [Cn[pl], mk], [Cpad])
            for ct in range(8):
                pb = gps()
                kb.tr(pb.t[:, 0:128], Cpad.t[:, ct, :], identf.t[:], [Cpad, identf], [pb])
                kb.cp(act, CT[pl].t[:, 4 * ct:4 * ct + 4, :].rearrange("p j c -> p (j c)"), pb.t[:, 0:128], [pb], [CT[pl]])
        Gbd = kb.sb("Gbd", [128, 32, 8, 2, 32], BF16, st)
        for m in range(1, 9):
            kb.tt(dve, w1.t[:], CT[0].t[:], bc(Pre.t[:, m, :], 32), ALU.mult, [CT[0], Pre], [w1])
            kb.tt(dve, w2.t[:], CT[1].t[:], bc(Pim.t[:, m, :], 32), ALU.mult, [CT[1], Pim], [w2])
            kb.tt(dve, Gbd.t[:, :, m - 1, 0, :], w1.t[:], w2.t[:], ALU.subtract, [w1, w2], [Gbd])
            kb.tt(dve, w1.t[:], CT[1].t[:], bc(Pre.t[:, m, :], 32), ALU.mult, [CT[1], Pre], [w1])
            kb.tt(dve, w2.t[:], CT[0].t[:], bc(Pim.t[:, m, :], 32), ALU.mult, [CT[0], Pim], [w2])
            kb.tt(dve, w1.t[:], w1.t[:], w2.t[:], ALU.add, [w1, w2], [w1])
            kb.ts(dve, Gbd.t[:, :, m - 1, 1, :], w1.t[:], -1.0, ALU.mult, [w1], [Gbd])
        kb.ts(dve, CT[1].t[:], CT[1].t[:], -1.0, ALU.mult, [CT[1]], [CT[1]])

        Qb = kb.sb("Qb", [128, 8, 2, 128], BF16, st)
        CTb = [kb.sb("CTbre", [128, 32, 32], BF16, st), kb.sb("CTbim", [128, 32, 32], BF16, st)]
        for pl in range(2):
            kb.cp(dve, CTb[pl].t[:], CT[pl].t[:], [CT[pl]], [CTb[pl]])
        Fst = [kb.sb(f"Fst{i}", [128, 2048], BF16, st) for i in range(2)]
        Kst = [kb.sb(f"Kst{i}", [128, 8, 128], BF16, st) for i in range(2)]
        for ct in range(8):
            fs = Fst[ct % 2]; ks = Kst[ct % 2]
            for half in range(4):
                pb = gps()
                for k4 in range(4):
                    idx = half * 4 + k4
                    tau, plane = idx // 2, idx % 2
                    kb.tr(pb.t[:, k4 * 128:(k4 + 1) * 128], Qbd.t[:, ct, 7 - tau, plane, :], identf.t[:], [Qbd, identf], [pb], inc=(k4 == 3))
                kb.cp(act if half % 2 else dve, fs.t[:, half * 512:(half + 1) * 512], pb.t[:, :], [pb], [fs])
            kb.dma(sp, W["s5f"].t[ct], fs.t[:], W["s5f"], fs)
            kb.cp(dve, Qb.t[:], Qbd.t[:, ct, :, :, :], [Qbd], [Qb])
            pb = gps()
            pv = pb.t[:, 0:256].rearrange("p (d c) -> p d c", c=32)
            for j in range(4):
                for dl in range(8):
                    for pl in range(2):
                        kb.mm(pv[32 * j:32 * j + 32, dl, :], Qb.t[:, dl, pl, 32 * j:32 * j + 32], CTb[pl].t[:, 4 * ct + j, :],
                              [Qb, CTb[pl]], [pb], start=(pl == 0), stop=(pl == 1), inc=(j == 3 and dl == 7 and pl == 1),
                              tile_position=(0, 32 * j))
            kb.memset(dve, ks.t[:], 0.0, [ks])
            for j in range(4):
                kb.cp(dve, ks.t[32 * j:32 * j + 32, :, 32 * j:32 * j + 32], pv[32 * j:32 * j + 32, :, :], [pb], [ks])
            kb.dma(sp, W["s5g"].t[ct].rearrange("p (j r) -> p j r", j=4),
                   Gbd.t[:, 4 * ct:4 * ct + 4, :, :, :].rearrange("p j m a c -> p j (m a c)"), W["s5g"], Gbd)
            kb.dma(sp, W["s5k"].t[ct], ks.t[:].rearrange("p d c -> p (d c)"), W["s5k"], ks)

    if stop == 3:
        kb.barrier(); return kb
    xT = kb.sb("xT", [128, 8, TB], F32)
    hT = kb.sb("hT", [128, 8, TB], BF16)
    hTs = [Buf(f"hT{c_}", hT.t) for c_ in range(8)]
    Sp = kb.sb("Sp", [128, 4, 2, 512], F32)
    s5c = [kb.sb("s5c_re", [128, 32], F32), kb.sb("s5c_im", [128, 32], F32)]
    convc = [kb.sb(f"convc{i}", [128, NFT, 1, 2], F32) for i in range(2)]
    convs = [kb.sb(f"convs{i}", [128, NFT, NS, 2], F32) for i in range(2)]
    kTm = [kb.sb(f"kTm{i}", [128, 8, 256], BF16) for i in range(2)]
    vm = [kb.sb(f"vm{i}", [128, 2, D], BF16) for i in range(2)]
    kb.memset(dve, Sp.t[:], 0.0, [Sp])
    for b_ in s5c:
        kb.memset(dve, b_.t[:], 0.0, [b_])
    for i in range(2):
        kb.memset(dve, convc[i].t[:], 0.0, [convc[i]])

    epsb = kb.sb("epsb", [128, 2], F32)
    kb.memset(dve, epsb.t[:, 0:1], EPS, [epsb]); kb.memset(dve, epsb.t[:, 1:2], GN_EPS, [epsb])
    ws = WStream(kb, 4, 4096)

    def sched_block():
        l = [(W["s5f"], ct, 2048) for ct in range(8)]
        l += [(W["s5k"], ct, 1024) for ct in range(8)] + [(W["s5g"], ct, 2048) for ct in range(8)]
        l += [(W["glu"], c, 4096) for c in range(2)]
        l += [(W["mq0"], c, 4096) for c in range(2)] + [(W["mo0"], c, 4096) for c in range(2)]
        l += [(W["up0"], c, 4096) for c in range(11)] + [(W["down0"], c, 4096 if c < 5 else 2048) for c in range(6)]
        for h in range(4):
            l += [(W["rq"], h, 2048), (W["rk"], h, 2048), (W["rv"], h, 4096), (W["rg"], h, 4096)]
        l += [(W["ro"], c, 4096) for c in range(4)]
        l += [(W["mq1"], c, 4096) for c in range(2)] + [(W["mo1"], c, 4096) for c in range(2)]
        l += [(W["up1"], c, 4096) for c in range(11)] + [(W["down1"], c, 4096 if c < 5 else 2048) for c in range(6)]
        return l

    rn_sq = kb.sb("rn_sq", [128, 4, TB], BF16)
    rn_sq2 = kb.sb("rn_sq2", [128, 4, TB], BF16)
    rn_rs = kb.sb("rn_rs", [128, TB], F32)

    def sq_tile(ot, N):
        buf = rn_sq if ot < 4 else rn_sq2
        kb.actf(buf.t[:, ot % 4, 0:N], xT.t[:, ot, 0:N], AF.Square, [xT], [buf])

    def rmsnorm(N, gain, out_buf, presq=False):
        sq = rn_sq; rs = rn_rs
        if not presq:
            kb.actf(sq.t[:, 0:4, 0:N], xT.t[:, 0:4, 0:N], AF.Square, [xT], [sq])
            kb.tt(pool, rn_sq2.t[:, :, 0:N], xT.t[:, 4:8, 0:N], xT.t[:, 4:8, 0:N], ALU.mult, [xT], [rn_sq2])
        pb = gps()
        for ct in range(8):
            sqb = sq if ct < 4 else rn_sq2
            kb.mm(pb.t[:, 0:N], onesb.t[:], sqb.t[:, ct % 4, 0:N], [onesb, sqb], [pb], start=(ct == 0), stop=(ct == 7), inc=(ct == 7))
        kb.actf(rs.t[:, 0:N], pb.t[:, 0:N], AF.Sqrt, [pb, epsb], [rs], scale=1.0 / D, bias=epsb.t[:, 0:1])
        kb.recip(rs.t[:, 0:N], rs.t[:, 0:N], [rs], [rs])
        for ct in range(8):
            ob = out_buf[ct] if isinstance(out_buf, list) else out_buf
            kb.stt(ob.t[:, ct, 0:N], xT.t[:, ct, 0:N], gain.t[:, ct:ct + 1], rs.t[:, 0:N], ALU.mult, ALU.mult, [xT, gain, rs], [ob])

    def gelu(out_ap, out_buf, x_ap, x_buf, t_ap, t_buf):
        kb.actf(t_ap, x_ap, AF.Square, [x_buf], [t_buf], scale=math.sqrt(0.044715))
        kb.stt(t_ap, t_ap, 1.0, x_ap, ALU.add, ALU.mult, [t_buf, x_buf], [t_buf])
        kb.actf(t_ap, t_ap, AF.Sigmoid, [t_buf], [t_buf], scale=GELU_C)
        kb.tt(pool, out_ap, t_ap, x_ap, ALU.mult, [t_buf, x_buf], [out_buf])

    def linear_fm(wname, nch, N, rhs_fn, rhs_bufs, consume):
        scr = W[wname]
        kt_n, oc = scr.kt, scr.oc
        for c in range(nch):
            slot = ws.get(scr, c)
            wv = slot.t[:, 0:kt_n * oc].rearrange("p (k o) -> p k o", o=oc)
            for o_ in range(oc // 128):
                pb = gps()
                for kt in range(kt_n):
                    rb = rhs_bufs(kt) if callable(rhs_bufs) else rhs_bufs
                    kb.mm(pb.t[:, 0:N], wv[:, kt, o_ * 128:(o_ + 1) * 128], rhs_fn(kt), [slot] + rb, [pb],
                          start=(kt == 0), stop=(kt == kt_n - 1), inc=(kt == kt_n - 1))
                consume(c * (oc // 128) + o_, pb)

    def add_to_x(N, presq=False):
        def f(ot, pb):
            kb.tt(dve, xT.t[:, ot, 0:N], xT.t[:, ot, 0:N], pb.t[:, 0:N], ALU.add, [xT, pb], [xT])
            if presq:
                sq_tile(ot, N)
        return f

    def load_x(src, row0, N):
        with kb.scope() as st:
            stg = [kb.sb(f"xstg{i}", [128, D], F32, st) for i in range(2)]
            for tt_ in range(N // 128):
                sg = stg[tt_ % 2]
                kb.dma(sp, sg.t[:], src.t[row0 + tt_ * 128:row0 + (tt_ + 1) * 128, :], sg, wsrc)
                for half in range(2):
                    pb = gps()
                    for k4 in range(4):
                        ct = half * 4 + k4
                        kb.tr(pb.t[:, k4 * 128:(k4 + 1) * 128], sg.t[:, ct * 128:(ct + 1) * 128], identf.t[:], [sg, identf], [pb], inc=(k4 == 3))
                    kb.cp(act if half else dve, xT.t[:, half * 4:half * 4 + 4, tt_ * 128:(tt_ + 1) * 128],
                          pb.t[:, :].rearrange("p (a b) -> p a b", b=128), [pb], [xT])
                    sqb = rn_sq2 if half else rn_sq
                    kb.actf(sqb.t[:, :, tt_ * 128:(tt_ + 1) * 128], xT.t[:, half * 4:half * 4 + 4, tt_ * 128:(tt_ + 1) * 128], AF.Square, [xT], [sqb])

    def store_y(dst, row0, N):
        with kb.scope() as st:
            yF = kb.sb("yF", [128, 8, N], F32, st)
            rmsnorm(N, g_fin, yF)
            stg = [kb.sb(f"ystg{i}", [128, D], F32, st) for i in range(2)]
            for tt_ in range(N // 128):
                sg = stg[tt_ % 2]
                for half in range(2):
                    pb = gps()
                    for k4 in range(4):
                        ct = half * 4 + k4
                        kb.tr(pb.t[:, k4 * 128:(k4 + 1) * 128], yF.t[:, ct, tt_ * 128:(tt_ + 1) * 128], identf.t[:], [yF, identf], [pb], inc=(k4 == 3))
                    kb.cp(act if half else dve, sg.t[:, half * 512:(half + 1) * 512], pb.t[:, :], [pb], [sg])
                kb.dma(sp, dst.t[row0 + tt_ * 128:row0 + (tt_ + 1) * 128, :], sg.t[:], dst, sg)

    def dump_dbg(N):
        kb.dma(sp, dbg_out.t[:, :, 0:N], xT.t[:, :, 0:N], dbg_out, xT)

    def s5_block(N, nseg, is_prompt):
        NK = N // 8
        Lc = NK // nseg
        Ltop = min(8, Lc)
        nl = int(math.log2(Ltop))
        rmsnorm(N, g_mix[0], hTs, presq=True)
        with kb.scope() as st:
            V = [[kb.sb(f"V{l}_{pl}", [128, 32, NK >> l], F32, st) for pl in range(2)] for l in range(nl + 1)]
            S = [kb.sb(f"S_{pl}", [128, 32, nseg, Lc + 1], F32, st) for pl in range(2)]
            Sb = [kb.sb(f"Sb_{pl}", [128, 32, NK], BF16, st) for pl in range(2)]
            T1 = kb.sb("T1", [128, 32, max(NK // 2, 4)], F32, st); T2 = kb.sb("T2", [128, 32, max(NK // 2, 4)], F32, st)
            gT = kb.sb("gT", [128, 8, N], BF16, st)
            g32 = gT.t[:].rearrange("p a n -> p (a n)").bitcast(F32)
            tw = max(NK // 2, 4)
            T3v = g32[:, 0:32 * tw].rearrange("p (g k) -> p g k", k=tw)
            T4v = g32[:, 32 * tw:64 * tw].rearrange("p (g k) -> p g k", k=tw)
            T3 = gT; T4 = gT
            yy = [kb.sb(f"yy{k}", [128, N], F32, st) for k in range(2)]
            tq = [kb.sb(f"tq{k}", [128, N], F32, st) for k in range(2)]
            if is_prompt:
                for pl in range(2):
                    kb.cp(pool, S[pl].t[:, :, 0, 0], s5c[pl].t[:], [s5c[pl]], [S[pl]])
            else:
                natS = kb.sb("natS", [32, 2, NS, 128], F32, st)
                kb.dma(sp, natS.t[:, 0, :, :], sre.t.rearrange("s g q -> g s q"), natS, wsrc)
                kb.dma(sp, natS.t[:, 1, :, :], sim.t.rearrange("s g q -> g s q"), natS, wsrc)
                for pl in range(2):
                    for s_ in range(NS):
                        tr_f32(S[pl].t[:, :, s_, 0], S[pl], natS.t[:, pl, s_, :], natS, 32)
            hv_all = hT.t[:, :, 0:N].rearrange("p c (k t) -> p c k t", t=8)
            if stop == 6:
                return
            import os as _os
            _nct = int(_os.environ.get("P1CT", "99")); _noev = _os.environ.get("P1NOEV") == "1"
            for half in range(2):
                for c4 in range(4):
                    ct = half * 4 + c4
                    if ct >= _nct:
                        continue
                    slot = ws.get(W["s5f"], ct)
                    Fv = slot.t[:, 0:2048].rearrange("p (t a q) -> p t a q", a=2, q=128)
                    for j in range(4):
                        r = slice(32 * j, 32 * j + 32)
                        vb = P[4 + j]
                        vbv = vb.t[:, 0:8 * NK].rearrange("p (c a k) -> p c a k", a=2, k=NK)
                        for pl in range(2):
                            for tau in range(8):
                                kb.mm(vbv[:, c4, pl, :], Fv[r, tau, pl, :], hv_all[r, ct, :, tau], [slot, hTs[ct]], [vb],
                                      start=(tau == 0), stop=(tau == 7), inc=(tau == 7 and j == 3 and pl == 1), tile_position=(32 * j, 0))
                for j in range(4):
                    if _noev or _nct < 99:
                        continue
                    vb = P[4 + j]
                    vbv = vb.t[:, 0:8 * NK].rearrange("p (c a k) -> p c a k", a=2, k=NK)
                    g0 = 16 * half + j
                    kb.cp(act, V[0][0].t[:, g0:g0 + 13:4, :], vbv[:, :, 0, :], [vb], [V[0][0]])
                    kb.cp(dve, V[0][1].t[:, g0:g0 + 13:4, :], vbv[:, :, 1, :], [vb], [V[0][1]])
            if stop == 7:
                return

            def cmadd(dre, dim_, dbufs, lv, sre_, sim_, sbufs, vre, vim, vbufs, shape):
                Are, Aim = Apow[lv]
                nfree = 1
                for x in shape[2:]:
                    nfree *= x
                if len(shape) == 3:
                    t1 = T1.t[:, :, 0:nfree]; t2 = T2.t[:, :, 0:nfree]; t3 = T3v[:, :, 0:nfree]; t4 = T4v[:, :, 0:nfree]
                    bre = Are.t[:].unsqueeze(2).to_broadcast(shape); bim = Aim.t[:].unsqueeze(2).to_broadcast(shape)
                else:
                    t1 = T1.t[:, :, 0:nfree].rearrange("p g (s k) -> p g s k", s=shape[2])
                    t2 = T2.t[:, :, 0:nfree].rearrange("p g (s k) -> p g s k", s=shape[2])
                    t3 = T3v[:, :, 0:nfree].rearrange("p g (s k) -> p g s k", s=shape[2])
                    t4 = T4v[:, :, 0:nfree].rearrange("p g (s k) -> p g s k", s=shape[2])
                    bre = Are.t[:].unsqueeze(2).unsqueeze(3).to_broadcast(shape); bim = Aim.t[:].unsqueeze(2).unsqueeze(3).to_broadcast(shape)
                kb.tt(pool, t1, sre_, bre, ALU.mult, sbufs + [Are], [T1])
                kb.tt(pool, t2, sim_, bim, ALU.mult, sbufs + [Aim], [T2])
                kb.tt(pool, t1, t1, t2, ALU.subtract, [T1, T2], [T1])
                kb.tt(dve, t3, sim_, bre, ALU.mult, sbufs + [Are], [T3])
                kb.tt(dve, t4, sre_, bim, ALU.mult, sbufs + [Aim], [T4])
                kb.tt(dve, t3, t3, t4, ALU.add, [T3, T4], [T3])
                kb.tt(pool, dre, t1, vre, ALU.add, [T1] + vbufs, [dbufs[0]])
                kb.tt(dve, dim_, t3, vim, ALU.add, [T3] + vbufs, [dbufs[1]])

            ybanks = [P[6], P[7], P[0], P[1], P[2], P[3], P[4], P[5]]

            def toep(ct, yb):
                slot = ws.get(W["s5k"], ct)
                Kv = slot.t[:, 0:1024].rearrange("p (d c) -> p d c", c=128)
                yv = yb.t[:, 0:N].rearrange("p (k t) -> p k t", t=8)
                kb.mm(yb.t[:, 0:N], Kv[:, 0, :], hT.t[:, ct, 0:N], [slot, hTs[ct]], [yb], start=True, stop=False, inc=False)
                for dl in range(1, 8):
                    for tau in range(dl, 8):
                        kb.mm(yv[:, :, tau], Kv[:, dl, :], hv_all[:, ct, :, tau - dl], [slot, hTs[ct]], [yb], start=False, stop=False,
                              inc=(dl == 7))

            def gpart(ct, yb):
                slot = ws.get(W["s5g"], ct)
                Gv = slot.t[:, 0:2048].rearrange("p (j t a c) -> p j t a c", j=4, t=8, a=2)
                yv = yb.t[:, 0:N].rearrange("p (k t) -> p k t", t=8)
                for j in range(4):
                    for pl in range(2):
                        for tau in range(8):
                            last = (j == 3 and pl == 1 and tau == 7)
                            kb.mm(yv[32 * j:32 * j + 32, :, tau], Gv[:, j, tau, pl, :], Sb[pl].t[:, 4 * ct + j, :], [slot, Sb[pl]], [yb],
                                  start=False, stop=(pl == 1 and tau == 7), inc=last, tile_position=(0, 32 * j))
                y_ = yy[ct % 2]; t_ = tq[ct % 2]
                kb.stt(y_.t[:], hT.t[:, ct, 0:N], v_d.t[:, ct:ct + 1], yb.t[:, 0:N], ALU.mult, ALU.add, [hTs[ct], v_d, yb], [y_])
                gelu(gT.t[:, ct, :], gT, y_.t[:], y_, t_.t[:], t_)

            for ct in range(8):
                toep(ct, ybanks[ct])

            for l in range(nl):
                n2 = NK >> (l + 1)
                src = [V[l][pl].t[:].rearrange("p g (k two) -> p g k two", two=2) for pl in range(2)]
                cmadd(V[l + 1][0].t[:], V[l + 1][1].t[:], V[l + 1], 8 << l, src[0][:, :, :, 0], src[1][:, :, :, 0], V[l],
                      src[0][:, :, :, 1], src[1][:, :, :, 1], V[l], [128, 32, n2])
            nt = Lc // Ltop
            vtop = [V[nl][pl].t[:].rearrange("p g (s u) -> p g s u", s=nseg) for pl in range(2)]
            for u in range(nt):
                cmadd(S[0].t[:, :, :, (u + 1) * Ltop], S[1].t[:, :, :, (u + 1) * Ltop], S, 8 << nl,
                      S[0].t[:, :, :, u * Ltop], S[1].t[:, :, :, u * Ltop], S, vtop[0][:, :, :, u], vtop[1][:, :, :, u], V[nl], [128, 32, nseg])
            for l in range(nl - 1, -1, -1):
                stp = 1 << l
                cnt = Lc // (2 * stp)
                vv = [V[l][pl].t[:].rearrange("p g (s k two) -> p g s k two", s=nseg, two=2) for pl in range(2)]
                dsl = slice(stp, stp + 2 * stp * (cnt - 1) + 1, 2 * stp)
                ssl = slice(0, 2 * stp * (cnt - 1) + 1, 2 * stp)
                cmadd(S[0].t[:, :, :, dsl], S[1].t[:, :, :, dsl], S, 8 << l, S[0].t[:, :, :, ssl], S[1].t[:, :, :, ssl], S,
                      vv[0][:, :, :, :, 0], vv[1][:, :, :, :, 0], V[l], [128, 32, nseg, cnt])
            if stop == 8:
                return
            for pl in range(2):
                kb.cp(act if pl else dve, Sb[pl].t[:].rearrange("p g (s k) -> p g s k", s=nseg), S[pl].t[:, :, :, 0:Lc], [S[pl]], [Sb[pl]])
            if stop == 9:
                return
            for ct in range(8):
                gpart(ct, ybanks[ct])
            if is_prompt:
                for pl in range(2):
                    kb.cp(pool, s5c[pl].t[:], S[pl].t[:, :, 0, Lc], [S[pl]], [s5c[pl]])
            else:
                fin = kb.sb("fin", [128, 2, NS, 32], F32, st)
                fo = kb.sb("fo", [32, 2, NS, 128], F32, st)
                for pl in range(2):
                    kb.cp(pool, fin.t[:, pl, :, :], S[pl].t[:, :, :, Lc].rearrange("p g s -> p s g"), [S[pl]], [fin])
                    for s_ in range(NS):
                        pb = gps()
                        kb.tr(pb.t[0:32, 0:128], fin.t[:, pl, s_, :], identf.t[:], [fin, identf], [pb])
                        kb.cp(dve, fo.t[:, pl, s_, :], pb.t[0:32, 0:128], [pb], [fo])
                kb.dma(sp, o_sre_s.t.rearrange("s g q -> g s q"), fo.t[:, 0, :, :], o_sre_s, fo)
                kb.dma(sp, o_sim_s.t.rearrange("s g q -> g s q"), fo.t[:, 1, :, :], o_sim_s, fo)
            if stages < 1 or stop == 10:
                return
            def glu_c(ot, pb):
                t_ = tq[ot % 2]
                kb.actf(t_.t[:], pb.t[:, 0:N], AF.Sigmoid, [pb, v_bglu], [t_], bias=v_bglu.t[:, ot:ot + 1])
                if ot > 0:
                    sq_tile(ot - 1, N)
                kb.tt(dve, t_.t[:], t_.t[:], gT.t[:, ot, :], ALU.mult, [t_, gT], [t_])
                kb.tt(dve, xT.t[:, ot, 0:N], xT.t[:, ot, 0:N], t_.t[:], ALU.add, [xT, t_], [xT])
            linear_fm("glu", 2, N, lambda kt: gT.t[:, kt, :], [gT], glu_c)
            sq_tile(7, N)

    def memattn(i, N, is_prompt):
        rmsnorm(N, g_mq[i], hTs, presq=True)
        with kb.scope() as st:
            qT = kb.sb("qT", [128, 8, N], BF16, st); oT = kb.sb("oT", [128, 8, N], BF16, st)
            E = [kb.sb(f"E{k}", [128, 2, N], BF16, st) for k in range(2)]
            rds = [kb.sb(f"rd{k}", [128, N], F32, st) for k in range(2)]
            if is_prompt:
                segs = [(0, N, kTm[i], vm[i])]
            else:
                segs = []
                Kf = [kb.sb(f"Kf{k}", [128, 2, D], F32, st) for k in range(2)]
                for s_ in range(NS):
                    kTs = kb.sb(f"kTs{s_}", [128, 8, 256], BF16, st); vs_ = kb.sb(f"vs{s_}", [128, 2, D], BF16, st)
                    kf = Kf[0]; vf = Kf[1]
                    kb.dma(sp, kf.t[:], cmk.t[i, s_].rearrange("(mt p) d -> p mt d", p=128), kf, wsrc)
                    kb.dma(sp, vf.t[:], cmv.t[i, s_].rearrange("(mt p) d -> p mt d", p=128), vf, wsrc)
                    kb.cp(pool, vs_.t[:], vf.t[:], [vf], [vs_])
                    for mt in range(2):
                        for half in range(2):
                            pb = gps()
                            for k4 in range(4):
                                c = half * 4 + k4
                                kb.tr(pb.t[:, k4 * 128:(k4 + 1) * 128], kf.t[:, mt, c * 128:(c + 1) * 128], identf.t[:], [kf, identf], [pb], inc=(k4 == 3))
                            kb.cp(act if half else dve, kTs.t[:, half * 4:half * 4 + 4, mt * 128:(mt + 1) * 128],
                                  pb.t[:, :].rearrange("p (a b) -> p a b", b=128), [pb], [kTs])
                    segs.append((LS * s_, LS, kTs, vs_))
            linear_fm(f"mq{i}", 2, N, lambda kt: hT.t[:, kt, 0:N], lambda kt: [hTs[kt]],
                      lambda ot, pb: kb.actf(qT.t[:, ot, :], pb.t[:, 0:N], AF.Copy, [pb], [qT], scale=1.0 / 16.0))
            abank_i = [0]

            def abank():
                bnk = P[abank_i[0] % 8]
                abank_i[0] += 1
                return bnk

            pss_h = {}

            def scores(h):
                pss = [abank(), abank()]
                pss_h[h] = pss
                for (c0, n, kT_b, v_b) in segs:
                    for mt in range(2):
                        for dt_ in range(2):
                            kb.mm(pss[mt].t[:, c0:c0 + n], kT_b.t[:, 2 * h + dt_, mt * 128:(mt + 1) * 128], qT.t[:, 2 * h + dt_, c0:c0 + n],
                                  [kT_b, qT], [pss[mt]], start=(dt_ == 0), stop=(dt_ == 1), inc=(dt_ == 1))
                Eh = E[h % 2]
                for mt in range(2):
                    kb.actf(Eh.t[:, mt, :], pss[mt].t[:, 0:N], AF.Exp, [pss[mt]], [Eh])

            def pv(h):
                Eh = E[h % 2]
                rdh = rds[h % 2]
                pd = abank()
                for mt in range(2):
                    kb.mm(pd.t[:, 0:N], onesb.t[:], Eh.t[:, mt, :], [onesb, Eh], [pd], start=(mt == 0), stop=(mt == 1), inc=(mt == 1))
                kb.recip(rdh.t[:], pd.t[:, 0:N], [pd], [rdh])
                for dt_ in range(2):
                    po = abank()
                    for (c0, n, kT_b, v_b) in segs:
                        for mt in range(2):
                            kb.mm(po.t[:, c0:c0 + n], v_b.t[:, mt, (2 * h + dt_) * 128:(2 * h + dt_ + 1) * 128], Eh.t[:, mt, c0:c0 + n],
                                  [v_b, Eh], [po], start=(mt == 0), stop=(mt == 1), inc=(mt == 1))
                    kb.tt(dve, oT.t[:, 2 * h + dt_, :], po.t[:, 0:N], rdh.t[:], ALU.mult, [po, rdh], [oT])

            scores(0)
            for h in range(4):
                if h + 1 < 4:
                    scores(h + 1)
                pv(h)
            linear_fm(f"mo{i}", 2, N, lambda kt: oT.t[:, kt, :], [oT], add_to_x(N, presq=True))

    def ffn(i, N, nseg, carry):
        L = N // nseg
        NB = 4
        rmsnorm(N, g_ffn[i], hTs, presq=True)
        with kb.scope() as st:
            actTs = [kb.sb(f"actT{c}", [128, 4, N], BF16, st) for c in range(6)]
            aext = [kb.sb(f"aext{k}", [128, nseg, L + 2], F32, st) for k in range(NB)]
            tcs = [kb.sb(f"tc{k}", [128, nseg, L], F32, st) for k in range(NB)]
            tgs = [kb.sb(f"tg{k}", [128, nseg, L], F32, st) for k in range(NB)]
            gss = [kb.sb(f"gs{k}", [128, N], BF16, st) for k in range(NB)]
            scr = W[f"up{i}"]
            cur = {}
            bank_i = [0]

            def bank():
                bnk = P[bank_i[0] % 8]
                bank_i[0] += 1
                return bnk

            def stage_a(ft):
                c, u = ft // 2, ft % 2
                if u == 0:
                    cur["slot"] = ws.get(scr, c)
                slot = cur["slot"]
                wv = slot.t[:, 0:4096].rearrange("p (k o) -> p k o", o=512)
                pa = bank(); pg = bank()
                for kt in range(8):
                    kb.mm(pa.t[:, 0:N], wv[:, kt, u * 128:(u + 1) * 128], hT.t[:, kt, 0:N], [slot, hTs[kt]], [pa], start=(kt == 0), stop=(kt == 7), inc=(kt == 7))
                for kt in range(8):
                    kb.mm(pg.t[:, 0:N], wv[:, kt, 256 + u * 128:256 + (u + 1) * 128], hT.t[:, kt, 0:N], [slot, hTs[kt]], [pg], start=(kt == 0), stop=(kt == 7), inc=(kt == 7))
                ae = aext[ft % NB]; tcb = tcs[ft % NB]; gs = gss[ft % NB]
                kb.cp(pool, ae.t[:, :, 0:2], carry.t[:, ft, :, :], [carry], [ae])
                kb.actf(ae.t[:, :, 2:L + 2], pa.t[:, 0:N].rearrange("p (s l) -> p s l", l=L), AF.Copy, [pa], [ae])
                kb.cp(act, gs.t[:], pg.t[:, 0:N], [pg], [gs])
                kb.cp(pool, carry.t[:, ft, :, :], ae.t[:, :, L:L + 2], [ae], [carry])
                kb.actf(tcb.t[:], ae.t[:, :, 2:L + 2], AF.Identity, [ae, cw[i][2], cb[i]], [tcb], scale=cw[i][2].t[:, ft:ft + 1], bias=cb[i].t[:, ft:ft + 1])
                kb.stt(tcb.t[:], ae.t[:, :, 1:L + 1], cw[i][1].t[:, ft:ft + 1], tcb.t[:], ALU.mult, ALU.add, [ae, cw[i][1], tcb], [tcb])
                kb.stt(tcb.t[:], ae.t[:, :, 0:L], cw[i][0].t[:, ft:ft + 1], tcb.t[:], ALU.mult, ALU.add, [ae, cw[i][0], tcb], [tcb])

            def stage_b(ft):
                tcb = tcs[ft % NB]; tgb = tgs[ft % NB]
                kb.actf(tgb.t[:], tcb.t[:], AF.Square, [tcb], [tgb], scale=math.sqrt(0.044715))
                kb.stt(tgb.t[:], tgb.t[:], 1.0, tcb.t[:], ALU.add, ALU.mult, [tgb, tcb], [tgb])

            def stage_c(ft):
                tcb = tcs[ft % NB]; tgb = tgs[ft % NB]; gs = gss[ft % NB]
                kb.actf(tgb.t[:], tgb.t[:], AF.Sigmoid, [tgb], [tgb], scale=GELU_C)
                kb.tt(pool, tgb.t[:], tgb.t[:], tcb.t[:], ALU.mult, [tgb, tcb], [tgb])
                kb.tt(pool, actTs[ft // 4].t[:, ft % 4, :].rearrange("p (s l) -> p s l", l=L), tgb.t[:], gs.t[:].rearrange("p (s l) -> p s l", l=L),
                      ALU.mult, [tgb, gs], [actTs[ft // 4]])

            dscr = W[f"down{i}"]

            def down_chunk(c):
                slot = ws.get(dscr, c)
                wv = slot.t[:, 0:4096].rearrange("p (k o) -> p k o", o=1024)
                nk = 4 if c < 5 else 2
                for ot in range(8):
                    for k in range(nk):
                        ft = 4 * c + k
                        kb.mm(P[ot].t[:, 0:N], wv[:, k, ot * 128:(ot + 1) * 128], actTs[c].t[:, k, :], [slot, actTs[c]], [P[ot]],
                              start=(ft == 0), stop=(ft == NFT - 1), inc=(k == nk - 1 and (ot == 7 or c == 5)))

            for step in range(NFT + 2):
                if step < NFT:
                    stage_a(step)
                if 0 <= step - 2 < NFT:
                    stage_c(step - 2)
                if 0 <= step - 1 < NFT:
                    stage_b(step - 1)
                if step == NFT - 1:
                    for c in range(5):
                        down_chunk(c)
            down_chunk(5)
            for ot in range(8):
                add_to_x(N)(ot, P[ot])

    def retention(N, is_prompt, blk):
        nmt = N // 128
        rmsnorm(N, g_mix[1], hTs)
        with kb.scope() as st:
            Cb = kb.sb("Cb", [128, N], F32, st); Sbt = kb.sb("Sbt", [128, N], F32, st)
            Ck = kb.sb("Ck", [128, N], F32, st); Sk = kb.sb("Sk", [128, N], F32, st)
            tA = kb.sb("tA", [128, N], F32, st); tB = kb.sb("tB", [128, N], F32, st)
            tC = kb.sb("tC", [128, N], F32, st); tD = kb.sb("tD", [128, N], F32, st)
            tE = kb.sb("tE", [128, N], F32, st); tF = kb.sb("tF", [128, N], F32, st)
            ygT = kb.sb("ygT", [128, 16, N], BF16, st)
            qTh = [kb.sb(f"qTh{k}", [128, 2, N], BF16, st) for k in range(2)]
            kTh = [kb.sb(f"kTh{k}", [128, 2, N], BF16, st) for k in range(2)]
            kTf = [kb.sb("kTf0", [128, 2, N], F32, st)] * 2
            vTok = [kb.sb(f"vTok{k}", [128, nmt, 512], BF16, st) for k in range(2)]
            kTok = [kb.sb(f"kTok{k}", [128, nmt, 256], BF16, st) for k in range(2)]
            PT = [kb.sb(f"PT{k}", [128, nmt, N], BF16, st) for k in range(2)]
            of32 = [kb.sb(f"of32_{k}", [128, 4, N], F32, st) for k in range(1)]; osq = kb.sb("osq", [128, 4, N], BF16, st)
            mean = kb.sb("mean", [128, N], F32, st); var = kb.sb("var", [128, N], F32, st)
            nseg = 1 if is_prompt else NS
            Lg = N // nseg
            S16 = [kb.sb(f"S16_{s_}", [128, 2, 512], BF16, st) for s_ in range(nseg)]
            Ssf = None if is_prompt else [kb.sb(f"Ssf_{s_}", [128, 2, 512], F32, st) for s_ in range(NS)]
            col = blk if is_prompt else 8
            if is_prompt:
                c0v = C0.t[:, 0:N]; s0v = S0.t[:, 0:N]
                vw = lambda b: b.t[:]
            else:
                c0v = C0.t[:, 0:LS].unsqueeze(1).to_broadcast([128, NS, LS]); s0v = S0.t[:, 0:LS].unsqueeze(1).to_broadcast([128, NS, LS])
                vw = lambda b: b.t[:].rearrange("p (s l) -> p s l", l=LS)
            kb.ts(dve, vw(tA), s0v, dS.t[:, col:col + 1], ALU.mult, [S0, dS], [tA])
            kb.stt(vw(Cb), c0v, dC.t[:, col:col + 1], vw(tA), ALU.mult, ALU.subtract, [C0, dC, tA], [Cb])
            kb.ts(dve, vw(tB), c0v, dS.t[:, col:col + 1], ALU.mult, [C0, dS], [tB])
            kb.stt(vw(Sbt), s0v, dC.t[:, col:col + 1], vw(tB), ALU.mult, ALU.add, [S0, dC, tB], [Sbt])
            kb.ts(dve, Ck.t[:], Cb.t[:], 1.0 / 16.0, ALU.mult, [Cb], [Ck])
            kb.ts(dve, Sk.t[:], Sbt.t[:], 1.0 / 16.0, ALU.mult, [Sbt], [Sk])
            gpow = [math.exp(LG[h] * Lg) for h in range(4)]
            def proj(h):
                qh = qTh[h % 2]; kh = kTh[h % 2]; kf = kTf[h % 2]; vt = vTok[h % 2]; kt_ = kTok[h % 2]; pt = PT[h % 2]
                inner = None
                inner = innerp.t[:, h, 0:N] if is_prompt else inners.t[:, h, :]
                slot = ws.get(W["rq"], h)
                wv = slot.t[:, 0:2048].rearrange("p (k o) -> p k o", o=256)
                p1 = gps(); p2 = gps()
                for kt in range(8):
                    kb.mm(p1.t[:, 0:N], wv[:, kt, 0:128], hT.t[:, kt, 0:N], [slot, hTs[kt]], [p1], start=(kt == 0), stop=(kt == 7), inc=(kt == 7))
                for kt in range(8):
                    kb.mm(p2.t[:, 0:N], wv[:, kt, 128:256], hT.t[:, kt, 0:N], [slot, hTs[kt]], [p2], start=(kt == 0), stop=(kt == 7), inc=(kt == 7))
                kb.tt(dve, tA.t[:], p1.t[:, 0:N], Cb.t[:], ALU.mult, [p1, Cb], [tA])
                kb.tt(dve, tB.t[:], p2.t[:, 0:N], Sbt.t[:], ALU.mult, [p2, Sbt], [tB])
                kb.tt(dve, tE.t[:], p2.t[:, 0:N], Cb.t[:], ALU.mult, [p2, Cb], [tE])
                kb.tt(dve, tF.t[:], p1.t[:, 0:N], Sbt.t[:], ALU.mult, [p1, Sbt], [tF])
                kb.tt(pool, tA.t[:], tA.t[:], tB.t[:], ALU.subtract, [tA, tB], [tA])
                kb.tt(pool, qh.t[:, 0, :], tA.t[:], inner, ALU.mult, [tA, innerp, inners], [qh])
                kb.tt(pool, tE.t[:], tE.t[:], tF.t[:], ALU.add, [tE, tF], [tE])
                kb.tt(pool, qh.t[:, 1, :], tE.t[:], inner, ALU.mult, [tE, innerp, inners], [qh])
                slot = ws.get(W["rk"], h)
                wv = slot.t[:, 0:2048].rearrange("p (k o) -> p k o", o=256)
                p1 = gps(); p2 = gps()
                for kt in range(8):
                    kb.mm(p1.t[:, 0:N], wv[:, kt, 0:128], hT.t[:, kt, 0:N], [slot, hTs[kt]], [p1], start=(kt == 0), stop=(kt == 7), inc=(kt == 7))
                for kt in range(8):
                    kb.mm(p2.t[:, 0:N], wv[:, kt, 128:256], hT.t[:, kt, 0:N], [slot, hTs[kt]], [p2], start=(kt == 0), stop=(kt == 7), inc=(kt == 7))
                kb.tt(dve, tA.t[:], p1.t[:, 0:N], Ck.t[:], ALU.mult, [p1, Ck], [tA])
                kb.tt(dve, tB.t[:], p2.t[:, 0:N], Sk.t[:], ALU.mult, [p2, Sk], [tB])
                kb.tt(dve, tE.t[:], p2.t[:, 0:N], Ck.t[:], ALU.mult, [p2, Ck], [tE])
                kb.tt(dve, tF.t[:], p1.t[:, 0:N], Sk.t[:], ALU.mult, [p1, Sk], [tF])
                kb.tt(pool, kf.t[:, 0, :], tA.t[:], tB.t[:], ALU.subtract, [tA, tB], [kf])
                kb.tt(pool, kf.t[:, 1, :], tE.t[:], tF.t[:], ALU.add, [tE, tF], [kf])
                kb.cp(act, kh.t[:], kf.t[:], [kf], [kh])
                slot = ws.get(W["rv"], h)
                wv = slot.t[:, 0:4096].rearrange("p (k o) -> p k o", o=512)
                for mt in range(nmt):
                    pb = gps()
                    for kt in range(8):
                        kb.mm(pb.t[:, 0:512], hT.t[:, kt, mt * 128:(mt + 1) * 128], wv[:, kt, :], [slot, hTs[kt]], [pb], start=(kt == 0), stop=(kt == 7), inc=(kt == 7))
                    kb.cp(act, vt.t[:, mt, :], pb.t[:, 0:512], [pb], [vt])
                slot = ws.get(W["rg"], h)
                wv = slot.t[:, 0:4096].rearrange("p (k o) -> p k o", o=512)
                for et in range(4):
                    pb = gps()
                    for kt in range(8):
                        kb.mm(pb.t[:, 0:N], wv[:, kt, et * 128:(et + 1) * 128], hT.t[:, kt, 0:N], [slot, hTs[kt]], [pb], start=(kt == 0), stop=(kt == 7), inc=(kt == 7))
                    kb.actf(ygT.t[:, 4 * h + et, :], pb.t[:, 0:N], AF.Silu, [pb], [ygT])
                for mt in range(nmt):
                    pb = gps()
                    for dt_ in range(2):
                        kb.tr(pb.t[:, dt_ * 128:(dt_ + 1) * 128], kf.t[:, dt_, mt * 128:(mt + 1) * 128], identf.t[:], [kf, identf], [pb], inc=(dt_ == 1))
                    tl = tailp.t[:, h, mt:mt + 1] if is_prompt else tails.t[:, h:h + 1]
                    kb.actf(kt_.t[:, mt, :], pb.t[:, 0:256], AF.Identity, [pb, tailp, tails], [kt_], scale=tl)
                for mt in range(nmt):
                    lo = 128 * mt if is_prompt else 0
                    pb = gps()
                    for dt_ in range(2):
                        kb.mm(pb.t[:, 0:N - lo], kh.t[:, dt_, mt * 128:(mt + 1) * 128], qh.t[:, dt_, lo:N], [kh, qh], [pb],
                              start=(dt_ == 0), stop=(dt_ == 1), inc=(dt_ == 1))
                    if is_prompt:
                        kb.stt(pt.t[:, mt, lo:N], pb.t[:, 0:N - lo], tinvp.t[:, h, mt:mt + 1], M01.t[:, 0:N - lo], ALU.mult, ALU.mult, [pb, tinvp, M01], [pt])
                    else:
                        kb.stt(pt.t[:, mt, :], pb.t[:, 0:N], tinvs.t[:, h:h + 1], M01s.t[:], ALU.mult, ALU.mult, [pb, tinvs, M01s], [pt])
            def attn_a(h):
                qh = qTh[h % 2]; kh = kTh[h % 2]; kf = kTf[h % 2]; vt = vTok[h % 2]; kt_ = kTok[h % 2]; pt = PT[h % 2]
                if is_prompt:
                    kb.cp(act, S16[0].t[:], Sp.t[:, h, :, :], [Sp], [S16[0]])
                else:
                    for s_ in range(NS):
                        kb.dma(sp, Ssf[s_].t[:], sret.t[s_, h].rearrange("(dt p) e -> p dt e", p=128), Ssf[s_], wsrc)
                        kb.cp(act, S16[s_].t[:], Ssf[s_].t[:], [Ssf[s_]], [S16[s_]])
                for et in range(4):
                    ob = P[4 + et]
                    for mt in range(nmt):
                        lo = 128 * mt if is_prompt else 0
                        kb.mm(ob.t[:, lo:N], vt.t[:, mt, et * 128:(et + 1) * 128], pt.t[:, mt, lo:N], [vt, pt], [ob], start=(mt == 0), stop=False, inc=False)
                    for s_ in range(nseg):
                        for dt_ in range(2):
                            last = (s_ == nseg - 1 and dt_ == 1)
                            kb.mm(ob.t[:, s_ * Lg:(s_ + 1) * Lg], S16[s_].t[:, dt_, et * 128:(et + 1) * 128], qh.t[:, dt_, s_ * Lg:(s_ + 1) * Lg],
                                  [S16[s_], qh], [ob], start=False, stop=last, inc=last)
                of = of32[0]
                for et in range(4):
                    kb.cp(act, of.t[:, et, :], P[4 + et].t[:, 0:N], [P[4 + et]], [of])
                    kb.actf(osq.t[:, et, :], P[4 + et].t[:, 0:N], AF.Square, [P[4 + et]], [osq])
                for s_ in range(nseg):
                    for dt_ in range(2):
                        pb = gps()
                        if is_prompt:
                            for mt in range(nmt):
                                kb.mm(pb.t[:, 0:512], kt_.t[:, mt, dt_ * 128:(dt_ + 1) * 128], vt.t[:, mt, :], [kt_, vt], [pb],
                                      start=(mt == 0), stop=(mt == nmt - 1), inc=(mt == nmt - 1))
                            sdst = Sp.t[:, h, dt_, :]; sbuf_ = Sp
                        else:
                            r = slice(32 * s_, 32 * s_ + 32)
                            kb.mm(pb.t[:, 0:512], kt_.t[r, 0, dt_ * 128:(dt_ + 1) * 128], vt.t[r, 0, :], [kt_, vt], [pb], start=True, stop=True,
                                  tile_position=(32 * s_, 0))
                            sdst = Ssf[s_].t[:, dt_, :]; sbuf_ = Ssf[s_]
                        kb.stt(sdst, sdst, gpow[h], pb.t[:, 0:512], ALU.mult, ALU.add, [sbuf_, pb], [sbuf_])
                    if not is_prompt:
                        kb.dma(sp, o_ret_s.t[s_, h].rearrange("(dt p) e -> p dt e", p=128), Ssf[s_].t[:], o_ret_s, Ssf[s_])
            def attn_b(h):
                qh = qTh[h % 2]; kh = kTh[h % 2]; kf = kTf[h % 2]; vt = vTok[h % 2]; kt_ = kTok[h % 2]; pt = PT[h % 2]
                of = of32[0]
                psm = gps(); psq = gps()
                for et in range(4):
                    kb.mm(psm.t[:, 0:N], onesf.t[:], of.t[:, et, :], [onesf, of], [psm], start=(et == 0), stop=(et == 3), inc=(et == 3))
                for et in range(4):
                    kb.mm(psq.t[:, 0:N], onesb.t[:], osq.t[:, et, :], [onesb, osq], [psq], start=(et == 0), stop=(et == 3), inc=(et == 3))
                kb.actf(mean.t[:], psm.t[:, 0:N], AF.Copy, [psm], [mean], scale=1.0 / 512.0)
                kb.tt(dve, tD.t[:], mean.t[:], mean.t[:], ALU.mult, [mean], [tD])
                kb.stt(var.t[:], psq.t[:, 0:N], 1.0 / 512.0, tD.t[:], ALU.mult, ALU.subtract, [psq, tD], [var])
                kb.actf(var.t[:], var.t[:], AF.Sqrt, [var, epsb], [var], bias=epsb.t[:, 1:2])
                kb.recip(var.t[:], var.t[:], [var], [var])
                for et in range(4):
                    tcx = tC
                    kb.tt(dve, tcx.t[:], of.t[:, et, :], mean.t[:], ALU.subtract, [of, mean], [tcx])
                    kb.tt(dve, tcx.t[:], tcx.t[:], var.t[:], ALU.mult, [tcx, var], [tcx])
                    kb.tt(pool, ygT.t[:, 4 * h + et, :], tcx.t[:], ygT.t[:, 4 * h + et, :], ALU.mult, [tcx, ygT], [ygT])
            proj(0)
            proj(1)
            attn_a(0)
            proj(2)
            attn_b(0)
            attn_a(1)
            proj(3)
            attn_b(1)
            attn_a(2)
            attn_b(2)
            attn_a(3)
            attn_b(3)
            linear_fm("ro", 4, N, lambda kt: ygT.t[:, kt, :], [ygT], add_to_x(N, presq=True))

    with kb.scope() as st:
        mtk = kb.sb("memt", [128, 2, D], F32, st)
        kb.dma(sp, mtk.t[:], mem.t.rearrange("(mt p) d -> p mt d", p=128), mtk, wsrc)
        ssq = kb.sb("ssq", [128, 2], F32, st); junk = kb.sb("junk", [128, D], BF16, st)
        for mt in range(2):
            kb.actf(junk.t[:], mtk.t[:, mt, :], AF.Square, [mtk], [junk, ssq], accum_out=ssq.t[:, mt:mt + 1])
        kb.actf(ssq.t[:], ssq.t[:], AF.Sqrt, [ssq, epsb], [ssq], scale=1.0 / D, bias=epsb.t[:, 0:1])
        kb.recip(ssq.t[:], ssq.t[:], [ssq], [ssq])
        for mt in range(2):
            kb.ts(dve, mtk.t[:, mt, :], mtk.t[:, mt, :], ssq.t[:, mt:mt + 1], ALU.mult, [mtk, ssq], [mtk])
        memnT = kb.sb("memnT", [128, 8, 256], F32, st)
        for mt in range(2):
            for half in range(2):
                pb = gps()
                for k4 in range(4):
                    ct = half * 4 + k4
                    kb.tr(pb.t[:, k4 * 128:(k4 + 1) * 128], mtk.t[:, mt, ct * 128:(ct + 1) * 128], identf.t[:], [mtk, identf], [pb], inc=(k4 == 3))
                kb.cp(act if half else dve, memnT.t[:, half * 4:half * 4 + 4, mt * 128:(mt + 1) * 128],
                      pb.t[:, :].rearrange("p (a b) -> p a b", b=128), [pb], [memnT])
        memhT = kb.sb("memhT", [128, 8, 256], BF16, st)
        wsl = [kb.sb(f"wsl{k}", [128, 4096], BF16, st) for k in range(2)]
        kof = [kb.sb(f"kof{k}", [128, 512], F32, st) for k in range(2)]
        nko = 0
        for i in range(2):
            for ct in range(8):
                kb.ts(dve, memhT.t[:, ct, :], memnT.t[:, ct, :], g_mkv[i].t[:, ct:ct + 1], ALU.mult, [memnT, g_mkv[i]], [memhT])
            scr = W[f"mkv{i}"]
            for c in range(4):
                sl = wsl[c % 2]
                scr.load(kb, sl, c, 4096)
                wv = sl.t[:].rearrange("p (k o) -> p k o", o=512)
                if c < 2:
                    for o_ in range(4):
                        pb = gps()
                        for kt in range(8):
                            kb.mm(pb.t[:, 0:256], wv[:, kt, o_ * 128:(o_ + 1) * 128], memhT.t[:, kt, :], [sl, memhT], [pb], start=(kt == 0), stop=(kt == 7), inc=(kt == 7))
                        kb.cp(act, kTm[i].t[:, c * 4 + o_, :], pb.t[:, 0:256], [pb], [kTm[i]])
                for mt in range(2):
                    pb = gps()
                    for kt in range(8):
                        kb.mm(pb.t[:, 0:512], memhT.t[:, kt, mt * 128:(mt + 1) * 128], wv[:, kt, :], [sl, memhT], [pb], start=(kt == 0), stop=(kt == 7), inc=(kt == 7))
                    ko = kof[nko % 2]; nko += 1
                    kb.cp(dve, ko.t[:], pb.t[:, 0:512], [pb], [ko])
                    dsto = o_mk if c < 2 else o_mv
                    kb.dma(sp, dsto.t[i, mt * 128:(mt + 1) * 128, (c % 2) * 512:(c % 2 + 1) * 512], ko.t[:], dsto, ko)
                    if c >= 2:
                        kb.cp(act, vm[i].t[:, mt, (c - 2) * 512:(c - 1) * 512], pb.t[:, 0:512], [pb], [vm[i]])

    if stop == 4:
        kb.barrier(); return kb
    with kb.scope() as st:
        cvn = [kb.sb(f"cvn{k}", [88, 128], F32, st) for k in range(2)]
        for i in range(2):
            for hh in range(2):
                cn = cvn[hh]
                kb.dma(sp, cn.t[:], cconv.t[i][2 * hh:2 * hh + 2].rearrange("s r (ft p) -> (s r ft) p", p=128), cn, wsrc)
                pb = gps()
                kb.tr(pb.t[:, 0:88], cn.t[:], identf.t[0:88, 0:88], [cn, identf], [pb])
                kb.cp(dve, convs[i].t[:, :, 2 * hh:2 * hh + 2, :], pb.t[:, 0:88].rearrange("p (s r ft) -> p ft s r", s=2, r=2), [pb], [convs[i]])
    cast_done = [False]
    def run_block(N, is_prompt, blk):
        nseg = 1 if is_prompt else NS
        if is_prompt:
            load_x(xp, blk * TB, N)
        else:
            load_x(xs, 0, N)
        if stop == 5:
            return
        s5_block(N, nseg, is_prompt)
        if not cast_done[0]:
            cast_done[0] = True
            cast_group(2)
        if 6 <= stop <= 10:
            return
        if dbg == "mix0" or stages < 1:
            return
        memattn(0, N, is_prompt)
        if dbg == "att0" or stages < 2:
            return
        ffn(0, N, nseg, convc[0] if is_prompt else convs[0])
        if dbg == "ffn0" or stages < 3:
            return
        retention(N, is_prompt, blk)
        if dbg == "mix1" or stages < 4:
            return
        memattn(1, N, is_prompt)
        if dbg == "att1" or stages < 5:
            return
        ffn(1, N, nseg, convc[1] if is_prompt else convs[1])

    order = [("p", 0)] + ([("s", 0)] if do_sample else []) + [("p", b) for b in range(1, nblk)]
    if stages >= 6:
        for _ in order:
            ws.plan(sched_block())
    for kind, b in order:
        if stages < 6:
            full = sched_block()
            cut = {"mix0": 22, "att0": 26, "ffn0": 43, "mix1": 63, "att1": 67}.get(dbg, len(full))
            ws.plan(full[:cut])
        if kind == "p":
            run_block(TB, True, b)
            if dbg is not None and dbg_blk == ("p", b):
                dump_dbg(TB)
            if dbg is None:
                store_y(yp, b * TB, TB)
        else:
            run_block(NS * LS, False, 0)
            if dbg is not None and dbg_blk == ("s", 0):
                dump_dbg(NS * LS)
            if dbg is None:
                store_y(ys, 0, NS * LS)
                with kb.scope() as st:
                    ctmp = [kb.sb(f"ctmp{k}", [128, 88], F32, st) for k in range(2)]
                    cout = [kb.sb(f"cout{k}", [88, 128], F32, st) for k in range(2)]
                    for i in range(2):
                        for hh in range(2):
                            k = (2 * i + hh) % 2
                            kb.cp(dve, ctmp[k].t[:].rearrange("p (s r ft) -> p ft s r", s=2, r=2), convs[i].t[:, :, 2 * hh:2 * hh + 2, :], [convs[i]], [ctmp[k]])
                            pb = gps()
                            kb.tr(pb.t[0:88, 0:128], ctmp[k].t[:], identf.t[:], [ctmp[k], identf], [pb])
                            kb.cp(dve, cout[k].t[:], pb.t[0:88, 0:128], [pb], [cout[k]])
                            kb.dma(sp, o_conv_s.t[i][2 * hh:2 * hh + 2].rearrange("s r (ft p) -> (s r ft) p", p=128), cout[k].t[:], o_conv_s, cout[k])

    with kb.scope() as st:
        fo = kb.sb("fo_p", [32, 2, 128], F32, st)
        for pl in range(2):
            pb = gps()
            kb.tr(pb.t[0:32, 0:128], s5c[pl].t[:], identf.t[:], [s5c[pl], identf], [pb])
            kb.cp(dve, fo.t[:, pl, :], pb.t[0:32, 0:128], [pb], [fo])
        kb.dma(sp, o_sre_p.t[:, :], fo.t[:, 0, :], o_sre_p, fo)
        kb.dma(sp, o_sim_p.t[:, :], fo.t[:, 1, :], o_sim_p, fo)
        for h in range(4):
            kb.dma(sp, o_ret_p.t[h].rearrange("(dt p) e -> p dt e", p=128), Sp.t[:, h, :, :], o_ret_p, Sp)
        for i in range(2):
            ctp = kb.sb(f"ctp{i}", [128, 44], F32, st)
            cop = kb.sb(f"cop{i}", [44, 128], F32, st)
            kb.cp(dve, ctp.t[:].rearrange("p (r ft) -> p ft r", r=2), convc[i].t[:, :, 0, :], [convc[i]], [ctp])
            pb = gps()
            kb.tr(pb.t[0:44, 0:128], ctp.t[:], identf.t[:], [ctp, identf], [pb])
            kb.cp(dve, cop.t[:], pb.t[0:44, 0:128], [pb], [cop])
            kb.dma(sp, o_conv_p.t[i].rearrange("r (ft p) -> (r ft) p", p=128), cop.t[:], o_conv_p, cop)
    kb.barrier(final=True)
    return kb


_W_KEYS = ["norm_mix", "norm_mem_q", "norm_mem_kv", "norm_ffn", "norm_final", "mem_w_q", "mem_w_kv", "mem_w_o",
           "ffn_w_up", "ffn_conv_w", "ffn_conv_b", "ffn_w_down"]
_W0_KEYS = ["ssm_log_dt", "ssm_b_re", "ssm_b_im", "ssm_c_re", "ssm_c_im", "ssm_d", "ssm_w_glu", "ssm_b_glu", "ret_w_qkvg", "ret_w_o"]


def make_in_maps(inputs, cores):
    f = lambda a: np.ascontiguousarray(np.asarray(a, dtype=np.float32))
    shared = {k: f(inputs[k]) for k in _W_KEYS}
    for k in _W0_KEYS:
        shared[k] = f(inputs[k][0])
    shared["ssm_a_re"] = f(inputs["ssm_a_re"][0]).reshape(32, 128)
    shared["ssm_a_im"] = f(inputs["ssm_a_im"][0]).reshape(32, 128)
    maps = []
    for c in cores:
        b = c % 4
        s = slice(NS * c, NS * c + NS)
        m = dict(shared)
        m["xp"] = f(inputs["x_prompt"][b])
        m["xs"] = f(inputs["x_sample"][s]).reshape(NS * LS, D)
        m["mem"] = f(inputs["mem_prompt"][b])
        m["sre"] = f(inputs["state_ssm_re"][0, s]).reshape(NS, 32, 128)
        m["sim"] = f(inputs["state_ssm_im"][0, s]).reshape(NS, 32, 128)
        m["sret"] = f(inputs["state_ret"][0, s])
        m["cmk"] = f(inputs["cache_mem_k"][:, s]).reshape(2, NS, 256, D)
        m["cmv"] = f(inputs["cache_mem_v"][:, s]).reshape(2, NS, 256, D)
        m["cconv"] = f(inputs["cache_conv"][:, s])
        maps.append(m)
    return maps


def kernel(**inputs):
    nc = bass.Bass("TRN2", target_bir_lowering=False)
    build(nc)
    cores = list(range(8))
    res = run_bass_kernel_spmd(nc, make_in_maps(inputs, cores), core_ids=cores)
    R = res.results
    y_prompt = np.stack([R[b]["yp"] for b in range(4)])
    y_sample = np.concatenate([R[c]["ys"].reshape(NS, LS, D) for c in cores])
    re_p = np.stack([R[b]["o_sre_p"].reshape(64, 64) for b in range(4)])[None]
    im_p = np.stack([R[b]["o_sim_p"].reshape(64, 64) for b in range(4)])[None]
    re_s = np.concatenate([R[c]["o_sre_s"].reshape(NS, 64, 64) for c in cores])[None]
    im_s = np.concatenate([R[c]["o_sim_s"].reshape(NS, 64, 64) for c in cores])[None]
    ret_p = np.stack([R[b]["o_ret_p"] for b in range(4)])[None]
    ret_s = np.concatenate([R[c]["o_ret_s"] for c in cores])[None]
    mk_p = np.stack([R[b]["o_mk"].reshape(2, 256, 4, 256) for b in range(4)], axis=1)
    mv_p = np.stack([R[b]["o_mv"].reshape(2, 256, 4, 256) for b in range(4)], axis=1)
    conv_p = np.stack([R[b]["o_conv_p"] for b in range(4)], axis=1)
    conv_s = np.concatenate([R[c]["o_conv_s"] for c in cores], axis=1)
    outs = (y_prompt, y_sample, re_p, im_p, re_s, im_s, ret_p, ret_s, mk_p, mv_p, conv_p, conv_s)
    return tuple(np.ascontiguousarray(o, dtype=np.float32) for o in outs)
```

```python
import math
from contextlib import ExitStack, contextmanager

import numpy as np
import concourse.bass as bass
import concourse.mybir as mybir
from concourse.bass_utils import run_bass_kernel_spmd

F32 = mybir.dt.float32
BF16 = mybir.dt.bfloat16
I32 = mybir.dt.int32
AF = mybir.ActivationFunctionType
ALU = mybir.AluOpType
AX = mybir.AxisListType

D = 1024
SEQ = 4096
TB = 512
NBLK = SEQ // TB
NS = 4
LS = 32
PAST = 1024
DFF = 2816
NFT = DFF // 128
EPS = 1e-6
GN_EPS = 1e-5
LG = [math.log(1.0 - 2.0 ** (-5.0 - h)) for h in range(4)]
GELU_C = 1.5957691216057308


class Sem:
    def __init__(self, h, name):
        self.h = h
        self.name = name


class Eng:
    def __init__(self, name, h, sem):
        self.name = name
        self.h = h
        self.sem = sem
        self.cnt = 0
        self.seen = {}


class Buf:
    def __init__(self, name, t, kind="sb"):
        self.name = name
        self.t = t
        self.kind = kind
        self.w = {}
        self.r = {}
        self.dsem = None
        self.dcnt = 0

    def __getitem__(self, k):
        return self.t[k]


class KB:
    def __init__(self, nc):
        self.nc = nc
        self.es = ExitStack()
        self.nsem = 0
        self.pe = Eng("pe", nc.tensor, self.newsem("pe"))
        self.act = Eng("act", nc.scalar, self.newsem("act"))
        self.dve = Eng("dve", nc.vector, self.newsem("dve"))
        self.pool = Eng("pool", nc.gpsimd, self.newsem("pool"))
        self.sp = Eng("sp", nc.sync, self.newsem("sp"))
        self.engs = [self.pe, self.act, self.dve, self.pool, self.sp]
        self.dbufs = []
        self.uid = 0

    def newsem(self, name):
        self.nsem += 1
        return Sem(self.es.enter_context(self.nc.semaphore(f"s{self.nsem}_{name}")), name)

    def sb(self, name, shape, dt, st=None):
        self.uid += 1
        t = (st or self.es).enter_context(self.nc.sbuf_tensor(f"{name}_{self.uid}", list(shape), dt))
        b = Buf(name, t)
        b.scoped = st is not None
        return b

    def psum(self, name, shape, dt):
        t = self.es.enter_context(self.nc.psum_tensor(name, list(shape), dt))
        return Buf(name, t, "ps")

    def dram(self, name, shape, dt, kind):
        h = self.nc.dram_tensor(name, list(shape), dt, kind=kind)
        return Buf(name, h.ap(), "dram")

    def _wait(self, e, sem, v):
        if v > e.seen.get(sem, 0):
            if sem is e.sem:
                assert v <= e.cnt, f"same-engine wait on pending inc {e.name}"
            e.h.wait_ge(sem.h, v)
            e.seen[sem] = v

    def op(self, e, fn, rd=(), wr=(), inc=True):
        if any(b.kind == "ps" for b in rd):
            wr = list(wr) + [b for b in rd if b.kind == "ps"]
            rd = [b for b in rd if b.kind != "ps"]
        need = {}
        for b in rd:
            for sem, v in b.w.items():
                if v > need.get(sem, 0):
                    need[sem] = v
        skip_self = e is self.pe
        for b in wr:
            for sem, v in b.r.items():
                if not (skip_self and sem is e.sem) and v > need.get(sem, 0):
                    need[sem] = v
            for sem, v in b.w.items():
                if not (skip_self and sem is e.sem) and v > need.get(sem, 0):
                    need[sem] = v
        for sem, v in need.items():
            self._wait(e, sem, v)
        ins = fn()
        val = e.cnt + 1
        if inc:
            ins.then_inc(e.sem.h, 1)
            e.cnt = val
        for b in rd:
            if b.r.get(e.sem, 0) < val:
                b.r[e.sem] = val
        for b in wr:
            if b.w.get(e.sem, 0) < val:
                b.w[e.sem] = val
        return ins

    def dma(self, q, out_ap, in_ap, dst, src, **kw):
        need = {}
        for sem, v in src.w.items():
            if v > need.get(sem, 0):
                need[sem] = v
        if dst.kind != "dram":
            for sem, v in dst.r.items():
                if v > need.get(sem, 0):
                    need[sem] = v
            for sem, v in dst.w.items():
                if v > need.get(sem, 0):
                    need[sem] = v
        owner = dst if dst.kind == "sb" else (src if src.kind == "sb" else dst)
        if owner.dsem is None:
            owner.dsem = self.newsem("d_" + owner.name)
            self.dbufs.append(owner)
        if owner.dcnt > 0 and owner.kind != "dram":
            need[owner.dsem] = owner.dcnt
        for sem, v in need.items():
            self._wait(q, sem, v)
        ins = q.h.dma_start(out=out_ap, in_=in_ap, **kw)
        ins.then_inc(owner.dsem.h, 16)
        owner.dcnt += 16
        dst.w[owner.dsem] = owner.dcnt
        src.r[owner.dsem] = owner.dcnt

    def barrier(self, final=False):
        sp = self.sp
        for e in self.engs:
            if e is not sp and (final or e.cnt != getattr(e, "bar_cnt", -1)):
                self._wait(sp, e.sem, e.cnt)
            e.bar_cnt = e.cnt
        for b in self.dbufs:
            if final or (b.kind != "dram" and getattr(b, "scoped", False)):
                self._wait(sp, b.dsem, b.dcnt)
        ins = sp.h.nop()
        ins.then_inc(sp.sem.h, 1)
        sp.cnt += 1
        for e in self.engs:
            if e is not sp:
                self._wait(e, sp.sem, sp.cnt)

    @contextmanager
    def scope(self):
        st = ExitStack()
        try:
            yield st
        finally:
            self.barrier()
            st.close()

    def mm(self, out, lhsT, rhs, rd, wr, start=True, stop=True, inc=True, **kw):
        nc = self.nc
        return self.op(self.pe, lambda: nc.tensor.matmul(out, lhsT, rhs, start=start, stop=stop, **kw), rd, wr, inc)

    def tr(self, out, in_, ident, rd, wr, inc=True):
        nc = self.nc
        return self.op(self.pe, lambda: nc.tensor.transpose(out, in_, ident), rd, wr, inc)

    def actf(self, out, in_, func, rd, wr, scale=None, bias=None, accum_out=None):
        nc = self.nc
        kw = {}
        if scale is not None:
            kw["scale"] = scale
        if bias is not None:
            kw["bias"] = bias
        if accum_out is not None:
            kw["accum_out"] = accum_out
        return self.op(self.act, lambda: nc.scalar.activation(out=out, in_=in_, func=func, **kw), rd, wr)

    def tt(self, e, out, in0, in1, op, rd, wr):
        return self.op(e, lambda: e.h.tensor_tensor(out=out, in0=in0, in1=in1, op=op), rd, wr)

    def ts(self, e, out, in0, s1, op0, rd, wr, s2=None, op1=None):
        if op1 is None:
            return self.op(e, lambda: e.h.tensor_scalar(out=out, in0=in0, scalar1=s1, scalar2=None, op0=op0), rd, wr)
        return self.op(e, lambda: e.h.tensor_scalar(out=out, in0=in0, scalar1=s1, scalar2=s2, op0=op0, op1=op1), rd, wr)

    def stt(self, out, in0, scalar, in1, op0, op1, rd, wr):
        nc = self.nc
        return self.op(self.dve, lambda: nc.vector.scalar_tensor_tensor(out=out, in0=in0, scalar=scalar, in1=in1, op0=op0, op1=op1), rd, wr)

    def cp(self, e, out, in_, rd, wr):
        if e is self.act:
            return self.actf(out, in_, AF.Copy, rd, wr)
        return self.op(e, lambda: e.h.tensor_copy(out=out, in_=in_), rd, wr)

    def memset(self, e, ap, val, wr):
        return self.op(e, lambda: e.h.memset(ap, val), (), wr)

    def recip(self, out, in_, rd, wr):
        nc = self.nc
        return self.op(self.dve, lambda: nc.vector.reciprocal(out=out, in_=in_), rd, wr)


class WStream:
    def __init__(self, kb, nslots, slot_elems):
        self.kb = kb
        self.n = nslots
        self.slots = [kb.sb(f"wslot{i}", [128, slot_elems], BF16) for i in range(nslots)]
        self.sched = []
        self.pos = 0
        self.issued = 0

    def plan(self, lst):
        self.sched.extend(lst)

    def _issue(self, i):
        scr, c, ne = self.sched[i]
        slot = self.slots[i % self.n]
        scr.load(self.kb, slot, c, ne)

    def get(self, scr, c):
        key = self.sched[self.pos]
        assert key[0] is scr and key[1] == c, f"wstream mismatch at {self.pos}: want {scr.name},{c} sched {key[0].name},{key[1]}"
        lim = min(len(self.sched), self.pos + self.n)
        while self.issued < lim:
            self._issue(self.issued)
            self.issued += 1
        slot = self.slots[self.pos % self.n]
        self.pos += 1
        return slot


def build(nc, nblk=NBLK, do_sample=True, dbg=None, dbg_blk=("p", 0), stages=99, stop=99):
    kb = KB(nc)
    pe, act, dve, pool, sp = kb.pe, kb.act, kb.dve, kb.pool, kb.sp

    def din(name, shape):
        return kb.dram(name, shape, F32, "ExternalInput")

    def dout(name, shape):
        return kb.dram(name, shape, F32, "ExternalOutput")

    xp = din("xp", [SEQ, D]); xs = din("xs", [NS * LS, D]); mem = din("mem", [256, D])
    sre = din("sre", [NS, 32, 128]); sim = din("sim", [NS, 32, 128])
    sret = din("sret", [NS, 4, 256, 512])
    cmk = din("cmk", [2, NS, 256, D]); cmv = din("cmv", [2, NS, 256, D])
    cconv = din("cconv", [2, NS, 2, DFF])
    norm_mix = din("norm_mix", [2, D]); norm_mem_q = din("norm_mem_q", [2, D])
    norm_mem_kv = din("norm_mem_kv", [2, D]); norm_ffn = din("norm_ffn", [2, D])
    norm_final = din("norm_final", [D])
    a_re = din("ssm_a_re", [32, 128]); a_im = din("ssm_a_im", [32, 128])
    log_dt = din("ssm_log_dt", [64])
    b_re = din("ssm_b_re", [64, 64, 16]); b_im = din("ssm_b_im", [64, 64, 16])
    c_re = din("ssm_c_re", [64, 16, 64]); c_im = din("ssm_c_im", [64, 16, 64])
    ssm_d = din("ssm_d", [D]); w_glu = din("ssm_w_glu", [D, D]); b_glu = din("ssm_b_glu", [D])
    w_qkvg = din("ret_w_qkvg", [D, 6144]); w_ro = din("ret_w_o", [2048, D])
    w_mq = din("mem_w_q", [2, D, D]); w_mkv = din("mem_w_kv", [2, D, 2048]); w_mo = din("mem_w_o", [2, D, D])
    w_up = din("ffn_w_up", [2, D, 2 * DFF]); conv_w = din("ffn_conv_w", [2, 3, DFF]); conv_b = din("ffn_conv_b", [2, DFF])
    w_down = din("ffn_w_down", [2, DFF, D])

    yp = dout("yp", [SEQ, D]); ys = dout("ys", [NS * LS, D])
    o_sre_p = dout("o_sre_p", [32, 128]); o_sim_p = dout("o_sim_p", [32, 128])
    o_sre_s = dout("o_sre_s", [NS, 32, 128]); o_sim_s = dout("o_sim_s", [NS, 32, 128])
    o_ret_p = dout("o_ret_p", [4, 256, 512]); o_ret_s = dout("o_ret_s", [NS, 4, 256, 512])
    o_mk = dout("o_mk", [2, 256, D]); o_mv = dout("o_mv", [2, 256, D])
    o_conv_p = dout("o_conv_p", [2, 2, DFF]); o_conv_s = dout("o_conv_s", [2, NS, 2, DFF])
    dbg_out = dout("dbg", [128, 8, TB]) if dbg is not None else None

    def wscr(name, nch, kt, oc):
        b = kb.dram("scr_" + name, [nch, 128, kt * oc], BF16, "Internal")
        b.kt = kt; b.oc = oc; b.nch = nch
        b.load = lambda kb_, slot, c, ne, b=b: kb_.dma(kb_.sp, slot.t[:, 0:ne], b.t[c][:, 0:ne], slot, b)
        return b

    def wpm(name, kt, ocols, oc, kind="cols"):
        b = kb.dram("scr_" + name, [128, kt, ocols], BF16, "Internal")
        b.kt = kt; b.oc = oc; b.nch = ocols // oc

        def load(kb_, slot, c, ne, b=b, kind=kind, kt=kt, oc=oc):
            if kind == "cols":
                kb_.dma(kb_.sp, slot.t[:, 0:kt * oc].rearrange("p (k o) -> p k o", o=oc), b.t[:, :, c * oc:(c + 1) * oc], slot, b)
            elif kind == "up":
                sv = slot.t[:, 0:kt * 512].rearrange("p (k o) -> p k o", o=512)
                kb_.dma(kb_.sp, sv[:, :, 0:256], b.t[:, :, c * 256:(c + 1) * 256], slot, b)
                kb_.dma(kb_.sp, sv[:, :, 256:512], b.t[:, :, DFF + c * 256:DFF + (c + 1) * 256], slot, b)
            else:
                nk = ne // 1024
                kb_.dma(kb_.sp, slot.t[:, 0:ne].rearrange("p (k o) -> p k o", o=1024), b.t[:, 4 * c:4 * c + nk, :], slot, b)
        b.load = load
        return b

    W = {}
    W["glu"] = wpm("glu", 8, D, 512)
    for i in range(2):
        W[f"mq{i}"] = wpm(f"mq{i}", 8, D, 512)
        W[f"mo{i}"] = wpm(f"mo{i}", 8, D, 512)
        W[f"mkv{i}"] = wpm(f"mkv{i}", 8, 2048, 512)
        W[f"up{i}"] = wpm(f"up{i}", 8, 2 * DFF, 512, "up")
        W[f"down{i}"] = wpm(f"down{i}", NFT, D, 1024, "rows")
    W["rq"] = wpm("rq", 8, 1024, 256); W["rk"] = wpm("rk", 8, 1024, 256)
    W["rv"] = wpm("rv", 8, 2048, 512); W["rg"] = wpm("rg", 8, 2048, 512)
    W["ro"] = wpm("ro", 16, D, 256)
    W["s5f"] = wscr("s5f", 8, 1, 2048)
    W["s5g"] = wscr("s5g", 8, 1, 2048)
    W["s5k"] = wscr("s5k", 8, 1, 1024)

    wsrc = Buf("wsrc", None, "dram")

    def cast_w(scr, src2d, col0=0):
        ncols = scr.nch * scr.oc
        kb.dma(pool, scr.t[:, :, :], src2d[:, col0:col0 + ncols].rearrange("(kt p) o -> p kt o", p=128), scr, wsrc)

    def cast_up(i):
        cast_w(W[f"up{i}"], w_up.t[i])

    def cast_down(i):
        scr = W[f"down{i}"]
        kb.dma(pool, scr.t[:, :, :], w_down.t[i].rearrange("(kt p) o -> p kt o", p=128), scr, wsrc)

    def cast_group(g):
        if g == 0:
            for i in range(2):
                cast_w(W[f"mkv{i}"], w_mkv.t[i])
        elif g == 1:
            cast_w(W["glu"], w_glu.t)
            cast_w(W["mq0"], w_mq.t[0]); cast_w(W["mo0"], w_mo.t[0])
            cast_up(0)
            cast_down(0)
        else:
            cast_w(W["rq"], w_qkvg.t, 0); cast_w(W["rk"], w_qkvg.t, 1024)
            cast_w(W["rv"], w_qkvg.t, 2048); cast_w(W["rg"], w_qkvg.t, 4096)
            cast_w(W["ro"], w_ro.t)
            cast_w(W["mq1"], w_mq.t[1]); cast_w(W["mo1"], w_mo.t[1])
            cast_up(1)
            cast_down(1)

    if stop == 0:
        kb.barrier(); return kb
    P = [kb.psum(f"pb{i}", [128, 512], F32) for i in range(8)]
    gen_i = [0]

    def gps():
        b = P[gen_i[0] % 4]
        gen_i[0] += 1
        return b

    identf = kb.sb("identf", [128, 128], F32)
    onesb = kb.sb("onesb", [128, 128], BF16)
    kb.memset(pool, identf.t[:], 1.0, [identf])
    kb.op(pool, lambda: nc.gpsimd.affine_select(out=identf.t[:], in_=identf.t[:], pattern=[[-1, 128]],
                                                compare_op=ALU.is_equal, fill=0.0, base=0, channel_multiplier=1),
          [identf], [identf])
    kb.memset(dve, onesb.t[:], 1.0, [onesb])
    onesf = kb.sb("onesf", [128, 128], F32)
    kb.memset(dve, onesf.t[:], 1.0, [onesf])

    def alias(base, name, ap):
        v = Buf(name, ap)
        v.w = base.w; v.r = base.r
        return v

    vec_nat = kb.sb("vec_nat", [88, 128], F32)
    vecT = kb.sb("vecT", [128, 88], F32)
    vsrcs = [norm_mix.t[0], norm_mix.t[1], norm_mem_q.t[0], norm_mem_q.t[1], norm_mem_kv.t[0], norm_mem_kv.t[1],
             norm_ffn.t[0], norm_ffn.t[1], norm_final.t, ssm_d.t, b_glu.t]
    for k, ap in enumerate(vsrcs):
        kb.dma(sp, vec_nat.t[8 * k:8 * k + 8, :], ap.rearrange("(t p) -> t p", p=128), vec_nat, wsrc)
    vviews = [alias(vecT, f"vec{k}", vecT.t[:, 8 * k:8 * k + 8]) for k in range(11)]
    g_mix = vviews[0:2]; g_mq = vviews[2:4]; g_mkv = vviews[4:6]; g_ffn = vviews[6:8]
    g_fin = vviews[8]; v_d = vviews[9]; v_bglu = vviews[10]
    cw = []; cb = []; cnat = []; cT = []
    for i in range(2):
        cn = kb.sb(f"cnat{i}", [88, 128], F32)
        ct_ = kb.sb(f"cT{i}", [128, 88], F32)
        kb.dma(sp, cn.t[0:66, :], conv_w.t[i].rearrange("r (ft p) -> (r ft) p", p=128), cn, wsrc)
        kb.dma(sp, cn.t[66:88, :], conv_b.t[i].rearrange("(ft p) -> ft p", p=128), cn, wsrc)
        cnat.append(cn); cT.append(ct_)
        cw.append([alias(ct_, f"cw{i}_{r}", ct_.t[:, 22 * r:22 * r + 22]) for r in range(3)])
        cb.append(alias(ct_, f"cb{i}", ct_.t[:, 66:88]))

    Apow = {lv: (kb.sb(f"A{lv}re", [128, 32], F32), kb.sb(f"A{lv}im", [128, 32], F32)) for lv in (8, 16, 32, 64)}
    C0 = kb.sb("C0", [128, TB], F32); S0 = kb.sb("S0", [128, TB], F32)
    dC = kb.sb("dC", [128, 16], F32); dS = kb.sb("dS", [128, 16], F32)
    M01 = kb.sb("M01", [128, TB], F32)
    M01s = kb.sb("M01s", [128, 128], F32)
    innerp = kb.sb("innerp", [128, 4, TB], F32)
    inners = kb.sb("inners", [128, 4, 128], F32)
    tailp = kb.sb("tailp", [128, 4, 4], F32)
    tinvp = kb.sb("tinvp", [128, 4, 4], F32)
    tails = kb.sb("tails", [128, 4], F32)
    tinvs = kb.sb("tinvs", [128, 4], F32)

    def tr_f32(dst_ap, dst_buf, src_ap, src_buf, n, e=None):
        pb = gps()
        kb.tr(pb.t[:, 0:n], src_ap, identf.t[0:n, 0:n], [src_buf, identf], [pb])
        kb.cp(e or dve, dst_ap, pb.t[:, 0:n], [pb], [dst_buf])

    tr_f32(vecT.t[:], vecT, vec_nat.t[:], vec_nat, 88)
    for i in range(2):
        tr_f32(cT[i].t[:], cT[i], cnat[i].t[:], cnat[i], 88)
    if stop == 1:
        kb.barrier(); return kb
    with kb.scope() as st:
        ii = kb.sb("ii", [128, TB], I32, st)
        ff = kb.sb("ff", [128, TB], F32, st)

        def iota_f(dst_ap, dst_buf, pattern, base, cm):
            nfree = 1
            for _, n in pattern:
                nfree *= n
            kb.op(pool, lambda: nc.gpsimd.iota(ii.t[:, 0:nfree], pattern=pattern, base=base, channel_multiplier=cm), (), [ii])
            kb.cp(dve, dst_ap, ii.t[:, 0:nfree], [ii], [dst_buf])

        fj = kb.sb("fj", [128, 1], F32, st)
        iota_f(fj.t[:], fj, [[0, 1]], 0, 1)
        kb.actf(fj.t[:], fj.t[:], AF.Exp, [fj], [fj], scale=-math.log(10000.0) / 128.0)
        hpi = kb.sb("hpi", [128, 1], F32, st)
        kb.memset(dve, hpi.t[:], math.pi / 2, [hpi])
        er = kb.sb("er", [128, 16], F32, st); ei = kb.sb("ei", [128, 16], F32, st)
        kb.actf(er.t[:, 0:1], fj.t[:], AF.Sin, [fj, hpi], [er], scale=-1.0, bias=hpi.t[:, 0:1])
        kb.actf(ei.t[:, 0:1], fj.t[:], AF.Sin, [fj], [ei])
        t1 = kb.sb("t1", [128, TB], F32, st); t2 = kb.sb("t2", [128, TB], F32, st)
        for k in range(10):
            kb.tt(dve, t1.t[:, 0:1], er.t[:, k:k + 1], er.t[:, k:k + 1], ALU.mult, [er], [t1])
            kb.tt(dve, t2.t[:, 0:1], ei.t[:, k:k + 1], ei.t[:, k:k + 1], ALU.mult, [ei], [t2])
            kb.tt(dve, er.t[:, k + 1:k + 2], t1.t[:, 0:1], t2.t[:, 0:1], ALU.subtract, [t1, t2], [er])
            kb.tt(dve, t1.t[:, 0:1], er.t[:, k:k + 1], ei.t[:, k:k + 1], ALU.mult, [er, ei], [t1])
            kb.ts(dve, ei.t[:, k + 1:k + 2], t1.t[:, 0:1], 2.0, ALU.mult, [t1], [ei])
        kb.memset(dve, C0.t[:, 0:1], 1.0, [C0]); kb.memset(dve, S0.t[:, 0:1], 0.0, [S0])
        for k in range(9):
            n = 1 << k
            cr = er.t[:, k:k + 1]; ci = ei.t[:, k:k + 1]
            kb.ts(dve, t1.t[:, 0:n], S0.t[:, 0:n], ci, ALU.mult, [S0, ei], [t1])
            kb.stt(C0.t[:, n:2 * n], C0.t[:, 0:n], cr, t1.t[:, 0:n], ALU.mult, ALU.subtract, [C0, er, t1], [C0])
            kb.ts(dve, t2.t[:, 0:n], C0.t[:, 0:n], ci, ALU.mult, [C0, ei], [t2])
            kb.stt(S0.t[:, n:2 * n], S0.t[:, 0:n], cr, t2.t[:, 0:n], ALU.mult, ALU.add, [S0, er, t2], [S0])
        kb.memset(dve, dC.t[:, 0:1], 1.0, [dC]); kb.memset(dve, dS.t[:, 0:1], 0.0, [dS])
        for b in range(1, 8):
            cr = er.t[:, 9:10]; ci = ei.t[:, 9:10]
            kb.ts(dve, t1.t[:, 0:1], dS.t[:, b - 1:b], ci, ALU.mult, [dS, ei], [t1])
            kb.stt(dC.t[:, b:b + 1], dC.t[:, b - 1:b], cr, t1.t[:, 0:1], ALU.mult, ALU.subtract, [dC, er, t1], [dC])
            kb.ts(dve, t2.t[:, 0:1], dC.t[:, b - 1:b], ci, ALU.mult, [dC, ei], [t2])
            kb.stt(dS.t[:, b:b + 1], dS.t[:, b - 1:b], cr, t2.t[:, 0:1], ALU.mult, ALU.add, [dS, er, t2], [dS])
        kb.cp(dve, dC.t[:, 8:9], er.t[:, 10:11], [er], [dC])
        kb.cp(dve, dS.t[:, 8:9], ei.t[:, 10:11], [ei], [dS])

        kb.memset(pool, M01.t[:], 1.0, [M01])
        kb.op(pool, lambda: nc.gpsimd.affine_select(out=M01.t[:], in_=M01.t[:], pattern=[[1, TB]], compare_op=ALU.is_ge,
                                                    fill=0.0, base=0, channel_multiplier=-1), [M01], [M01])
        kb.memset(pool, M01s.t[:], 1.0, [M01s])
        for s_ in range(NS):
            blk = M01s.t[:, 32 * s_:32 * s_ + 32]
            kb.op(pool, lambda blk=blk, s_=s_: nc.gpsimd.affine_select(out=blk, in_=blk, pattern=[[1, 32]], compare_op=ALU.is_ge,
                                                                      fill=0.0, base=32 * s_, channel_multiplier=-1), [M01s], [M01s])
            kb.op(pool, lambda blk=blk, s_=s_: nc.gpsimd.affine_select(out=blk, in_=blk, pattern=[[0, 32]], compare_op=ALU.is_ge,
                                                                      fill=0.0, base=-32 * s_, channel_multiplier=1), [M01s], [M01s])
        rt = kb.sb("rt", [128, 128], F32, st)
        for h in range(4):
            iota_f(ff.t[:, 0:TB], ff, [[1, TB]], 1, 0)
            kb.actf(innerp.t[:, h, :], ff.t[:, 0:TB], AF.Exp, [ff], [innerp], scale=LG[h])
            iota_f(ff.t[:, 0:128], ff, [[0, 4], [1, 32]], 1, 0)
            kb.actf(inners.t[:, h, :], ff.t[:, 0:128], AF.Exp, [ff], [inners], scale=LG[h])
            iota_f(ff.t[:, 0:4], ff, [[-128, 4]], TB - 1, -1)
            kb.actf(tailp.t[:, h, :], ff.t[:, 0:4], AF.Exp, [ff], [tailp], scale=LG[h])
            iota_f(ff.t[:, 0:4], ff, [[128, 4]], 1, 1)
            kb.actf(tinvp.t[:, h, :], ff.t[:, 0:4], AF.Exp, [ff], [tinvp], scale=-LG[h])
            iota_f(ff.t[:, 0:128], ff, [[0, 4], [-1, 32]], 31, 0)
            kb.actf(rt.t[:], ff.t[:, 0:128], AF.Exp, [ff], [rt], scale=LG[h])
            tr_f32(tails.t[:, h:h + 1], tails, rt.t[0:1, :], rt, 1)
            kb.recip(rt.t[:], inners.t[:, h, :], [inners], [rt])
            tr_f32(tinvs.t[:, h:h + 1], tinvs, rt.t[0:1, :], rt, 1)

    cast_group(0)
    cast_group(1)
    if stop == 2:
        kb.barrier(); return kb
    with kb.scope() as st:
        ewi = [0]

        def ew():
            ewi[0] += 1
            return dve if ewi[0] % 2 else pool

        nat = kb.sb("nat", [32, 3, 128], F32, st)
        kb.dma(sp, nat.t[:, 0, :], a_re.t[:, :], nat, wsrc)
        kb.dma(sp, nat.t[:, 1, :], a_im.t[:, :], nat, wsrc)
        ld2 = kb.sb("ld2", [32, 2], F32, st)
        kb.dma(sp, ld2.t[:], log_dt.t.rearrange("(gp gpar) -> gp gpar", gpar=2), ld2, wsrc)
        kb.cp(dve, nat.t[:, 2, :].rearrange("g (a p) -> g a p", p=64), ld2.t[:].unsqueeze(2).to_broadcast([32, 2, 64]), [ld2], [nat])
        q_are = kb.sb("q_are", [128, 32], F32, st); q_aim = kb.sb("q_aim", [128, 32], F32, st); q_dt = kb.sb("q_dt", [128, 32], F32, st)
        tr_f32(q_are.t[:], q_are, nat.t[:, 0, :], nat, 32)
        tr_f32(q_aim.t[:], q_aim, nat.t[:, 1, :], nat, 32)
        tr_f32(q_dt.t[:], q_dt, nat.t[:, 2, :], nat, 32)
        kb.actf(q_dt.t[:], q_dt.t[:], AF.Exp, [q_dt], [q_dt])
        lr = kb.sb("lr", [128, 32], F32, st); li = kb.sb("li", [128, 32], F32, st)
        kb.tt(dve, lr.t[:], q_are.t[:], q_dt.t[:], ALU.mult, [q_are, q_dt], [lr])
        kb.tt(dve, li.t[:], q_aim.t[:], q_dt.t[:], ALU.mult, [q_aim, q_dt], [li])
        hpi = kb.sb("hpi2", [128, 1], F32, st)
        kb.memset(dve, hpi.t[:], math.pi / 2, [hpi])
        e8 = kb.sb("e8", [128, 32], F32, st); ar = kb.sb("ar", [128, 32], F32, st); ai = kb.sb("ai", [128, 32], F32, st)
        u1 = kb.sb("u1", [128, 32], F32, st); u2 = kb.sb("u2", [128, 32], F32, st)
        kb.actf(e8.t[:], lr.t[:], AF.Exp, [lr], [e8], scale=0.125)
        kb.actf(u1.t[:], li.t[:], AF.Sin, [li, hpi], [u1], scale=-0.125, bias=hpi.t[:, 0:1])
        kb.actf(u2.t[:], li.t[:], AF.Sin, [li], [u2], scale=0.125)
        kb.tt(dve, ar.t[:], e8.t[:], u1.t[:], ALU.mult, [e8, u1], [ar])
        kb.tt(dve, ai.t[:], e8.t[:], u2.t[:], ALU.mult, [e8, u2], [ai])

        def csq(r, i):
            kb.tt(dve, u1.t[:], r.t[:], r.t[:], ALU.mult, [r], [u1])
            kb.tt(dve, u2.t[:], i.t[:], i.t[:], ALU.mult, [i], [u2])
            kb.tt(dve, u2.t[:], u1.t[:], u2.t[:], ALU.subtract, [u1, u2], [u2])
            kb.tt(dve, u1.t[:], r.t[:], i.t[:], ALU.mult, [r, i], [u1])
            kb.ts(dve, i.t[:], u1.t[:], 2.0, ALU.mult, [u1], [i])
            kb.cp(dve, r.t[:], u2.t[:], [u2], [r])

        for _ in range(3):
            csq(ar, ai)
        Pre = kb.sb("Pre", [128, 9, 32], F32, st); Pim = kb.sb("Pim", [128, 9, 32], F32, st)
        kb.memset(dve, Pre.t[:, 0, :], 1.0, [Pre]); kb.memset(dve, Pim.t[:, 0, :], 0.0, [Pim])
        for m in range(8):
            kb.tt(dve, u1.t[:], Pre.t[:, m, :], ar.t[:], ALU.mult, [Pre, ar], [u1])
            kb.tt(dve, u2.t[:], Pim.t[:, m, :], ai.t[:], ALU.mult, [Pim, ai], [u2])
            kb.tt(dve, Pre.t[:, m + 1, :], u1.t[:], u2.t[:], ALU.subtract, [u1, u2], [Pre])
            kb.tt(dve, u1.t[:], Pre.t[:, m, :], ai.t[:], ALU.mult, [Pre, ai], [u1])
            kb.tt(dve, u2.t[:], Pim.t[:, m, :], ar.t[:], ALU.mult, [Pim, ar], [u2])
            kb.tt(dve, Pim.t[:, m + 1, :], u1.t[:], u2.t[:], ALU.add, [u1, u2], [Pim])
        kb.cp(dve, Apow[8][0].t[:], Pre.t[:, 8, :], [Pre], [Apow[8][0]])
        kb.cp(dve, Apow[8][1].t[:], Pim.t[:, 8, :], [Pim], [Apow[8][1]])
        for lv in (16, 32, 64):
            kb.cp(dve, Apow[lv][0].t[:], Apow[lv // 2][0].t[:], [Apow[lv // 2][0]], [Apow[lv][0]])
            kb.cp(dve, Apow[lv][1].t[:], Apow[lv // 2][1].t[:], [Apow[lv // 2][1]], [Apow[lv][1]])
            csq(Apow[lv][0], Apow[lv][1])
        cre = kb.sb("cre", [128, 32], F32, st); cim = kb.sb("cim", [128, 32], F32, st)
        xm1 = kb.sb("xm1", [128, 32], F32, st); rden = kb.sb("rden", [128, 32], F32, st)
        kb.ts(dve, xm1.t[:], ar.t[:], -1.0, ALU.add, [ar], [xm1])
        kb.tt(dve, u1.t[:], q_are.t[:], q_are.t[:], ALU.mult, [q_are], [u1])
        kb.tt(dve, u2.t[:], q_aim.t[:], q_aim.t[:], ALU.mult, [q_aim], [u2])
        kb.tt(dve, u1.t[:], u1.t[:], u2.t[:], ALU.add, [u1, u2], [u1])
        kb.recip(rden.t[:], u1.t[:], [u1], [rden])
        kb.tt(dve, u1.t[:], xm1.t[:], q_are.t[:], ALU.mult, [xm1, q_are], [u1])
        kb.tt(dve, u2.t[:], ai.t[:], q_aim.t[:], ALU.mult, [ai, q_aim], [u2])
        kb.tt(dve, u1.t[:], u1.t[:], u2.t[:], ALU.add, [u1, u2], [u1])
        kb.tt(dve, cre.t[:], u1.t[:], rden.t[:], ALU.mult, [u1, rden], [cre])
        kb.tt(dve, u1.t[:], ai.t[:], q_are.t[:], ALU.mult, [ai, q_are], [u1])
        kb.tt(dve, u2.t[:], xm1.t[:], q_aim.t[:], ALU.mult, [xm1, q_aim], [u2])
        kb.tt(dve, u1.t[:], u1.t[:], u2.t[:], ALU.subtract, [u1, u2], [u1])
        kb.tt(dve, cim.t[:], u1.t[:], rden.t[:], ALU.mult, [u1, rden], [cim])

        Bre = kb.sb("Bre", [128, 32, 16], F32, st); Bim = kb.sb("Bim", [128, 32, 16], F32, st)
        for gpar in range(2):
            kb.dma(sp, Bre.t[64 * gpar:64 * gpar + 64, :, :],
                   b_re.t.rearrange("(gp gpar) p c -> gpar p gp c", gpar=2)[gpar], Bre, wsrc)
            kb.dma(sp, Bim.t[64 * gpar:64 * gpar + 64, :, :],
                   b_im.t.rearrange("(gp gpar) p c -> gpar p gp c", gpar=2)[gpar], Bim, wsrc)
        Bbre = kb.sb("Bbre", [128, 32, 16], F32, st); Bbim = kb.sb("Bbim", [128, 32, 16], F32, st)
        w1 = kb.sb("w1", [128, 32, 32], F32, st); w2 = kb.sb("w2", [128, 32, 32], F32, st)

        def bc(buf_ap, n):
            return buf_ap.unsqueeze(2).to_broadcast([128, 32, n])

        kb.tt(dve, w1.t[:, :, 0:16], Bre.t[:], bc(cre.t[:], 16), ALU.mult, [Bre, cre], [w1])
        kb.tt(dve, w2.t[:, :, 0:16], Bim.t[:], bc(cim.t[:], 16), ALU.mult, [Bim, cim], [w2])
        kb.tt(dve, Bbre.t[:], w1.t[:, :, 0:16], w2.t[:, :, 0:16], ALU.subtract, [w1, w2], [Bbre])
        kb.tt(dve, w1.t[:, :, 0:16], Bim.t[:], bc(cre.t[:], 16), ALU.mult, [Bim, cre], [w1])
        kb.tt(dve, w2.t[:, :, 0:16], Bre.t[:], bc(cim.t[:], 16), ALU.mult, [Bre, cim], [w2])
        kb.tt(dve, Bbim.t[:], w1.t[:, :, 0:16], w2.t[:, :, 0:16], ALU.add, [w1, w2], [Bbim])

        Qbd = kb.sb("Qbd", [128, 8, 8, 2, 128], F32, st)
        kb.memset(dve, Qbd.t[:], 0.0, [Qbd])
        for m in range(8):
            for plane in range(2):
                pa, pb_ = (Pre, Pim) if plane == 0 else (Pim, Pre)
                kb.tt(dve, w1.t[:, :, 0:16], Bbre.t[:], bc(pa.t[:, m, :], 16), ALU.mult, [Bbre, pa], [w1])
                kb.tt(dve, w2.t[:, :, 0:16], Bbim.t[:], bc(pb_.t[:, m, :], 16), ALU.mult, [Bbim, pb_], [w2])
                for gpar in range(2):
                    ps_ = slice(64 * gpar, 64 * gpar + 64)
                    dst = Qbd.t[ps_, :, m, plane, :].rearrange("p ct (j c) -> p ct j c", c=32)[:, :, :, 16 * gpar:16 * gpar + 16]
                    i0 = w1.t[ps_, :, 0:16].rearrange("p (ct j) c -> p ct j c", j=4)
                    i1 = w2.t[ps_, :, 0:16].rearrange("p (ct j) c -> p ct j c", j=4)
                    kb.tt(dve, dst, i0, i1, ALU.subtract if plane == 0 else ALU.add, [w1, w2], [Qbd])

        Cn = [kb.sb("Cnre", [128, 8, 64], F32, st), kb.sb("Cnim", [128, 8, 64], F32, st)]
        kb.dma(sp, Cn[0].t[:], c_re.t.rearrange("(ct gl) co p -> (gl co) ct p", gl=8), Cn[0], wsrc)
        kb.dma(sp, Cn[1].t[:], c_im.t.rearrange("(ct gl) co p -> (gl co) ct p", gl=8), Cn[1], wsrc)
        mk = kb.sb("mk", [128, 2], F32, st)
        idv = identf.t[:].rearrange("p (a b c) -> p a b c", b=2, c=16)
        for par in range(2):
            kb.op(dve, lambda par=par: nc.vector.tensor_reduce(out=mk.t[:, par:par + 1], in_=idv[:, :, par, :], axis=AX.XY, op=ALU.add),
                  [identf], [mk])
        Cpad = kb.sb("Cpad", [128, 8, 128], F32, st)
        CT = [kb.sb("CTre", [128, 32, 32], F32, st), kb.sb("CTimN", [128, 32, 32], F32, st)]
        for pl in range(2):
            for par in range(2):
                kb.ts(dve, Cpad.t[:, :, 64 * par:64 * par + 64], Cn[pl].t[:], mk.t[:, par:par + 1], ALU.mult, [Cn[pl], mk], [Cpad])
            for ct in range(8):
                pb = gps()
                kb.tr(pb.t[:, 0:128], Cpad.t[:, ct, :], identf.t[:], [Cpad, identf], [pb])
                kb.cp(act, CT[pl].t[:, 4 * ct:4 * ct + 4, :].rearrange("p j c -> p (j c)"), pb.t[:, 0:128], [pb], [CT[pl]])
        Gbd = kb.sb("Gbd", [128, 32, 8, 2, 32], BF16, st)
        for m in range(1, 9):
            kb.tt(dve, w1.t[:], CT[0].t[:], bc(Pre.t[:, m, :], 32), ALU.mult, [CT[0], Pre], [w1])
            kb.tt(dve, w2.t[:], CT[1].t[:], bc(Pim.t[:, m, :], 32), ALU.mult, [CT[1], Pim], [w2])
            kb.tt(dve, Gbd.t[:, :, m - 1, 0, :], w1.t[:], w2.t[:], ALU.subtract, [w1, w2], [Gbd])
            kb.tt(dve, w1.t[:], CT[1].t[:], bc(Pre.t[:, m, :], 32), ALU.mult, [CT[1], Pre], [w1])
            kb.tt(dve, w2.t[:], CT[0].t[:], bc(Pim.t[:, m, :], 32), ALU.mult, [CT[0], Pim], [w2])
            kb.tt(dve, w1.t[:], w1.t[:], w2.t[:], ALU.add, [w1, w2], [w1])
            kb.ts(dve, Gbd.t[:, :, m - 1, 1, :], w1.t[:], -1.0, ALU.mult, [w1], [Gbd])
        kb.ts(dve, CT[1].t[:], CT[1].t[:], -1.0, ALU.mult, [CT[1]], [CT[1]])

        Qb = kb.sb("Qb", [128, 8, 2, 128], BF16, st)
        CTb = [kb.sb("CTbre", [128, 32, 32], BF16, st), kb.sb("CTbim", [128, 32, 32], BF16, st)]
        for pl in range(2):
            kb.cp(dve, CTb[pl].t[:], CT[pl].t[:], [CT[pl]], [CTb[pl]])
        Fst = [kb.sb(f"Fst{i}", [128, 2048], BF16, st) for i in range(2)]
        Kst = [kb.sb(f"Kst{i}", [128, 8, 128], BF16, st) for i in range(2)]
        for ct in range(8):
            fs = Fst[ct % 2]; ks = Kst[ct % 2]
            for half in range(4):
                pb = gps()
                for k4 in range(4):
                    idx = half * 4 + k4
                    tau, plane = idx // 2, idx % 2
                    kb.tr(pb.t[:, k4 * 128:(k4 + 1) * 128], Qbd.t[:, ct, 7 - tau, plane, :], identf.t[:], [Qbd, identf], [pb], inc=(k4 == 3))
                kb.cp(act if half % 2 else dve, fs.t[:, half * 512:(half + 1) * 512], pb.t[:, :], [pb], [fs])
            kb.dma(sp, W["s5f"].t[ct], fs.t[:], W["s5f"], fs)
            kb.cp(dve, Qb.t[:], Qbd.t[:, ct, :, :, :], [Qbd], [Qb])
            pb = gps()
            pv = pb.t[:, 0:256].rearrange("p (d c) -> p d c", c=32)
            for j in range(4):
                for dl in range(8):
                    for pl in range(2):
                        kb.mm(pv[32 * j:32 * j + 32, dl, :], Qb.t[:, dl, pl, 32 * j:32 * j + 32], CTb[pl].t[:, 4 * ct + j, :],
                              [Qb, CTb[pl]], [pb], start=(pl == 0), stop=(pl == 1), inc=(j == 3 and dl == 7 and pl == 1),
                              tile_position=(0, 32 * j))
            kb.memset(dve, ks.t[:], 0.0, [ks])
            for j in range(4):
                kb.cp(dve, ks.t[32 * j:32 * j + 32, :, 32 * j:32 * j + 32], pv[32 * j:32 * j + 32, :, :], [pb], [ks])
            kb.dma(sp, W["s5g"].t[ct].rearrange("p (j r) -> p j r", j=4),
                   Gbd.t[:, 4 * ct:4 * ct + 4, :, :, :].rearrange("p j m a c -> p j (m a c)"), W["s5g"], Gbd)
            kb.dma(sp, W["s5k"].t[ct], ks.t[:].rearrange("p d c -> p (d c)"), W["s5k"], ks)

    if stop == 3:
        kb.barrier(); return kb
    xT = kb.sb("xT", [128, 8, TB], F32)
    hT = kb.sb("hT", [128, 8, TB], BF16)
    hTs = [Buf(f"hT{c_}", hT.t) for c_ in range(8)]
    Sp = kb.sb("Sp", [128, 4, 2, 512], F32)
    s5c = [kb.sb("s5c_re", [128, 32], F32), kb.sb("s5c_im", [128, 32], F32)]
    convc = [kb.sb(f"convc{i}", [128, NFT, 1, 2], F32) for i in range(2)]
    convs = [kb.sb(f"convs{i}", [128, NFT, NS, 2], F32) for i in range(2)]
    kTm = [kb.sb(f"kTm{i}", [128, 8, 256], BF16) for i in range(2)]
    vm = [kb.sb(f"vm{i}", [128, 2, D], BF16) for i in range(2)]
    kb.memset(dve, Sp.t[:], 0.0, [Sp])
    for b_ in s5c:
        kb.memset(dve, b_.t[:], 0.0, [b_])
    for i in range(2):
        kb.memset(dve, convc[i].t[:], 0.0, [convc[i]])

    epsb = kb.sb("epsb", [128, 2], F32)
    kb.memset(dve, epsb.t[:, 0:1], EPS, [epsb]); kb.memset(dve, epsb.t[:, 1:2], GN_EPS, [epsb])
    ws = WStream(kb, 4, 4096)

    def sched_block():
        l = [(W["s5f"], ct, 2048) for ct in range(8)]
        l += [(W["s5k"], ct, 1024) for ct in range(8)] + [(W["s5g"], ct, 2048) for ct in range(8)]
        l += [(W["glu"], c, 4096) for c in range(2)]
        l += [(W["mq0"], c, 4096) for c in range(2)] + [(W["mo0"], c, 4096) for c in range(2)]
        l += [(W["up0"], c, 4096) for c in range(11)] + [(W["down0"], c, 4096 if c < 5 else 2048) for c in range(6)]
        for h in range(4):
            l += [(W["rq"], h, 2048), (W["rk"], h, 2048), (W["rv"], h, 4096), (W["rg"], h, 4096)]
        l += [(W["ro"], c, 4096) for c in range(4)]
        l += [(W["mq1"], c, 4096) for c in range(2)] + [(W["mo1"], c, 4096) for c in range(2)]
        l += [(W["up1"], c, 4096) for c in range(11)] + [(W["down1"], c, 4096 if c < 5 else 2048) for c in range(6)]
        return l

    rn_sq = kb.sb("rn_sq", [128, 4, TB], BF16)
    rn_sq2 = kb.sb("rn_sq2", [128, 4, TB], BF16)
    rn_rs = kb.sb("rn_rs", [128, TB], F32)

    def sq_tile(ot, N):
        buf = rn_sq if ot < 4 else rn_sq2
        kb.actf(buf.t[:, ot % 4, 0:N], xT.t[:, ot, 0:N], AF.Square, [xT], [buf])

    def rmsnorm(N, gain, out_buf, presq=False):
        sq = rn_sq; rs = rn_rs
        if not presq:
            kb.tt(pool, rn_sq2.t[:, 2:4, 0:N], xT.t[:, 6:8, 0:N], xT.t[:, 6:8, 0:N], ALU.mult, [xT], [rn_sq2])
            kb.actf(sq.t[:, 0:4, 0:N], xT.t[:, 0:4, 0:N], AF.Square, [xT], [sq])
            kb.actf(rn_sq2.t[:, 0:2, 0:N], xT.t[:, 4:6, 0:N], AF.Square, [xT], [rn_sq2])
        pb = gps()
        for ct in range(8):
            sqb = sq if ct < 4 else rn_sq2
            kb.mm(pb.t[:, 0:N], onesb.t[:], sqb.t[:, ct % 4, 0:N], [onesb, sqb], [pb], start=(ct == 0), stop=(ct == 7), inc=(ct == 7))
        kb.actf(rs.t[:, 0:N], pb.t[:, 0:N], AF.Sqrt, [pb, epsb], [rs], scale=1.0 / D, bias=epsb.t[:, 0:1])
        kb.recip(rs.t[:, 0:N], rs.t[:, 0:N], [rs], [rs])
        for ct in range(8):
            ob = out_buf[ct] if isinstance(out_buf, list) else out_buf
            kb.stt(ob.t[:, ct, 0:N], xT.t[:, ct, 0:N], gain.t[:, ct:ct + 1], rs.t[:, 0:N], ALU.mult, ALU.mult, [xT, gain, rs], [ob])

    def gelu(out_ap, out_buf, x_ap, x_buf, t_ap, t_buf):
        kb.actf(t_ap, x_ap, AF.Square, [x_buf], [t_buf], scale=math.sqrt(0.044715))
        kb.stt(t_ap, t_ap, 1.0, x_ap, ALU.add, ALU.mult, [t_buf, x_buf], [t_buf])
        kb.actf(t_ap, t_ap, AF.Sigmoid, [t_buf], [t_buf], scale=GELU_C)
        kb.tt(pool, out_ap, t_ap, x_ap, ALU.mult, [t_buf, x_buf], [out_buf])

    def linear_fm(wname, nch, N, rhs_fn, rhs_bufs, consume):
        scr = W[wname]
        kt_n, oc = scr.kt, scr.oc
        for c in range(nch):
            slot = ws.get(scr, c)
            wv = slot.t[:, 0:kt_n * oc].rearrange("p (k o) -> p k o", o=oc)
            for o_ in range(oc // 128):
                pb = gps()
                for kt in range(kt_n):
                    rb = rhs_bufs(kt) if callable(rhs_bufs) else rhs_bufs
                    kb.mm(pb.t[:, 0:N], wv[:, kt, o_ * 128:(o_ + 1) * 128], rhs_fn(kt), [slot] + rb, [pb],
                          start=(kt == 0), stop=(kt == kt_n - 1), inc=(kt == kt_n - 1))
                consume(c * (oc // 128) + o_, pb)

    def add_to_x(N, presq=False):
        def f(ot, pb):
            kb.tt(dve, xT.t[:, ot, 0:N], xT.t[:, ot, 0:N], pb.t[:, 0:N], ALU.add, [xT, pb], [xT])
            if presq:
                sq_tile(ot, N)
        return f

    def load_x(src, row0, N):
        with kb.scope() as st:
            stg = [kb.sb(f"xstg{i}", [128, D], F32, st) for i in range(2)]
            for tt_ in range(N // 128):
                sg = stg[tt_ % 2]
                kb.dma(sp, sg.t[:], src.t[row0 + tt_ * 128:row0 + (tt_ + 1) * 128, :], sg, wsrc)
                for half in range(2):
                    pb = gps()
                    for k4 in range(4):
                        ct = half * 4 + k4
                        kb.tr(pb.t[:, k4 * 128:(k4 + 1) * 128], sg.t[:, ct * 128:(ct + 1) * 128], identf.t[:], [sg, identf], [pb], inc=(k4 == 3))
                    kb.cp(act if half else dve, xT.t[:, half * 4:half * 4 + 4, tt_ * 128:(tt_ + 1) * 128],
                          pb.t[:, :].rearrange("p (a b) -> p a b", b=128), [pb], [xT])
                    sqb = rn_sq2 if half else rn_sq
                    kb.actf(sqb.t[:, :, tt_ * 128:(tt_ + 1) * 128], xT.t[:, half * 4:half * 4 + 4, tt_ * 128:(tt_ + 1) * 128], AF.Square, [xT], [sqb])

    def store_y(dst, row0, N):
        with kb.scope() as st:
            yF = kb.sb("yF", [128, 8, N], F32, st)
            rmsnorm(N, g_fin, yF)
            stg = [kb.sb(f"ystg{i}", [128, D], F32, st) for i in range(2)]
            for tt_ in range(N // 128):
                sg = stg[tt_ % 2]
                for half in range(2):
                    pb = gps()
                    for k4 in range(4):
                        ct = half * 4 + k4
                        kb.tr(pb.t[:, k4 * 128:(k4 + 1) * 128], yF.t[:, ct, tt_ * 128:(tt_ + 1) * 128], identf.t[:], [yF, identf], [pb], inc=(k4 == 3))
                    kb.cp(act if half else dve, sg.t[:, half * 512:(half + 1) * 512], pb.t[:, :], [pb], [sg])
                kb.dma(sp, dst.t[row0 + tt_ * 128:row0 + (tt_ + 1) * 128, :], sg.t[:], dst, sg)

    def dump_dbg(N):
        kb.dma(sp, dbg_out.t[:, :, 0:N], xT.t[:, :, 0:N], dbg_out, xT)

    def s5_block(N, nseg, is_prompt):
        NK = N // 8
        Lc = NK // nseg
        Ltop = min(8, Lc)
        nl = int(math.log2(Ltop))
        rmsnorm(N, g_mix[0], hTs, presq=True)
        with kb.scope() as st:
            V = [[kb.sb(f"V{l}_{pl}", [128, 32, NK >> l], F32, st) for pl in range(2)] for l in range(nl + 1)]
            S = [kb.sb(f"S_{pl}", [128, 32, nseg, Lc + 1], F32, st) for pl in range(2)]
            Sb = [kb.sb(f"Sb_{pl}", [128, 32, NK], BF16, st) for pl in range(2)]
            T1 = kb.sb("T1", [128, 32, max(NK // 2, 4)], F32, st); T2 = kb.sb("T2", [128, 32, max(NK // 2, 4)], F32, st)
            gT = kb.sb("gT", [128, 8, N], BF16, st)
            g32 = gT.t[:].rearrange("p a n -> p (a n)").bitcast(F32)
            tw = max(NK // 2, 4)
            T3v = g32[:, 0:32 * tw].rearrange("p (g k) -> p g k", k=tw)
            T4v = g32[:, 32 * tw:64 * tw].rearrange("p (g k) -> p g k", k=tw)
            T3 = gT; T4 = gT
            yy = [kb.sb(f"yy{k}", [128, N], F32, st) for k in range(2)]
            tq = [kb.sb(f"tq{k}", [128, N], F32, st) for k in range(2)]
            if is_prompt:
                for pl in range(2):
                    kb.cp(pool, S[pl].t[:, :, 0, 0], s5c[pl].t[:], [s5c[pl]], [S[pl]])
            else:
                natS = kb.sb("natS", [32, 2, NS, 128], F32, st)
                kb.dma(sp, natS.t[:, 0, :, :], sre.t.rearrange("s g q -> g s q"), natS, wsrc)
                kb.dma(sp, natS.t[:, 1, :, :], sim.t.rearrange("s g q -> g s q"), natS, wsrc)
                for pl in range(2):
                    for s_ in range(NS):
                        tr_f32(S[pl].t[:, :, s_, 0], S[pl], natS.t[:, pl, s_, :], natS, 32)
            hv_all = hT.t[:, :, 0:N].rearrange("p c (k t) -> p c k t", t=8)
            if stop == 6:
                return
            import os as _os
            _nct = int(_os.environ.get("P1CT", "99")); _noev = _os.environ.get("P1NOEV") == "1"
            for half in range(2):
                for c4 in range(4):
                    ct = half * 4 + c4
                    if ct >= _nct:
                        continue
                    slot = ws.get(W["s5f"], ct)
                    Fv = slot.t[:, 0:2048].rearrange("p (t a q) -> p t a q", a=2, q=128)
                    for j in range(4):
                        r = slice(32 * j, 32 * j + 32)
                        vb = P[4 + j]
                        vbv = vb.t[:, 0:8 * NK].rearrange("p (c a k) -> p c a k", a=2, k=NK)
                        for pl in range(2):
                            for tau in range(8):
                                kb.mm(vbv[:, c4, pl, :], Fv[r, tau, pl, :], hv_all[r, ct, :, tau], [slot, hTs[ct]], [vb],
                                      start=(tau == 0), stop=(tau == 7), inc=(tau == 7 and j == 3 and pl == 1), tile_position=(32 * j, 0))
                for j in range(4):
                    if _noev or _nct < 99:
                        continue
                    vb = P[4 + j]
                    vbv = vb.t[:, 0:8 * NK].rearrange("p (c a k) -> p c a k", a=2, k=NK)
                    g0 = 16 * half + j
                    kb.cp(act, V[0][0].t[:, g0:g0 + 13:4, :], vbv[:, :, 0, :], [vb], [V[0][0]])
                    kb.cp(dve, V[0][1].t[:, g0:g0 + 13:4, :], vbv[:, :, 1, :], [vb], [V[0][1]])
            if stop == 7:
                return

            def cmadd(dre, dim_, dbufs, lv, sre_, sim_, sbufs, vre, vim, vbufs, shape):
                Are, Aim = Apow[lv]
                nfree = 1
                for x in shape[2:]:
                    nfree *= x
                if len(shape) == 3:
                    t1 = T1.t[:, :, 0:nfree]; t2 = T2.t[:, :, 0:nfree]; t3 = T3v[:, :, 0:nfree]; t4 = T4v[:, :, 0:nfree]
                    bre = Are.t[:].unsqueeze(2).to_broadcast(shape); bim = Aim.t[:].unsqueeze(2).to_broadcast(shape)
                else:
                    t1 = T1.t[:, :, 0:nfree].rearrange("p g (s k) -> p g s k", s=shape[2])
                    t2 = T2.t[:, :, 0:nfree].rearrange("p g (s k) -> p g s k", s=shape[2])
                    t3 = T3v[:, :, 0:nfree].rearrange("p g (s k) -> p g s k", s=shape[2])
                    t4 = T4v[:, :, 0:nfree].rearrange("p g (s k) -> p g s k", s=shape[2])
                    bre = Are.t[:].unsqueeze(2).unsqueeze(3).to_broadcast(shape); bim = Aim.t[:].unsqueeze(2).unsqueeze(3).to_broadcast(shape)
                kb.tt(pool, t1, sre_, bre, ALU.mult, sbufs + [Are], [T1])
                kb.tt(pool, t2, sim_, bim, ALU.mult, sbufs + [Aim], [T2])
                kb.tt(pool, t1, t1, t2, ALU.subtract, [T1, T2], [T1])
                kb.tt(dve, t3, sim_, bre, ALU.mult, sbufs + [Are], [T3])
                kb.tt(dve, t4, sre_, bim, ALU.mult, sbufs + [Aim], [T4])
                kb.tt(dve, t3, t3, t4, ALU.add, [T3, T4], [T3])
                kb.tt(pool, dre, t1, vre, ALU.add, [T1] + vbufs, [dbufs[0]])
                kb.tt(dve, dim_, t3, vim, ALU.add, [T3] + vbufs, [dbufs[1]])

            ybanks = [P[6], P[7], P[0], P[1], P[2], P[3], P[4], P[5]]

            def toep(ct, yb):
                slot = ws.get(W["s5k"], ct)
                Kv = slot.t[:, 0:1024].rearrange("p (d c) -> p d c", c=128)
                yv = yb.t[:, 0:N].rearrange("p (k t) -> p k t", t=8)
                kb.mm(yb.t[:, 0:N], Kv[:, 0, :], hT.t[:, ct, 0:N], [slot, hTs[ct]], [yb], start=True, stop=False, inc=False)
                for dl in range(1, 8):
                    for tau in range(dl, 8):
                        kb.mm(yv[:, :, tau], Kv[:, dl, :], hv_all[:, ct, :, tau - dl], [slot, hTs[ct]], [yb], start=False, stop=False,
                              inc=(dl == 7))

            def gpart(ct, yb):
                slot = ws.get(W["s5g"], ct)
                Gv = slot.t[:, 0:2048].rearrange("p (j t a c) -> p j t a c", j=4, t=8, a=2)
                yv = yb.t[:, 0:N].rearrange("p (k t) -> p k t", t=8)
                for j in range(4):
                    for pl in range(2):
                        for tau in range(8):
                            last = (j == 3 and pl == 1 and tau == 7)
                            kb.mm(yv[32 * j:32 * j + 32, :, tau], Gv[:, j, tau, pl, :], Sb[pl].t[:, 4 * ct + j, :], [slot, Sb[pl]], [yb],
                                  start=False, stop=(pl == 1 and tau == 7), inc=last, tile_position=(0, 32 * j))
                y_ = yy[ct % 2]; t_ = tq[ct % 2]
                kb.stt(y_.t[:], hT.t[:, ct, 0:N], v_d.t[:, ct:ct + 1], yb.t[:, 0:N], ALU.mult, ALU.add, [hTs[ct], v_d, yb], [y_])
                gelu(gT.t[:, ct, :], gT, y_.t[:], y_, t_.t[:], t_)

            for ct in range(8):
                toep(ct, ybanks[ct])

            for l in range(nl):
                n2 = NK >> (l + 1)
                src = [V[l][pl].t[:].rearrange("p g (k two) -> p g k two", two=2) for pl in range(2)]
                cmadd(V[l + 1][0].t[:], V[l + 1][1].t[:], V[l + 1], 8 << l, src[0][:, :, :, 0], src[1][:, :, :, 0], V[l],
                      src[0][:, :, :, 1], src[1][:, :, :, 1], V[l], [128, 32, n2])
            nt = Lc // Ltop
            vtop = [V[nl][pl].t[:].rearrange("p g (s u) -> p g s u", s=nseg) for pl in range(2)]
            for u in range(nt):
                cmadd(S[0].t[:, :, :, (u + 1) * Ltop], S[1].t[:, :, :, (u + 1) * Ltop], S, 8 << nl,
                      S[0].t[:, :, :, u * Ltop], S[1].t[:, :, :, u * Ltop], S, vtop[0][:, :, :, u], vtop[1][:, :, :, u], V[nl], [128, 32, nseg])
            for l in range(nl - 1, -1, -1):
                stp = 1 << l
                cnt = Lc // (2 * stp)
                vv = [V[l][pl].t[:].rearrange("p g (s k two) -> p g s k two", s=nseg, two=2) for pl in range(2)]
                dsl = slice(stp, stp + 2 * stp * (cnt - 1) + 1, 2 * stp)
                ssl = slice(0, 2 * stp * (cnt - 1) + 1, 2 * stp)
                cmadd(S[0].t[:, :, :, dsl], S[1].t[:, :, :, dsl], S, 8 << l, S[0].t[:, :, :, ssl], S[1].t[:, :, :, ssl], S,
                      vv[0][:, :, :, :, 0], vv[1][:, :, :, :, 0], V[l], [128, 32, nseg, cnt])
            if stop == 8:
                return
            for pl in range(2):
                kb.cp(act if pl else dve, Sb[pl].t[:].rearrange("p g (s k) -> p g s k", s=nseg), S[pl].t[:, :, :, 0:Lc], [S[pl]], [Sb[pl]])
            if stop == 9:
                return
            for ct in range(8):
                gpart(ct, ybanks[ct])
            if is_prompt:
                for pl in range(2):
                    kb.cp(pool, s5c[pl].t[:], S[pl].t[:, :, 0, Lc], [S[pl]], [s5c[pl]])
            else:
                fin = kb.sb("fin", [128, 2, NS, 32], F32, st)
                fo = kb.sb("fo", [32, 2, NS, 128], F32, st)
                for pl in range(2):
                    kb.cp(pool, fin.t[:, pl, :, :], S[pl].t[:, :, :, Lc].rearrange("p g s -> p s g"), [S[pl]], [fin])
                    for s_ in range(NS):
                        pb = gps()
                        kb.tr(pb.t[0:32, 0:128], fin.t[:, pl, s_, :], identf.t[:], [fin, identf], [pb])
                        kb.cp(dve, fo.t[:, pl, s_, :], pb.t[0:32, 0:128], [pb], [fo])
                kb.dma(sp, o_sre_s.t.rearrange("s g q -> g s q"), fo.t[:, 0, :, :], o_sre_s, fo)
                kb.dma(sp, o_sim_s.t.rearrange("s g q -> g s q"), fo.t[:, 1, :, :], o_sim_s, fo)
            if stages < 1 or stop == 10:
                return
            def glu_c(ot, pb):
                t_ = tq[ot % 2]
                kb.actf(t_.t[:], pb.t[:, 0:N], AF.Sigmoid, [pb, v_bglu], [t_], bias=v_bglu.t[:, ot:ot + 1])
                if ot > 0:
                    sq_tile(ot - 1, N)
                kb.tt(dve, t_.t[:], t_.t[:], gT.t[:, ot, :], ALU.mult, [t_, gT], [t_])
                kb.tt(dve, xT.t[:, ot, 0:N], xT.t[:, ot, 0:N], t_.t[:], ALU.add, [xT, t_], [xT])
            linear_fm("glu", 2, N, lambda kt: gT.t[:, kt, :], [gT], glu_c)
            sq_tile(7, N)

    def memattn(i, N, is_prompt):
        rmsnorm(N, g_mq[i], hTs, presq=True)
        with kb.scope() as st:
            qT = kb.sb("qT", [128, 8, N], BF16, st); oT = kb.sb("oT", [128, 8, N], BF16, st)
            E = [kb.sb(f"E{k}", [128, 2, N], BF16, st) for k in range(2)]
            rds = [kb.sb(f"rd{k}", [128, N], F32, st) for k in range(2)]
            if is_prompt:
                segs = [(0, N, kTm[i], vm[i])]
            else:
                segs = []
                Kf = [kb.sb(f"Kf{k}", [128, 2, D], F32, st) for k in range(2)]
                for s_ in range(NS):
                    kTs = kb.sb(f"kTs{s_}", [128, 8, 256], BF16, st); vs_ = kb.sb(f"vs{s_}", [128, 2, D], BF16, st)
                    kf = Kf[0]; vf = Kf[1]
                    kb.dma(sp, kf.t[:], cmk.t[i, s_].rearrange("(mt p) d -> p mt d", p=128), kf, wsrc)
                    kb.dma(sp, vf.t[:], cmv.t[i, s_].rearrange("(mt p) d -> p mt d", p=128), vf, wsrc)
                    kb.cp(pool, vs_.t[:], vf.t[:], [vf], [vs_])
                    for mt in range(2):
                        for half in range(2):
                            pb = gps()
                            for k4 in range(4):
                                c = half * 4 + k4
                                kb.tr(pb.t[:, k4 * 128:(k4 + 1) * 128], kf.t[:, mt, c * 128:(c + 1) * 128], identf.t[:], [kf, identf], [pb], inc=(k4 == 3))
                            kb.cp(act if half else dve, kTs.t[:, half * 4:half * 4 + 4, mt * 128:(mt + 1) * 128],
                                  pb.t[:, :].rearrange("p (a b) -> p a b", b=128), [pb], [kTs])
                    segs.append((LS * s_, LS, kTs, vs_))
            linear_fm(f"mq{i}", 2, N, lambda kt: hT.t[:, kt, 0:N], lambda kt: [hTs[kt]],
                      lambda ot, pb: kb.actf(qT.t[:, ot, :], pb.t[:, 0:N], AF.Copy, [pb], [qT], scale=1.0 / 16.0))
            abank_i = [0]

            def abank():
                bnk = P[abank_i[0] % 8]
                abank_i[0] += 1
                return bnk

            pss_h = {}

            def scores(h):
                pss = [abank(), abank()]
                pss_h[h] = pss
                for (c0, n, kT_b, v_b) in segs:
                    for mt in range(2):
                        for dt_ in range(2):
                            kb.mm(pss[mt].t[:, c0:c0 + n], kT_b.t[:, 2 * h + dt_, mt * 128:(mt + 1) * 128], qT.t[:, 2 * h + dt_, c0:c0 + n],
                                  [kT_b, qT], [pss[mt]], start=(dt_ == 0), stop=(dt_ == 1), inc=(dt_ == 1))
                Eh = E[h % 2]
                for mt in range(2):
                    kb.actf(Eh.t[:, mt, :], pss[mt].t[:, 0:N], AF.Exp, [pss[mt]], [Eh])

            def pv(h):
                Eh = E[h % 2]
                rdh = rds[h % 2]
                pd = abank()
                for mt in range(2):
                    kb.mm(pd.t[:, 0:N], onesb.t[:], Eh.t[:, mt, :], [onesb, Eh], [pd], start=(mt == 0), stop=(mt == 1), inc=(mt == 1))
                kb.recip(rdh.t[:], pd.t[:, 0:N], [pd], [rdh])
                for dt_ in range(2):
                    po = abank()
                    for (c0, n, kT_b, v_b) in segs:
                        for mt in range(2):
                            kb.mm(po.t[:, c0:c0 + n], v_b.t[:, mt, (2 * h + dt_) * 128:(2 * h + dt_ + 1) * 128], Eh.t[:, mt, c0:c0 + n],
                                  [v_b, Eh], [po], start=(mt == 0), stop=(mt == 1), inc=(mt == 1))
                    kb.tt(dve, oT.t[:, 2 * h + dt_, :], po.t[:, 0:N], rdh.t[:], ALU.mult, [po, rdh], [oT])

            scores(0)
            for h in range(4):
                if h + 1 < 4:
                    scores(h + 1)
                pv(h)
            linear_fm(f"mo{i}", 2, N, lambda kt: oT.t[:, kt, :], [oT], add_to_x(N, presq=True))

    def ffn(i, N, nseg, carry):
        L = N // nseg
        NB = 4
        rmsnorm(N, g_ffn[i], hTs, presq=True)
        with kb.scope() as st:
            actTs = [kb.sb(f"actT{c}", [128, 4, N], BF16, st) for c in range(6)]
            aext = [kb.sb(f"aext{k}", [128, nseg, L + 2], F32, st) for k in range(NB)]
            tcs = [kb.sb(f"tc{k}", [128, nseg, L], F32, st) for k in range(NB)]
            tgs = [kb.sb(f"tg{k}", [128, nseg, L], F32, st) for k in range(NB)]
            gss = [kb.sb(f"gs{k}", [128, N], BF16, st) for k in range(NB)]
            scr = W[f"up{i}"]
            cur = {}
            bank_i = [0]

            def bank():
                bnk = P[bank_i[0] % 8]
                bank_i[0] += 1
                return bnk

            def stage_a(ft):
                c, u = ft // 2, ft % 2
                if u == 0:
                    cur["slot"] = ws.get(scr, c)
                slot = cur["slot"]
                wv = slot.t[:, 0:4096].rearrange("p (k o) -> p k o", o=512)
                pa = bank(); pg = bank()
                for kt in range(8):
                    kb.mm(pa.t[:, 0:N], wv[:, kt, u * 128:(u + 1) * 128], hT.t[:, kt, 0:N], [slot, hTs[kt]], [pa], start=(kt == 0), stop=(kt == 7), inc=(kt == 7))
                for kt in range(8):
                    kb.mm(pg.t[:, 0:N], wv[:, kt, 256 + u * 128:256 + (u + 1) * 128], hT.t[:, kt, 0:N], [slot, hTs[kt]], [pg], start=(kt == 0), stop=(kt == 7), inc=(kt == 7))
                ae = aext[ft % NB]; tcb = tcs[ft % NB]; gs = gss[ft % NB]
                kb.cp(act, ae.t[:, :, 0:2], carry.t[:, ft, :, :], [carry], [ae])
                kb.actf(ae.t[:, :, 2:L + 2], pa.t[:, 0:N].rearrange("p (s l) -> p s l", l=L), AF.Copy, [pa], [ae])
                kb.cp(act, gs.t[:], pg.t[:, 0:N], [pg], [gs])
                kb.cp(act, carry.t[:, ft, :, :], ae.t[:, :, L:L + 2], [ae], [carry])
                kb.actf(tcb.t[:], ae.t[:, :, 2:L + 2], AF.Identity, [ae, cw[i][2], cb[i]], [tcb], scale=cw[i][2].t[:, ft:ft + 1], bias=cb[i].t[:, ft:ft + 1])
                kb.stt(tcb.t[:], ae.t[:, :, 1:L + 1], cw[i][1].t[:, ft:ft + 1], tcb.t[:], ALU.mult, ALU.add, [ae, cw[i][1], tcb], [tcb])
                kb.stt(tcb.t[:], ae.t[:, :, 0:L], cw[i][0].t[:, ft:ft + 1], tcb.t[:], ALU.mult, ALU.add, [ae, cw[i][0], tcb], [tcb])

            def stage_b(ft):
                tcb = tcs[ft % NB]; tgb = tgs[ft % NB]
                kb.actf(tgb.t[:], tcb.t[:], AF.Square, [tcb], [tgb], scale=math.sqrt(0.044715))
                kb.stt(tgb.t[:], tgb.t[:], 1.0, tcb.t[:], ALU.add, ALU.mult, [tgb, tcb], [tgb])

            def stage_c(ft):
                tcb = tcs[ft % NB]; tgb = tgs[ft % NB]; gs = gss[ft % NB]
                kb.actf(tgb.t[:], tgb.t[:], AF.Sigmoid, [tgb], [tgb], scale=GELU_C)
                kb.tt(pool, tgb.t[:], tgb.t[:], tcb.t[:], ALU.mult, [tgb, tcb], [tgb])
                kb.tt(pool, actTs[ft // 4].t[:, ft % 4, :].rearrange("p (s l) -> p s l", l=L), tgb.t[:], gs.t[:].rearrange("p (s l) -> p s l", l=L),
                      ALU.mult, [tgb, gs], [actTs[ft // 4]])

            dscr = W[f"down{i}"]

            def down_chunk(c):
                slot = ws.get(dscr, c)
                wv = slot.t[:, 0:4096].rearrange("p (k o) -> p k o", o=1024)
                nk = 4 if c < 5 else 2
                for ot in range(8):
                    for k in range(nk):
                        ft = 4 * c + k
                        kb.mm(P[ot].t[:, 0:N], wv[:, k, ot * 128:(ot + 1) * 128], actTs[c].t[:, k, :], [slot, actTs[c]], [P[ot]],
                              start=(ft == 0), stop=(ft == NFT - 1), inc=(k == nk - 1 and (ot == 7 or c == 5)))

            for step in range(NFT + 2):
                if step < NFT:
                    stage_a(step)
                if 0 <= step - 2 < NFT:
                    stage_c(step - 2)
                if 0 <= step - 1 < NFT:
                    stage_b(step - 1)
                if step == NFT - 1:
                    for c in range(5):
                        down_chunk(c)
            down_chunk(5)
            for ot in range(8):
                add_to_x(N)(ot, P[ot])

    def retention(N, is_prompt, blk):
        nmt = N // 128
        rmsnorm(N, g_mix[1], hTs)
        with kb.scope() as st:
            Cb = kb.sb("Cb", [128, N], F32, st); Sbt = kb.sb("Sbt", [128, N], F32, st)
            Ck = kb.sb("Ck", [128, N], F32, st); Sk = kb.sb("Sk", [128, N], F32, st)
            tA = kb.sb("tA", [128, N], F32, st); tB = kb.sb("tB", [128, N], F32, st)
            tC = kb.sb("tC", [128, N], F32, st); tD = kb.sb("tD", [128, N], F32, st)
            tE = kb.sb("tE", [128, N], F32, st); tF = kb.sb("tF", [128, N], F32, st)
            ygT = kb.sb("ygT", [128, 16, N], BF16, st)
            qTh = [kb.sb(f"qTh{k}", [128, 2, N], BF16, st) for k in range(2)]
            kTh = [kb.sb(f"kTh{k}", [128, 2, N], BF16, st) for k in range(2)]
            kTf = [kb.sb("kTf0", [128, 2, N], F32, st)] * 2
            vTok = [kb.sb(f"vTok{k}", [128, nmt, 512], BF16, st) for k in range(2)]
            kTok = [kb.sb(f"kTok{k}", [128, nmt, 256], BF16, st) for k in range(2)]
            PT = [kb.sb(f"PT{k}", [128, nmt, N], BF16, st) for k in range(2)]
            of32 = [kb.sb(f"of32_{k}", [128, 4, N], F32, st) for k in range(1)]; osq = kb.sb("osq", [128, 4, N], BF16, st)
            mean = kb.sb("mean", [128, N], F32, st); var = kb.sb("var", [128, N], F32, st)
            nseg = 1 if is_prompt else NS
            Lg = N // nseg
            S16 = [kb.sb(f"S16_{s_}", [128, 2, 512], BF16, st) for s_ in range(nseg)]
            Ssf = None if is_prompt else [kb.sb(f"Ssf_{s_}", [128, 2, 512], F32, st) for s_ in range(NS)]
            col = blk if is_prompt else 8
            if is_prompt:
                c0v = C0.t[:, 0:N]; s0v = S0.t[:, 0:N]
                vw = lambda b: b.t[:]
            else:
                c0v = C0.t[:, 0:LS].unsqueeze(1).to_broadcast([128, NS, LS]); s0v = S0.t[:, 0:LS].unsqueeze(1).to_broadcast([128, NS, LS])
                vw = lambda b: b.t[:].rearrange("p (s l) -> p s l", l=LS)
            kb.ts(dve, vw(tA), s0v, dS.t[:, col:col + 1], ALU.mult, [S0, dS], [tA])
            kb.stt(vw(Cb), c0v, dC.t[:, col:col + 1], vw(tA), ALU.mult, ALU.subtract, [C0, dC, tA], [Cb])
            kb.ts(dve, vw(tB), c0v, dS.t[:, col:col + 1], ALU.mult, [C0, dS], [tB])
            kb.stt(vw(Sbt), s0v, dC.t[:, col:col + 1], vw(tB), ALU.mult, ALU.add, [S0, dC, tB], [Sbt])
            kb.ts(dve, Ck.t[:], Cb.t[:], 1.0 / 16.0, ALU.mult, [Cb], [Ck])
            kb.ts(dve, Sk.t[:], Sbt.t[:], 1.0 / 16.0, ALU.mult, [Sbt], [Sk])
            gpow = [math.exp(LG[h] * Lg) for h in range(4)]
            def proj(h):
                qh = qTh[h % 2]; kh = kTh[h % 2]; kf = kTf[h % 2]; vt = vTok[h % 2]; kt_ = kTok[h % 2]; pt = PT[h % 2]
                inner = None
                inner = innerp.t[:, h, 0:N] if is_prompt else inners.t[:, h, :]
                slot = ws.get(W["rq"], h)
                wv = slot.t[:, 0:2048].rearrange("p (k o) -> p k o", o=256)
                p1 = gps(); p2 = gps()
                for kt in range(8):
                    kb.mm(p1.t[:, 0:N], wv[:, kt, 0:128], hT.t[:, kt, 0:N], [slot, hTs[kt]], [p1], start=(kt == 0), stop=(kt == 7), inc=(kt == 7))
                for kt in range(8):
                    kb.mm(p2.t[:, 0:N], wv[:, kt, 128:256], hT.t[:, kt, 0:N], [slot, hTs[kt]], [p2], start=(kt == 0), stop=(kt == 7), inc=(kt == 7))
                kb.tt(dve, tA.t[:], p1.t[:, 0:N], Cb.t[:], ALU.mult, [p1, Cb], [tA])
                kb.tt(dve, tB.t[:], p2.t[:, 0:N], Sbt.t[:], ALU.mult, [p2, Sbt], [tB])
                kb.tt(dve, tE.t[:], p2.t[:, 0:N], Cb.t[:], ALU.mult, [p2, Cb], [tE])
                kb.tt(dve, tF.t[:], p1.t[:, 0:N], Sbt.t[:], ALU.mult, [p1, Sbt], [tF])
                kb.tt(pool, tA.t[:], tA.t[:], tB.t[:], ALU.subtract, [tA, tB], [tA])
                kb.tt(pool, qh.t[:, 0, :], tA.t[:], inner, ALU.mult, [tA, innerp, inners], [qh])
                kb.tt(pool, tE.t[:], tE.t[:], tF.t[:], ALU.add, [tE, tF], [tE])
                kb.tt(pool, qh.t[:, 1, :], tE.t[:], inner, ALU.mult, [tE, innerp, inners], [qh])
                slot = ws.get(W["rk"], h)
                wv = slot.t[:, 0:2048].rearrange("p (k o) -> p k o", o=256)
                p1 = gps(); p2 = gps()
                for kt in range(8):
                    kb.mm(p1.t[:, 0:N], wv[:, kt, 0:128], hT.t[:, kt, 0:N], [slot, hTs[kt]], [p1], start=(kt == 0), stop=(kt == 7), inc=(kt == 7))
                for kt in range(8):
                    kb.mm(p2.t[:, 0:N], wv[:, kt, 128:256], hT.t[:, kt, 0:N], [slot, hTs[kt]], [p2], start=(kt == 0), stop=(kt == 7), inc=(kt == 7))
                kb.tt(dve, tA.t[:], p1.t[:, 0:N], Ck.t[:], ALU.mult, [p1, Ck], [tA])
                kb.tt(dve, tB.t[:], p2.t[:, 0:N], Sk.t[:], ALU.mult, [p2, Sk], [tB])
                kb.tt(dve, tE.t[:], p2.t[:, 0:N], Ck.t[:], ALU.mult, [p2, Ck], [tE])
                kb.tt(dve, tF.t[:], p1.t[:, 0:N], Sk.t[:], ALU.mult, [p1, Sk], [tF])
                kb.tt(pool, kf.t[:, 0, :], tA.t[:], tB.t[:], ALU.subtract, [tA, tB], [kf])
                kb.tt(pool, kf.t[:, 1, :], tE.t[:], tF.t[:], ALU.add, [tE, tF], [kf])
                kb.cp(act, kh.t[:], kf.t[:], [kf], [kh])
                slot = ws.get(W["rv"], h)
                wv = slot.t[:, 0:4096].rearrange("p (k o) -> p k o", o=512)
                for mt in range(nmt):
                    pb = gps()
                    for kt in range(8):
                        kb.mm(pb.t[:, 0:512], hT.t[:, kt, mt * 128:(mt + 1) * 128], wv[:, kt, :], [slot, hTs[kt]], [pb], start=(kt == 0), stop=(kt == 7), inc=(kt == 7))
                    kb.cp(act, vt.t[:, mt, :], pb.t[:, 0:512], [pb], [vt])
                slot = ws.get(W["rg"], h)
                wv = slot.t[:, 0:4096].rearrange("p (k o) -> p k o", o=512)
                for et in range(4):
                    pb = gps()
                    for kt in range(8):
                        kb.mm(pb.t[:, 0:N], wv[:, kt, et * 128:(et + 1) * 128], hT.t[:, kt, 0:N], [slot, hTs[kt]], [pb], start=(kt == 0), stop=(kt == 7), inc=(kt == 7))
                    kb.actf(ygT.t[:, 4 * h + et, :], pb.t[:, 0:N], AF.Silu, [pb], [ygT])
                for mt in range(nmt):
                    pb = gps()
                    for dt_ in range(2):
                        kb.tr(pb.t[:, dt_ * 128:(dt_ + 1) * 128], kf.t[:, dt_, mt * 128:(mt + 1) * 128], identf.t[:], [kf, identf], [pb], inc=(dt_ == 1))
                    tl = tailp.t[:, h, mt:mt + 1] if is_prompt else tails.t[:, h:h + 1]
                    kb.actf(kt_.t[:, mt, :], pb.t[:, 0:256], AF.Identity, [pb, tailp, tails], [kt_], scale=tl)
                for mt in range(nmt):
                    lo = 128 * mt if is_prompt else 0
                    pb = gps()
                    for dt_ in range(2):
                        kb.mm(pb.t[:, 0:N - lo], kh.t[:, dt_, mt * 128:(mt + 1) * 128], qh.t[:, dt_, lo:N], [kh, qh], [pb],
                              start=(dt_ == 0), stop=(dt_ == 1), inc=(dt_ == 1))
                    if is_prompt:
                        kb.stt(pt.t[:, mt, lo:N], pb.t[:, 0:N - lo], tinvp.t[:, h, mt:mt + 1], M01.t[:, 0:N - lo], ALU.mult, ALU.mult, [pb, tinvp, M01], [pt])
                    else:
                        kb.stt(pt.t[:, mt, :], pb.t[:, 0:N], tinvs.t[:, h:h + 1], M01s.t[:], ALU.mult, ALU.mult, [pb, tinvs, M01s], [pt])
            def attn_a(h):
                qh = qTh[h % 2]; kh = kTh[h % 2]; kf = kTf[h % 2]; vt = vTok[h % 2]; kt_ = kTok[h % 2]; pt = PT[h % 2]
                if is_prompt:
                    kb.cp(act, S16[0].t[:], Sp.t[:, h, :, :], [Sp], [S16[0]])
                else:
                    for s_ in range(NS):
                        kb.dma(sp, Ssf[s_].t[:], sret.t[s_, h].rearrange("(dt p) e -> p dt e", p=128), Ssf[s_], wsrc)
                        kb.cp(act, S16[s_].t[:], Ssf[s_].t[:], [Ssf[s_]], [S16[s_]])
                for et in range(4):
                    ob = P[4 + et]
                    for mt in range(nmt):
                        lo = 128 * mt if is_prompt else 0
                        kb.mm(ob.t[:, lo:N], vt.t[:, mt, et * 128:(et + 1) * 128], pt.t[:, mt, lo:N], [vt, pt], [ob], start=(mt == 0), stop=False, inc=False)
                    for s_ in range(nseg):
                        for dt_ in range(2):
                            last = (s_ == nseg - 1 and dt_ == 1)
                            kb.mm(ob.t[:, s_ * Lg:(s_ + 1) * Lg], S16[s_].t[:, dt_, et * 128:(et + 1) * 128], qh.t[:, dt_, s_ * Lg:(s_ + 1) * Lg],
                                  [S16[s_], qh], [ob], start=False, stop=last, inc=last)
                of = of32[0]
                for et in range(4):
                    kb.cp(act, of.t[:, et, :], P[4 + et].t[:, 0:N], [P[4 + et]], [of])
                    kb.actf(osq.t[:, et, :], P[4 + et].t[:, 0:N], AF.Square, [P[4 + et]], [osq])
                for s_ in range(nseg):
                    for dt_ in range(2):
                        pb = gps()
                        if is_prompt:
                            for mt in range(nmt):
                                kb.mm(pb.t[:, 0:512], kt_.t[:, mt, dt_ * 128:(dt_ + 1) * 128], vt.t[:, mt, :], [kt_, vt], [pb],
                                      start=(mt == 0), stop=(mt == nmt - 1), inc=(mt == nmt - 1))
                            sdst = Sp.t[:, h, dt_, :]; sbuf_ = Sp
                        else:
                            r = slice(32 * s_, 32 * s_ + 32)
                            kb.mm(pb.t[:, 0:512], kt_.t[r, 0, dt_ * 128:(dt_ + 1) * 128], vt.t[r, 0, :], [kt_, vt], [pb], start=True, stop=True,
                                  tile_position=(32 * s_, 0))
                            sdst = Ssf[s_].t[:, dt_, :]; sbuf_ = Ssf[s_]
                        kb.stt(sdst, sdst, gpow[h], pb.t[:, 0:512], ALU.mult, ALU.add, [sbuf_, pb], [sbuf_])
                    if not is_prompt:
                        kb.dma(sp, o_ret_s.t[s_, h].rearrange("(dt p) e -> p dt e", p=128), Ssf[s_].t[:], o_ret_s, Ssf[s_])
            def attn_b(h):
                qh = qTh[h % 2]; kh = kTh[h % 2]; kf = kTf[h % 2]; vt = vTok[h % 2]; kt_ = kTok[h % 2]; pt = PT[h % 2]
                of = of32[0]
                psm = gps(); psq = gps()
                for et in range(4):
                    kb.mm(psm.t[:, 0:N], onesf.t[:], of.t[:, et, :], [onesf, of], [psm], start=(et == 0), stop=(et == 3), inc=(et == 3))
                for et in range(4):
                    kb.mm(psq.t[:, 0:N], onesb.t[:], osq.t[:, et, :], [onesb, osq], [psq], start=(et == 0), stop=(et == 3), inc=(et == 3))
                kb.actf(mean.t[:], psm.t[:, 0:N], AF.Copy, [psm], [mean], scale=1.0 / 512.0)
                kb.tt(dve, tD.t[:], mean.t[:], mean.t[:], ALU.mult, [mean], [tD])
                kb.stt(var.t[:], psq.t[:, 0:N], 1.0 / 512.0, tD.t[:], ALU.mult, ALU.subtract, [psq, tD], [var])
                kb.actf(var.t[:], var.t[:], AF.Sqrt, [var, epsb], [var], bias=epsb.t[:, 1:2])
                kb.recip(var.t[:], var.t[:], [var], [var])
                for et in range(4):
                    tcx = tC
                    kb.tt(dve, tcx.t[:], of.t[:, et, :], mean.t[:], ALU.subtract, [of, mean], [tcx])
                    kb.tt(dve, tcx.t[:], tcx.t[:], var.t[:], ALU.mult, [tcx, var], [tcx])
                    kb.tt(pool, ygT.t[:, 4 * h + et, :], tcx.t[:], ygT.t[:, 4 * h + et, :], ALU.mult, [tcx, ygT], [ygT])
            proj(0)
            proj(1)
            attn_a(0)
            proj(2)
            attn_b(0)
            attn_a(1)
            proj(3)
            attn_b(1)
            attn_a(2)
            attn_b(2)
            attn_a(3)
            attn_b(3)
            linear_fm("ro", 4, N, lambda kt: ygT.t[:, kt, :], [ygT], add_to_x(N, presq=True))

    with kb.scope() as st:
        mtk = kb.sb("memt", [128, 2, D], F32, st)
        kb.dma(sp, mtk.t[:], mem.t.rearrange("(mt p) d -> p mt d", p=128), mtk, wsrc)
        ssq = kb.sb("ssq", [128, 2], F32, st); junk = kb.sb("junk", [128, D], BF16, st)
        for mt in range(2):
            kb.actf(junk.t[:], mtk.t[:, mt, :], AF.Square, [mtk], [junk, ssq], accum_out=ssq.t[:, mt:mt + 1])
        kb.actf(ssq.t[:], ssq.t[:], AF.Sqrt, [ssq, epsb], [ssq], scale=1.0 / D, bias=epsb.t[:, 0:1])
        kb.recip(ssq.t[:], ssq.t[:], [ssq], [ssq])
        for mt in range(2):
            kb.ts(dve, mtk.t[:, mt, :], mtk.t[:, mt, :], ssq.t[:, mt:mt + 1], ALU.mult, [mtk, ssq], [mtk])
        memnT = kb.sb("memnT", [128, 8, 256], F32, st)
        for mt in range(2):
            for half in range(2):
                pb = gps()
                for k4 in range(4):
                    ct = half * 4 + k4
                    kb.tr(pb.t[:, k4 * 128:(k4 + 1) * 128], mtk.t[:, mt, ct * 128:(ct + 1) * 128], identf.t[:], [mtk, identf], [pb], inc=(k4 == 3))
                kb.cp(act if half else dve, memnT.t[:, half * 4:half * 4 + 4, mt * 128:(mt + 1) * 128],
                      pb.t[:, :].rearrange("p (a b) -> p a b", b=128), [pb], [memnT])
        memhT = kb.sb("memhT", [128, 8, 256], BF16, st)
        wsl = [kb.sb(f"wsl{k}", [128, 4096], BF16, st) for k in range(2)]
        kof = [kb.sb(f"kof{k}", [128, 512], F32, st) for k in range(2)]
        nko = 0
        for i in range(2):
            for ct in range(8):
                kb.ts(dve, memhT.t[:, ct, :], memnT.t[:, ct, :], g_mkv[i].t[:, ct:ct + 1], ALU.mult, [memnT, g_mkv[i]], [memhT])
            scr = W[f"mkv{i}"]
            for c in range(4):
                sl = wsl[c % 2]
                scr.load(kb, sl, c, 4096)
                wv = sl.t[:].rearrange("p (k o) -> p k o", o=512)
                if c < 2:
                    for o_ in range(4):
                        pb = gps()
                        for kt in range(8):
                            kb.mm(pb.t[:, 0:256], wv[:, kt, o_ * 128:(o_ + 1) * 128], memhT.t[:, kt, :], [sl, memhT], [pb], start=(kt == 0), stop=(kt == 7), inc=(kt == 7))
                        kb.cp(act, kTm[i].t[:, c * 4 + o_, :], pb.t[:, 0:256], [pb], [kTm[i]])
                for mt in range(2):
                    pb = gps()
                    for kt in range(8):
                        kb.mm(pb.t[:, 0:512], memhT.t[:, kt, mt * 128:(mt + 1) * 128], wv[:, kt, :], [sl, memhT], [pb], start=(kt == 0), stop=(kt == 7), inc=(kt == 7))
                    ko = kof[nko % 2]; nko += 1
                    kb.cp(dve, ko.t[:], pb.t[:, 0:512], [pb], [ko])
                    dsto = o_mk if c < 2 else o_mv
                    kb.dma(sp, dsto.t[i, mt * 128:(mt + 1) * 128, (c % 2) * 512:(c % 2 + 1) * 512], ko.t[:], dsto, ko)
                    if c >= 2:
                        kb.cp(act, vm[i].t[:, mt, (c - 2) * 512:(c - 1) * 512], pb.t[:, 0:512], [pb], [vm[i]])

    if stop == 4:
        kb.barrier(); return kb
    with kb.scope() as st:
        cvn = [kb.sb(f"cvn{k}", [88, 128], F32, st) for k in range(2)]
        for i in range(2):
            for hh in range(2):
                cn = cvn[hh]
                kb.dma(sp, cn.t[:], cconv.t[i][2 * hh:2 * hh + 2].rearrange("s r (ft p) -> (s r ft) p", p=128), cn, wsrc)
                pb = gps()
                kb.tr(pb.t[:, 0:88], cn.t[:], identf.t[0:88, 0:88], [cn, identf], [pb])
                kb.cp(dve, convs[i].t[:, :, 2 * hh:2 * hh + 2, :], pb.t[:, 0:88].rearrange("p (s r ft) -> p ft s r", s=2, r=2), [pb], [convs[i]])
    cast_done = [False]
    def run_block(N, is_prompt, blk):
        nseg = 1 if is_prompt else NS
        if is_prompt:
            load_x(xp, blk * TB, N)
        else:
            load_x(xs, 0, N)
        if stop == 5:
            return
        s5_block(N, nseg, is_prompt)
        if not cast_done[0]:
            cast_done[0] = True
            cast_group(2)
        if 6 <= stop <= 10:
            return
        if dbg == "mix0" or stages < 1:
            return
        memattn(0, N, is_prompt)
        if dbg == "att0" or stages < 2:
            return
        ffn(0, N, nseg, convc[0] if is_prompt else convs[0])
        if dbg == "ffn0" or stages < 3:
            return
        retention(N, is_prompt, blk)
        if dbg == "mix1" or stages < 4:
            return
        memattn(1, N, is_prompt)
        if dbg == "att1" or stages < 5:
            return
        ffn(1, N, nseg, convc[1] if is_prompt else convs[1])

    order = [("p", 0)] + ([("s", 0)] if do_sample else []) + [("p", b) for b in range(1, nblk)]
    if stages >= 6:
        for _ in order:
            ws.plan(sched_block())
    for kind, b in order:
        if stages < 6:
            full = sched_block()
            cut = {"mix0": 22, "att0": 26, "ffn0": 43, "mix1": 63, "att1": 67}.get(dbg, len(full))
            ws.plan(full[:cut])
        if kind == "p":
            run_block(TB, True, b)
            if dbg is not None and dbg_blk == ("p", b):
                dump_dbg(TB)
            if dbg is None:
                store_y(yp, b * TB, TB)
        else:
            run_block(NS * LS, False, 0)
            if dbg is not None and dbg_blk == ("s", 0):
                dump_dbg(NS * LS)
            if dbg is None:
                store_y(ys, 0, NS * LS)
                with kb.scope() as st:
                    ctmp = [kb.sb(f"ctmp{k}", [128, 88], F32, st) for k in range(2)]
                    cout = [kb.sb(f"cout{k}", [88, 128], F32, st) for k in range(2)]
                    for i in range(2):
                        for hh in range(2):
                            k = (2 * i + hh) % 2
                            kb.cp(dve, ctmp[k].t[:].rearrange("p (s r ft) -> p ft s r", s=2, r=2), convs[i].t[:, :, 2 * hh:2 * hh + 2, :], [convs[i]], [ctmp[k]])
                            pb = gps()
                            kb.tr(pb.t[0:88, 0:128], ctmp[k].t[:], identf.t[:], [ctmp[k], identf], [pb])
                            kb.cp(dve, cout[k].t[:], pb.t[0:88, 0:128], [pb], [cout[k]])
                            kb.dma(sp, o_conv_s.t[i][2 * hh:2 * hh + 2].rearrange("s r (ft p) -> (s r ft) p", p=128), cout[k].t[:], o_conv_s, cout[k])

    with kb.scope() as st:
        fo = kb.sb("fo_p", [32, 2, 128], F32, st)
        for pl in range(2):
            pb = gps()
            kb.tr(pb.t[0:32, 0:128], s5c[pl].t[:], identf.t[:], [s5c[pl], identf], [pb])
            kb.cp(dve, fo.t[:, pl, :], pb.t[0:32, 0:128], [pb], [fo])
        kb.dma(sp, o_sre_p.t[:, :], fo.t[:, 0, :], o_sre_p, fo)
        kb.dma(sp, o_sim_p.t[:, :], fo.t[:, 1, :], o_sim_p, fo)
        for h in range(4):
            kb.dma(sp, o_ret_p.t[h].rearrange("(dt p) e -> p dt e", p=128), Sp.t[:, h, :, :], o_ret_p, Sp)
        for i in range(2):
            ctp = kb.sb(f"ctp{i}", [128, 44], F32, st)
            cop = kb.sb(f"cop{i}", [44, 128], F32, st)
            kb.cp(dve, ctp.t[:].rearrange("p (r ft) -> p ft r", r=2), convc[i].t[:, :, 0, :], [convc[i]], [ctp])
            pb = gps()
            kb.tr(pb.t[0:44, 0:128], ctp.t[:], identf.t[:], [ctp, identf], [pb])
            kb.cp(dve, cop.t[:], pb.t[0:44, 0:128], [pb], [cop])
            kb.dma(sp, o_conv_p.t[i].rearrange("r (ft p) -> (r ft) p", p=128), cop.t[:], o_conv_p, cop)
    kb.barrier(final=True)
    return kb


_W_KEYS = ["norm_mix", "norm_mem_q", "norm_mem_kv", "norm_ffn", "norm_final", "mem_w_q", "mem_w_kv", "mem_w_o",
           "ffn_w_up", "ffn_conv_w", "ffn_conv_b", "ffn_w_down"]
_W0_KEYS = ["ssm_log_dt", "ssm_b_re", "ssm_b_im", "ssm_c_re", "ssm_c_im", "ssm_d", "ssm_w_glu", "ssm_b_glu", "ret_w_qkvg", "ret_w_o"]


def make_in_maps(inputs, cores):
    f = lambda a: np.ascontiguousarray(np.asarray(a, dtype=np.float32))
    shared = {k: f(inputs[k]) for k in _W_KEYS}
    for k in _W0_KEYS:
        shared[k] = f(inputs[k][0])
    shared["ssm_a_re"] = f(inputs["ssm_a_re"][0]).reshape(32, 128)
    shared["ssm_a_im"] = f(inputs["ssm_a_im"][0]).reshape(32, 128)
    maps = []
    for c in cores:
        b = c % 4
        s = slice(NS * c, NS * c + NS)
        m = dict(shared)
        m["xp"] = f(inputs["x_prompt"][b])
        m["xs"] = f(inputs["x_sample"][s]).reshape(NS * LS, D)
        m["mem"] = f(inputs["mem_prompt"][b])
        m["sre"] = f(inputs["state_ssm_re"][0, s]).reshape(NS, 32, 128)
        m["sim"] = f(inputs["state_ssm_im"][0, s]).reshape(NS, 32, 128)
        m["sret"] = f(inputs["state_ret"][0, s])
        m["cmk"] = f(inputs["cache_mem_k"][:, s]).reshape(2, NS, 256, D)
        m["cmv"] = f(inputs["cache_mem_v"][:, s]).reshape(2, NS, 256, D)
        m["cconv"] = f(inputs["cache_conv"][:, s])
        maps.append(m)
    return maps


def kernel(**inputs):
    nc = bass.Bass("TRN2", target_bir_lowering=False)
    build(nc)
    cores = list(range(8))
    res = run_bass_kernel_spmd(nc, make_in_maps(inputs, cores), core_ids=cores)
    R = res.results
    y_prompt = np.stack([R[b]["yp"] for b in range(4)])
    y_sample = np.concatenate([R[c]["ys"].reshape(NS, LS, D) for c in cores])
    re_p = np.stack([R[b]["o_sre_p"].reshape(64, 64) for b in range(4)])[None]
    im_p = np.stack([R[b]["o_sim_p"].reshape(64, 64) for b in range(4)])[None]
    re_s = np.concatenate([R[c]["o_sre_s"].reshape(NS, 64, 64) for c in cores])[None]
    im_s = np.concatenate([R[c]["o_sim_s"].reshape(NS, 64, 64) for c in cores])[None]
    ret_p = np.stack([R[b]["o_ret_p"] for b in range(4)])[None]
    ret_s = np.concatenate([R[c]["o_ret_s"] for c in cores])[None]
    mk_p = np.stack([R[b]["o_mk"].reshape(2, 256, 4, 256) for b in range(4)], axis=1)
    mv_p = np.stack([R[b]["o_mv"].reshape(2, 256, 4, 256) for b in range(4)], axis=1)
    conv_p = np.stack([R[b]["o_conv_p"] for b in range(4)], axis=1)
    conv_s = np.concatenate([R[c]["o_conv_s"] for c in cores], axis=1)
    outs = (y_prompt, y_sample, re_p, im_p, re_s, im_s, ret_p, ret_s, mk_p, mv_p, conv_p, conv_s)
    return tuple(np.ascontiguousarray(o, dtype=np.float32) for o in outs)
```

```python
import math
from contextlib import ExitStack, contextmanager

import numpy as np
import concourse.bass as bass
import concourse.mybir as mybir
from concourse.bass_utils import run_bass_kernel_spmd

F32 = mybir.dt.float32
BF16 = mybir.dt.bfloat16
I32 = mybir.dt.int32
AF = mybir.ActivationFunctionType
ALU = mybir.AluOpType
AX = mybir.AxisListType

D = 1024
SEQ = 4096
TB = 512
NBLK = SEQ // TB
NS = 4
LS = 32
PAST = 1024
DFF = 2816
NFT = DFF // 128
EPS = 1e-6
GN_EPS = 1e-5
LG = [math.log(1.0 - 2.0 ** (-5.0 - h)) for h in range(4)]
GELU_C = 1.5957691216057308


class Sem:
    def __init__(self, h, name):
        self.h = h
        self.name = name


class Eng:
    def __init__(self, name, h, sem):
        self.name = name
        self.h = h
        self.sem = sem
        self.cnt = 0
        self.seen = {}


class Buf:
    def __init__(self, name, t, kind="sb"):
        self.name = name
        self.t = t
        self.kind = kind
        self.w = {}
        self.r = {}
        self.dsem = None
        self.dcnt = 0

    def __getitem__(self, k):
        return self.t[k]


class KB:
    def __init__(self, nc):
        self.nc = nc
        self.es = ExitStack()
        self.nsem = 0
        self.pe = Eng("pe", nc.tensor, self.newsem("pe"))
        self.act = Eng("act", nc.scalar, self.newsem("act"))
        self.dve = Eng("dve", nc.vector, self.newsem("dve"))
        self.pool = Eng("pool", nc.gpsimd, self.newsem("pool"))
        self.sp = Eng("sp", nc.sync, self.newsem("sp"))
        self.engs = [self.pe, self.act, self.dve, self.pool, self.sp]
        self.dbufs = []
        self.uid = 0

    def newsem(self, name):
        self.nsem += 1
        return Sem(self.es.enter_context(self.nc.semaphore(f"s{self.nsem}_{name}")), name)

    def sb(self, name, shape, dt, st=None):
        self.uid += 1
        t = (st or self.es).enter_context(self.nc.sbuf_tensor(f"{name}_{self.uid}", list(shape), dt))
        b = Buf(name, t)
        b.scoped = st is not None
        return b

    def psum(self, name, shape, dt):
        t = self.es.enter_context(self.nc.psum_tensor(name, list(shape), dt))
        return Buf(name, t, "ps")

    def dram(self, name, shape, dt, kind):
        h = self.nc.dram_tensor(name, list(shape), dt, kind=kind)
        return Buf(name, h.ap(), "dram")

    def _wait(self, e, sem, v):
        if v > e.seen.get(sem, 0):
            if sem is e.sem:
                assert v <= e.cnt, f"same-engine wait on pending inc {e.name}"
            e.h.wait_ge(sem.h, v)
            e.seen[sem] = v

    def op(self, e, fn, rd=(), wr=(), inc=True):
        if any(b.kind == "ps" for b in rd):
            wr = list(wr) + [b for b in rd if b.kind == "ps"]
            rd = [b for b in rd if b.kind != "ps"]
        need = {}
        for b in rd:
            for sem, v in b.w.items():
                if v > need.get(sem, 0):
                    need[sem] = v
        skip_self = e is self.pe
        for b in wr:
            for sem, v in b.r.items():
                if not (skip_self and sem is e.sem) and v > need.get(sem, 0):
                    need[sem] = v
            for sem, v in b.w.items():
                if not (skip_self and sem is e.sem) and v > need.get(sem, 0):
                    need[sem] = v
        for sem, v in need.items():
            self._wait(e, sem, v)
        ins = fn()
        val = e.cnt + 1
        if inc:
            ins.then_inc(e.sem.h, 1)
            e.cnt = val
        for b in rd:
            if b.r.get(e.sem, 0) < val:
                b.r[e.sem] = val
        for b in wr:
            if b.w.get(e.sem, 0) < val:
                b.w[e.sem] = val
        return ins

    def dma(self, q, out_ap, in_ap, dst, src, **kw):
        need = {}
        for sem, v in src.w.items():
            if v > need.get(sem, 0):
                need[sem] = v
        if dst.kind != "dram":
            for sem, v in dst.r.items():
                if v > need.get(sem, 0):
                    need[sem] = v
            for sem, v in dst.w.items():
                if v > need.get(sem, 0):
                    need[sem] = v
        owner = dst if dst.kind == "sb" else (src if src.kind == "sb" else dst)
        if owner.dsem is None:
            owner.dsem = self.newsem("d_" + owner.name)
            self.dbufs.append(owner)
        if owner.dcnt > 0 and owner.kind != "dram":
            need[owner.dsem] = owner.dcnt
        for sem, v in need.items():
            self._wait(q, sem, v)
        ins = q.h.dma_start(out=out_ap, in_=in_ap, **kw)
        ins.then_inc(owner.dsem.h, 16)
        owner.dcnt += 16
        dst.w[owner.dsem] = owner.dcnt
        src.r[owner.dsem] = owner.dcnt

    def barrier(self, final=False):
        sp = self.sp
        for e in self.engs:
            if e is not sp and (final or e.cnt != getattr(e, "bar_cnt", -1)):
                self._wait(sp, e.sem, e.cnt)
            e.bar_cnt = e.cnt
        for b in self.dbufs:
            if final or (b.kind != "dram" and getattr(b, "scoped", False)):
                self._wait(sp, b.dsem, b.dcnt)
        ins = sp.h.nop()
        ins.then_inc(sp.sem.h, 1)
        sp.cnt += 1
        for e in self.engs:
            if e is not sp:
                self._wait(e, sp.sem, sp.cnt)

    @contextmanager
    def scope(self):
        st = ExitStack()
        try:
            yield st
        finally:
            self.barrier()
            st.close()

    def mm(self, out, lhsT, rhs, rd, wr, start=True, stop=True, inc=True, **kw):
        nc = self.nc
        return self.op(self.pe, lambda: nc.tensor.matmul(out, lhsT, rhs, start=start, stop=stop, **kw), rd, wr, inc)

    def tr(self, out, in_, ident, rd, wr, inc=True):
        nc = self.nc
        return self.op(self.pe, lambda: nc.tensor.transpose(out, in_, ident), rd, wr, inc)

    def actf(self, out, in_, func, rd, wr, scale=None, bias=None, accum_out=None):
        nc = self.nc
        kw = {}
        if scale is not None:
            kw["scale"] = scale
        if bias is not None:
            kw["bias"] = bias
        if accum_out is not None:
            kw["accum_out"] = accum_out
        return self.op(self.act, lambda: nc.scalar.activation(out=out, in_=in_, func=func, **kw), rd, wr)

    def tt(self, e, out, in0, in1, op, rd, wr):
        return self.op(e, lambda: e.h.tensor_tensor(out=out, in0=in0, in1=in1, op=op), rd, wr)

    def ts(self, e, out, in0, s1, op0, rd, wr, s2=None, op1=None):
        if op1 is None:
            return self.op(e, lambda: e.h.tensor_scalar(out=out, in0=in0, scalar1=s1, scalar2=None, op0=op0), rd, wr)
        return self.op(e, lambda: e.h.tensor_scalar(out=out, in0=in0, scalar1=s1, scalar2=s2, op0=op0, op1=op1), rd, wr)

    def stt(self, out, in0, scalar, in1, op0, op1, rd, wr):
        nc = self.nc
        return self.op(self.dve, lambda: nc.vector.scalar_tensor_tensor(out=out, in0=in0, scalar=scalar, in1=in1, op0=op0, op1=op1), rd, wr)

    def cp(self, e, out, in_, rd, wr):
        if e is self.act:
            return self.actf(out, in_, AF.Copy, rd, wr)
        return self.op(e, lambda: e.h.tensor_copy(out=out, in_=in_), rd, wr)

    def memset(self, e, ap, val, wr):
        return self.op(e, lambda: e.h.memset(ap, val), (), wr)

    def recip(self, out, in_, rd, wr):
        nc = self.nc
        return self.op(self.dve, lambda: nc.vector.reciprocal(out=out, in_=in_), rd, wr)


class WStream:
    def __init__(self, kb, nslots, slot_elems):
        self.kb = kb
        self.n = nslots
        self.slots = [kb.sb(f"wslot{i}", [128, slot_elems], BF16) for i in range(nslots)]
        self.sched = []
        self.pos = 0
        self.issued = 0

    def plan(self, lst):
        self.sched.extend(lst)

    def _issue(self, i):
        scr, c, ne = self.sched[i]
        slot = self.slots[i % self.n]
        scr.load(self.kb, slot, c, ne)

    def get(self, scr, c):
        key = self.sched[self.pos]
        assert key[0] is scr and key[1] == c, f"wstream mismatch at {self.pos}: want {scr.name},{c} sched {key[0].name},{key[1]}"
        lim = min(len(self.sched), self.pos + self.n)
        while self.issued < lim:
            self._issue(self.issued)
            self.issued += 1
        slot = self.slots[self.pos % self.n]
        self.pos += 1
        return slot


def build(nc, nblk=NBLK, do_sample=True, dbg=None, dbg_blk=("p", 0), stages=99, stop=99):
    kb = KB(nc)
    pe, act, dve, pool, sp = kb.pe, kb.act, kb.dve, kb.pool, kb.sp

    def din(name, shape):
        return kb.dram(name, shape, F32, "ExternalInput")

    def dout(name, shape):
        return kb.dram(name, shape, F32, "ExternalOutput")

    xp = din("xp", [SEQ, D]); xs = din("xs", [NS * LS, D]); mem = din("mem", [256, D])
    sre = din("sre", [NS, 32, 128]); sim = din("sim", [NS, 32, 128])
    sret = din("sret", [NS, 4, 256, 512])
    cmk = din("cmk", [2, NS, 256, D]); cmv = din("cmv", [2, NS, 256, D])
    cconv = din("cconv", [2, NS, 2, DFF])
    norm_mix = din("norm_mix", [2, D]); norm_mem_q = din("norm_mem_q", [2, D])
    norm_mem_kv = din("norm_mem_kv", [2, D]); norm_ffn = din("norm_ffn", [2, D])
    norm_final = din("norm_final", [D])
    a_re = din("ssm_a_re", [32, 128]); a_im = din("ssm_a_im", [32, 128])
    log_dt = din("ssm_log_dt", [64])
    b_re = din("ssm_b_re", [64, 64, 16]); b_im = din("ssm_b_im", [64, 64, 16])
    c_re = din("ssm_c_re", [64, 16, 64]); c_im = din("ssm_c_im", [64, 16, 64])
    ssm_d = din("ssm_d", [D]); w_glu = din("ssm_w_glu", [D, D]); b_glu = din("ssm_b_glu", [D])
    w_qkvg = din("ret_w_qkvg", [D, 6144]); w_ro = din("ret_w_o", [2048, D])
    w_mq = din("mem_w_q", [2, D, D]); w_mkv = din("mem_w_kv", [2, D, 2048]); w_mo = din("mem_w_o", [2, D, D])
    w_up = din("ffn_w_up", [2, D, 2 * DFF]); conv_w = din("ffn_conv_w", [2, 3, DFF]); conv_b = din("ffn_conv_b", [2, DFF])
    w_down = din("ffn_w_down", [2, DFF, D])

    yp = dout("yp", [SEQ, D]); ys = dout("ys", [NS * LS, D])
    o_sre_p = dout("o_sre_p", [32, 128]); o_sim_p = dout("o_sim_p", [32, 128])
    o_sre_s = dout("o_sre_s", [NS, 32, 128]); o_sim_s = dout("o_sim_s", [NS, 32, 128])
    o_ret_p = dout("o_ret_p", [4, 256, 512]); o_ret_s = dout("o_ret_s", [NS, 4, 256, 512])
    o_mk = dout("o_mk", [2, 256, D]); o_mv = dout("o_mv", [2, 256, D])
    o_conv_p = dout("o_conv_p", [2, 2, DFF]); o_conv_s = dout("o_conv_s", [2, NS, 2, DFF])
    dbg_out = dout("dbg", [128, 8, TB]) if dbg is not None else None

    def wscr(name, nch, kt, oc):
        b = kb.dram("scr_" + name, [nch, 128, kt * oc], BF16, "Internal")
        b.kt = kt; b.oc = oc; b.nch = nch
        b.load = lambda kb_, slot, c, ne, b=b: kb_.dma(kb_.sp, slot.t[:, 0:ne], b.t[c][:, 0:ne], slot, b)
        return b

    def wpm(name, kt, ocols, oc, kind="cols"):
        b = kb.dram("scr_" + name, [128, kt, ocols], BF16, "Internal")
        b.kt = kt; b.oc = oc; b.nch = ocols // oc

        def load(kb_, slot, c, ne, b=b, kind=kind, kt=kt, oc=oc):
            if kind == "cols":
                kb_.dma(kb_.sp, slot.t[:, 0:kt * oc].rearrange("p (k o) -> p k o", o=oc), b.t[:, :, c * oc:(c + 1) * oc], slot, b)
            elif kind == "up":
                sv = slot.t[:, 0:kt * 512].rearrange("p (k o) -> p k o", o=512)
                kb_.dma(kb_.sp, sv[:, :, 0:256], b.t[:, :, c * 256:(c + 1) * 256], slot, b)
                kb_.dma(kb_.sp, sv[:, :, 256:512], b.t[:, :, DFF + c * 256:DFF + (c + 1) * 256], slot, b)
            else:
                nk = ne // 1024
                kb_.dma(kb_.sp, slot.t[:, 0:ne].rearrange("p (k o) -> p k o", o=1024), b.t[:, 4 * c:4 * c + nk, :], slot, b)
        b.load = load
        return b

    W = {}
    W["glu"] = wpm("glu", 8, D, 512)
    for i in range(2):
        W[f"mq{i}"] = wpm(f"mq{i}", 8, D, 512)
        W[f"mo{i}"] = wpm(f"mo{i}", 8, D, 512)
        W[f"mkv{i}"] = wpm(f"mkv{i}", 8, 2048, 512)
        W[f"up{i}"] = wpm(f"up{i}", 8, 2 * DFF, 512, "up")
        W[f"down{i}"] = wpm(f"down{i}", NFT, D, 1024, "rows")
    W["rq"] = wpm("rq", 8, 1024, 256); W["rk"] = wpm("rk", 8, 1024, 256)
    W["rv"] = wpm("rv", 8, 2048, 512); W["rg"] = wpm("rg", 8, 2048, 512)
    W["ro"] = wpm("ro", 16, D, 256)
    W["s5f"] = wscr("s5f", 8, 1, 2048)
    W["s5g"] = wscr("s5g", 8, 1, 2048)
    W["s5k"] = wscr("s5k", 8, 1, 1024)

    wsrc = Buf("wsrc", None, "dram")

    def cast_w(scr, src2d, col0=0):
        ncols = scr.nch * scr.oc
        kb.dma(pool, scr.t[:, :, :], src2d[:, col0:col0 + ncols].rearrange("(kt p) o -> p kt o", p=128), scr, wsrc)

    def cast_up(i):
        cast_w(W[f"up{i}"], w_up.t[i])

    def cast_down(i):
        scr = W[f"down{i}"]
        kb.dma(pool, scr.t[:, :, :], w_down.t[i].rearrange("(kt p) o -> p kt o", p=128), scr, wsrc)

    def cast_group(g):
        if g == 0:
            for i in range(2):
                cast_w(W[f"mkv{i}"], w_mkv.t[i])
        elif g == 1:
            cast_w(W["glu"], w_glu.t)
            cast_w(W["mq0"], w_mq.t[0]); cast_w(W["mo0"], w_mo.t[0])
            cast_up(0)
            cast_down(0)
        else:
            cast_w(W["rq"], w_qkvg.t, 0); cast_w(W["rk"], w_qkvg.t, 1024)
            cast_w(W["rv"], w_qkvg.t, 2048); cast_w(W["rg"], w_qkvg.t, 4096)
            cast_w(W["ro"], w_ro.t)
            cast_w(W["mq1"], w_mq.t[1]); cast_w(W["mo1"], w_mo.t[1])
            cast_up(1)
            cast_down(1)

    if stop == 0:
        kb.barrier(); return kb
    P = [kb.psum(f"pb{i}", [128, 512], F32) for i in range(8)]
    gen_i = [0]

    def gps():
        b = P[gen_i[0] % 4]
        gen_i[0] += 1
        return b

    identf = kb.sb("identf", [128, 128], F32)
    onesb = kb.sb("onesb", [128, 128], BF16)
    kb.memset(pool, identf.t[:], 1.0, [identf])
    kb.op(pool, lambda: nc.gpsimd.affine_select(out=identf.t[:], in_=identf.t[:], pattern=[[-1, 128]],
                                                compare_op=ALU.is_equal, fill=0.0, base=0, channel_multiplier=1),
          [identf], [identf])
    kb.memset(dve, onesb.t[:], 1.0, [onesb])
    onesf = kb.sb("onesf", [128, 128], F32)
    kb.memset(dve, onesf.t[:], 1.0, [onesf])

    def alias(base, name, ap):
        v = Buf(name, ap)
        v.w = base.w; v.r = base.r
        return v

    vec_nat = kb.sb("vec_nat", [88, 128], F32)
    vecT = kb.sb("vecT", [128, 88], F32)
    vsrcs = [norm_mix.t[0], norm_mix.t[1], norm_mem_q.t[0], norm_mem_q.t[1], norm_mem_kv.t[0], norm_mem_kv.t[1],
             norm_ffn.t[0], norm_ffn.t[1], norm_final.t, ssm_d.t, b_glu.t]
    for k, ap in enumerate(vsrcs):
        kb.dma(sp, vec_nat.t[8 * k:8 * k + 8, :], ap.rearrange("(t p) -> t p", p=128), vec_nat, wsrc)
    vviews = [alias(vecT, f"vec{k}", vecT.t[:, 8 * k:8 * k + 8]) for k in range(11)]
    g_mix = vviews[0:2]; g_mq = vviews[2:4]; g_mkv = vviews[4:6]; g_ffn = vviews[6:8]
    g_fin = vviews[8]; v_d = vviews[9]; v_bglu = vviews[10]
    cw = []; cb = []; cnat = []; cT = []
    for i in range(2):
        cn = kb.sb(f"cnat{i}", [88, 128], F32)
        ct_ = kb.sb(f"cT{i}", [128, 88], F32)
        kb.dma(sp, cn.t[0:66, :], conv_w.t[i].rearrange("r (ft p) -> (r ft) p", p=128), cn, wsrc)
        kb.dma(sp, cn.t[66:88, :], conv_b.t[i].rearrange("(ft p) -> ft p", p=128), cn, wsrc)
        cnat.append(cn); cT.append(ct_)
        cw.append([alias(ct_, f"cw{i}_{r}", ct_.t[:, 22 * r:22 * r + 22]) for r in range(3)])
        cb.append(alias(ct_, f"cb{i}", ct_.t[:, 66:88]))

    Apow = {lv: (kb.sb(f"A{lv}re", [128, 32], F32), kb.sb(f"A{lv}im", [128, 32], F32)) for lv in (8, 16, 32, 64)}
    C0 = kb.sb("C0", [128, TB], F32); S0 = kb.sb("S0", [128, TB], F32)
    dC = kb.sb("dC", [128, 16], F32); dS = kb.sb("dS", [128, 16], F32)
    M01 = kb.sb("M01", [128, TB], F32)
    M01s = kb.sb("M01s", [128, 128], F32)
    innerp = kb.sb("innerp", [128, 4, TB], F32)
    inners = kb.sb("inners", [128, 4, 128], F32)
    tailp = kb.sb("tailp", [128, 4, 4], F32)
    tinvp = kb.sb("tinvp", [128, 4, 4], F32)
    tails = kb.sb("tails", [128, 4], F32)
    tinvs = kb.sb("tinvs", [128, 4], F32)

    def tr_f32(dst_ap, dst_buf, src_ap, src_buf, n, e=None):
        pb = gps()
        kb.tr(pb.t[:, 0:n], src_ap, identf.t[0:n, 0:n], [src_buf, identf], [pb])
        kb.cp(e or dve, dst_ap, pb.t[:, 0:n], [pb], [dst_buf])

    tr_f32(vecT.t[:], vecT, vec_nat.t[:], vec_nat, 88)
    for i in range(2):
        tr_f32(cT[i].t[:], cT[i], cnat[i].t[:], cnat[i], 88)
    if stop == 1:
        kb.barrier(); return kb
    with kb.scope() as st:
        ii = kb.sb("ii", [128, TB], I32, st)
        ff = kb.sb("ff", [128, TB], F32, st)

        def iota_f(dst_ap, dst_buf, pattern, base, cm):
            nfree = 1
            for _, n in pattern:
                nfree *= n
            kb.op(pool, lambda: nc.gpsimd.iota(ii.t[:, 0:nfree], pattern=pattern, base=base, channel_multiplier=cm), (), [ii])
            kb.cp(dve, dst_ap, ii.t[:, 0:nfree], [ii], [dst_buf])

        fj = kb.sb("fj", [128, 1], F32, st)
        iota_f(fj.t[:], fj, [[0, 1]], 0, 1)
        kb.actf(fj.t[:], fj.t[:], AF.Exp, [fj], [fj], scale=-math.log(10000.0) / 128.0)
        hpi = kb.sb("hpi", [128, 1], F32, st)
        kb.memset(dve, hpi.t[:], math.pi / 2, [hpi])
        er = kb.sb("er", [128, 16], F32, st); ei = kb.sb("ei", [128, 16], F32, st)
        kb.actf(er.t[:, 0:1], fj.t[:], AF.Sin, [fj, hpi], [er], scale=-1.0, bias=hpi.t[:, 0:1])
        kb.actf(ei.t[:, 0:1], fj.t[:], AF.Sin, [fj], [ei])
        t1 = kb.sb("t1", [128, TB], F32, st); t2 = kb.sb("t2", [128, TB], F32, st)
        for k in range(10):
            kb.tt(dve, t1.t[:, 0:1], er.t[:, k:k + 1], er.t[:, k:k + 1], ALU.mult, [er], [t1])
            kb.tt(dve, t2.t[:, 0:1], ei.t[:, k:k + 1], ei.t[:, k:k + 1], ALU.mult, [ei], [t2])
            kb.tt(dve, er.t[:, k + 1:k + 2], t1.t[:, 0:1], t2.t[:, 0:1], ALU.subtract, [t1, t2], [er])
            kb.tt(dve, t1.t[:, 0:1], er.t[:, k:k + 1], ei.t[:, k:k + 1], ALU.mult, [er, ei], [t1])
            kb.ts(dve, ei.t[:, k + 1:k + 2], t1.t[:, 0:1], 2.0, ALU.mult, [t1], [ei])
        kb.memset(dve, C0.t[:, 0:1], 1.0, [C0]); kb.memset(dve, S0.t[:, 0:1], 0.0, [S0])
        for k in range(9):
            n = 1 << k
            cr = er.t[:, k:k + 1]; ci = ei.t[:, k:k + 1]
            kb.ts(dve, t1.t[:, 0:n], S0.t[:, 0:n], ci, ALU.mult, [S0, ei], [t1])
            kb.stt(C0.t[:, n:2 * n], C0.t[:, 0:n], cr, t1.t[:, 0:n], ALU.mult, ALU.subtract, [C0, er, t1], [C0])
            kb.ts(dve, t2.t[:, 0:n], C0.t[:, 0:n], ci, ALU.mult, [C0, ei], [t2])
            kb.stt(S0.t[:, n:2 * n], S0.t[:, 0:n], cr, t2.t[:, 0:n], ALU.mult, ALU.add, [S0, er, t2], [S0])
        kb.memset(dve, dC.t[:, 0:1], 1.0, [dC]); kb.memset(dve, dS.t[:, 0:1], 0.0, [dS])
        for b in range(1, 8):
            cr = er.t[:, 9:10]; ci = ei.t[:, 9:10]
            kb.ts(dve, t1.t[:, 0:1], dS.t[:, b - 1:b], ci, ALU.mult, [dS, ei], [t1])
            kb.stt(dC.t[:, b:b + 1], dC.t[:, b - 1:b], cr, t1.t[:, 0:1], ALU.mult, ALU.subtract, [dC, er, t1], [dC])
            kb.ts(dve, t2.t[:, 0:1], dC.t[:, b - 1:b], ci, ALU.mult, [dC, ei], [t2])
            kb.stt(dS.t[:, b:b + 1], dS.t[:, b - 1:b], cr, t2.t[:, 0:1], ALU.mult, ALU.add, [dS, er, t2], [dS])
        kb.cp(dve, dC.t[:, 8:9], er.t[:, 10:11], [er], [dC])
        kb.cp(dve, dS.t[:, 8:9], ei.t[:, 10:11], [ei], [dS])

        kb.memset(pool, M01.t[:], 1.0, [M01])
        kb.op(pool, lambda: nc.gpsimd.affine_select(out=M01.t[:], in_=M01.t[:], pattern=[[1, TB]], compare_op=ALU.is_ge,
                                                    fill=0.0, base=0, channel_multiplier=-1), [M01], [M01])
        kb.memset(pool, M01s.t[:], 1.0, [M01s])
        for s_ in range(NS):
            blk = M01s.t[:, 32 * s_:32 * s_ + 32]
            kb.op(pool, lambda blk=blk, s_=s_: nc.gpsimd.affine_select(out=blk, in_=blk, pattern=[[1, 32]], compare_op=ALU.is_ge,
                                                                      fill=0.0, base=32 * s_, channel_multiplier=-1), [M01s], [M01s])
            kb.op(pool, lambda blk=blk, s_=s_: nc.gpsimd.affine_select(out=blk, in_=blk, pattern=[[0, 32]], compare_op=ALU.is_ge,
                                                                      fill=0.0, base=-32 * s_, channel_multiplier=1), [M01s], [M01s])
        rt = kb.sb("rt", [128, 128], F32, st)
        for h in range(4):
            iota_f(ff.t[:, 0:TB], ff, [[1, TB]], 1, 0)
            kb.actf(innerp.t[:, h, :], ff.t[:, 0:TB], AF.Exp, [ff], [innerp], scale=LG[h])
            iota_f(ff.t[:, 0:128], ff, [[0, 4], [1, 32]], 1, 0)
            kb.actf(inners.t[:, h, :], ff.t[:, 0:128], AF.Exp, [ff], [inners], scale=LG[h])
            iota_f(ff.t[:, 0:4], ff, [[-128, 4]], TB - 1, -1)
            kb.actf(tailp.t[:, h, :], ff.t[:, 0:4], AF.Exp, [ff], [tailp], scale=LG[h])
            iota_f(ff.t[:, 0:4], ff, [[128, 4]], 1, 1)
            kb.actf(tinvp.t[:, h, :], ff.t[:, 0:4], AF.Exp, [ff], [tinvp], scale=-LG[h])
            iota_f(ff.t[:, 0:128], ff, [[0, 4], [-1, 32]], 31, 0)
            kb.actf(rt.t[:], ff.t[:, 0:128], AF.Exp, [ff], [rt], scale=LG[h])
            tr_f32(tails.t[:, h:h + 1], tails, rt.t[0:1, :], rt, 1)
            kb.recip(rt.t[:], inners.t[:, h, :], [inners], [rt])
            tr_f32(tinvs.t[:, h:h + 1], tinvs, rt.t[0:1, :], rt, 1)

    cast_group(0)
    cast_group(1)
    if stop == 2:
        kb.barrier(); return kb
    with kb.scope() as st:
        ewi = [0]

        def ew():
            ewi[0] += 1
            return dve if ewi[0] % 2 else pool

        nat = kb.sb("nat", [32, 3, 128], F32, st)
        kb.dma(sp, nat.t[:, 0, :], a_re.t[:, :], nat, wsrc)
        kb.dma(sp, nat.t[:, 1, :], a_im.t[:, :], nat, wsrc)
        ld2 = kb.sb("ld2", [32, 2], F32, st)
        kb.dma(sp, ld2.t[:], log_dt.t.rearrange("(gp gpar) -> gp gpar", gpar=2), ld2, wsrc)
        kb.cp(dve, nat.t[:, 2, :].rearrange("g (a p) -> g a p", p=64), ld2.t[:].unsqueeze(2).to_broadcast([32, 2, 64]), [ld2], [nat])
        q_are = kb.sb("q_are", [128, 32], F32, st); q_aim = kb.sb("q_aim", [128, 32], F32, st); q_dt = kb.sb("q_dt", [128, 32], F32, st)
        tr_f32(q_are.t[:], q_are, nat.t[:, 0, :], nat, 32)
        tr_f32(q_aim.t[:], q_aim, nat.t[:, 1, :], nat, 32)
        tr_f32(q_dt.t[:], q_dt, nat.t[:, 2, :], nat, 32)
        kb.actf(q_dt.t[:], q_dt.t[:], AF.Exp, [q_dt], [q_dt])
        lr = kb.sb("lr", [128, 32], F32, st); li = kb.sb("li", [128, 32], F32, st)
        kb.tt(dve, lr.t[:], q_are.t[:], q_dt.t[:], ALU.mult, [q_are, q_dt], [lr])
        kb.tt(dve, li.t[:], q_aim.t[:], q_dt.t[:], ALU.mult, [q_aim, q_dt], [li])
        hpi = kb.sb("hpi2", [128, 1], F32, st)
        kb.memset(dve, hpi.t[:], math.pi / 2, [hpi])
        e8 = kb.sb("e8", [128, 32], F32, st); ar = kb.sb("ar", [128, 32], F32, st); ai = kb.sb("ai", [128, 32], F32, st)
        u1 = kb.sb("u1", [128, 32], F32, st); u2 = kb.sb("u2", [128, 32], F32, st)
        kb.actf(e8.t[:], lr.t[:], AF.Exp, [lr], [e8], scale=0.125)
        kb.actf(u1.t[:], li.t[:], AF.Sin, [li, hpi], [u1], scale=-0.125, bias=hpi.t[:, 0:1])
        kb.actf(u2.t[:], li.t[:], AF.Sin, [li], [u2], scale=0.125)
        kb.tt(dve, ar.t[:], e8.t[:], u1.t[:], ALU.mult, [e8, u1], [ar])
        kb.tt(dve, ai.t[:], e8.t[:], u2.t[:], ALU.mult, [e8, u2], [ai])

        def csq(r, i):
            kb.tt(dve, u1.t[:], r.t[:], r.t[:], ALU.mult, [r], [u1])
            kb.tt(dve, u2.t[:], i.t[:], i.t[:], ALU.mult, [i], [u2])
            kb.tt(dve, u2.t[:], u1.t[:], u2.t[:], ALU.subtract, [u1, u2], [u2])
            kb.tt(dve, u1.t[:], r.t[:], i.t[:], ALU.mult, [r, i], [u1])
            kb.ts(dve, i.t[:], u1.t[:], 2.0, ALU.mult, [u1], [i])
            kb.cp(dve, r.t[:], u2.t[:], [u2], [r])

        for _ in range(3):
            csq(ar, ai)
        Pre = kb.sb("Pre", [128, 9, 32], F32, st); Pim = kb.sb("Pim", [128, 9, 32], F32, st)
        kb.memset(dve, Pre.t[:, 0, :], 1.0, [Pre]); kb.memset(dve, Pim.t[:, 0, :], 0.0, [Pim])
        for m in range(8):
            kb.tt(dve, u1.t[:], Pre.t[:, m, :], ar.t[:], ALU.mult, [Pre, ar], [u1])
            kb.tt(dve, u2.t[:], Pim.t[:, m, :], ai.t[:], ALU.mult, [Pim, ai], [u2])
            kb.tt(dve, Pre.t[:, m + 1, :], u1.t[:], u2.t[:], ALU.subtract, [u1, u2], [Pre])
            kb.tt(dve, u1.t[:], Pre.t[:, m, :], ai.t[:], ALU.mult, [Pre, ai], [u1])
            kb.tt(dve, u2.t[:], Pim.t[:, m, :], ar.t[:], ALU.mult, [Pim, ar], [u2])
            kb.tt(dve, Pim.t[:, m + 1, :], u1.t[:], u2.t[:], ALU.add, [u1, u2], [Pim])
        kb.cp(dve, Apow[8][0].t[:], Pre.t[:, 8, :], [Pre], [Apow[8][0]])
        kb.cp(dve, Apow[8][1].t[:], Pim.t[:, 8, :], [Pim], [Apow[8][1]])
        for lv in (16, 32, 64):
            kb.cp(dve, Apow[lv][0].t[:], Apow[lv // 2][0].t[:], [Apow[lv // 2][0]], [Apow[lv][0]])
            kb.cp(dve, Apow[lv][1].t[:], Apow[lv // 2][1].t[:], [Apow[lv // 2][1]], [Apow[lv][1]])
            csq(Apow[lv][0], Apow[lv][1])
        cre = kb.sb("cre", [128, 32], F32, st); cim = kb.sb("cim", [128, 32], F32, st)
        xm1 = kb.sb("xm1", [128, 32], F32, st); rden = kb.sb("rden", [128, 32], F32, st)
        kb.ts(dve, xm1.t[:], ar.t[:], -1.0, ALU.add, [ar], [xm1])
        kb.tt(dve, u1.t[:], q_are.t[:], q_are.t[:], ALU.mult, [q_are], [u1])
        kb.tt(dve, u2.t[:], q_aim.t[:], q_aim.t[:], ALU.mult, [q_aim], [u2])
        kb.tt(dve, u1.t[:], u1.t[:], u2.t[:], ALU.add, [u1, u2], [u1])
        kb.recip(rden.t[:], u1.t[:], [u1], [rden])
        kb.tt(dve, u1.t[:], xm1.t[:], q_are.t[:], ALU.mult, [xm1, q_are], [u1])
        kb.tt(dve, u2.t[:], ai.t[:], q_aim.t[:], ALU.mult, [ai, q_aim], [u2])
        kb.tt(dve, u1.t[:], u1.t[:], u2.t[:], ALU.add, [u1, u2], [u1])
        kb.tt(dve, cre.t[:], u1.t[:], rden.t[:], ALU.mult, [u1, rden], [cre])
        kb.tt(dve, u1.t[:], ai.t[:], q_are.t[:], ALU.mult, [ai, q_are], [u1])
        kb.tt(dve, u2.t[:], xm1.t[:], q_aim.t[:], ALU.mult, [xm1, q_aim], [u2])
        kb.tt(dve, u1.t[:], u1.t[:], u2.t[:], ALU.subtract, [u1, u2], [u1])
        kb.tt(dve, cim.t[:], u1.t[:], rden.t[:], ALU.mult, [u1, rden], [cim])

        Bre = kb.sb("Bre", [128, 32, 16], F32, st); Bim = kb.sb("Bim", [128, 32, 16], F32, st)
        for gpar in range(2):
            kb.dma(sp, Bre.t[64 * gpar:64 * gpar + 64, :, :],
                   b_re.t.rearrange("(gp gpar) p c -> gpar p gp c", gpar=2)[gpar], Bre, wsrc)
            kb.dma(sp, Bim.t[64 * gpar:64 * gpar + 64, :, :],
                   b_im.t.rearrange("(gp gpar) p c -> gpar p gp c", gpar=2)[gpar], Bim, wsrc)
        Bbre = kb.sb("Bbre", [128, 32, 16], F32, st); Bbim = kb.sb("Bbim", [128, 32, 16], F32, st)
        w1 = kb.sb("w1", [128, 32, 32], F32, st); w2 = kb.sb("w2", [128, 32, 32], F32, st)

        def bc(buf_ap, n):
            return buf_ap.unsqueeze(2).to_broadcast([128, 32, n])

        kb.tt(dve, w1.t[:, :, 0:16], Bre.t[:], bc(cre.t[:], 16), ALU.mult, [Bre, cre], [w1])
        kb.tt(dve, w2.t[:, :, 0:16], Bim.t[:], bc(cim.t[:], 16), ALU.mult, [Bim, cim], [w2])
        kb.tt(dve, Bbre.t[:], w1.t[:, :, 0:16], w2.t[:, :, 0:16], ALU.subtract, [w1, w2], [Bbre])
        kb.tt(dve, w1.t[:, :, 0:16], Bim.t[:], bc(cre.t[:], 16), ALU.mult, [Bim, cre], [w1])
        kb.tt(dve, w2.t[:, :, 0:16], Bre.t[:], bc(cim.t[:], 16), ALU.mult, [Bre, cim], [w2])
        kb.tt(dve, Bbim.t[:], w1.t[:, :, 0:16], w2.t[:, :, 0:16], ALU.add, [w1, w2], [Bbim])

        Qbd = kb.sb("Qbd", [128, 8, 8, 2, 128], F32, st)
        kb.memset(dve, Qbd.t[:], 0.0, [Qbd])
        for m in range(8):
            for plane in range(2):
                pa, pb_ = (Pre, Pim) if plane == 0 else (Pim, Pre)
                kb.tt(dve, w1.t[:, :, 0:16], Bbre.t[:], bc(pa.t[:, m, :], 16), ALU.mult, [Bbre, pa], [w1])
                kb.tt(dve, w2.t[:, :, 0:16], Bbim.t[:], bc(pb_.t[:, m, :], 16), ALU.mult, [Bbim, pb_], [w2])
                for gpar in range(2):
                    ps_ = slice(64 * gpar, 64 * gpar + 64)
                    dst = Qbd.t[ps_, :, m, plane, :].rearrange("p ct (j c) -> p ct j c", c=32)[:, :, :, 16 * gpar:16 * gpar + 16]
                    i0 = w1.t[ps_, :, 0:16].rearrange("p (ct j) c -> p ct j c", j=4)
                    i1 = w2.t[ps_, :, 0:16].rearrange("p (ct j) c -> p ct j c", j=4)
                    kb.tt(dve, dst, i0, i1, ALU.subtract if plane == 0 else ALU.add, [w1, w2], [Qbd])

        Cn = [kb.sb("Cnre", [128, 8, 64], F32, st), kb.sb("Cnim", [128, 8, 64], F32, st)]
        kb.dma(sp, Cn[0].t[:], c_re.t.rearrange("(ct gl) co p -> (gl co) ct p", gl=8), Cn[0], wsrc)
        kb.dma(sp, Cn[1].t[:], c_im.t.rearrange("(ct gl) co p -> (gl co) ct p", gl=8), Cn[1], wsrc)
        mk = kb.sb("mk", [128, 2], F32, st)
        idv = identf.t[:].rearrange("p (a b c) -> p a b c", b=2, c=16)
        for par in range(2):
            kb.op(dve, lambda par=par: nc.vector.tensor_reduce(out=mk.t[:, par:par + 1], in_=idv[:, :, par, :], axis=AX.XY, op=ALU.add),
                  [identf], [mk])
        Cpad = kb.sb("Cpad", [128, 8, 128], F32, st)
        CT = [kb.sb("CTre", [128, 32, 32], F32, st), kb.sb("CTimN", [128, 32, 32], F32, st)]
        for pl in range(2):
            for par in range(2):
                kb.ts(dve, Cpad.t[:, :, 64 * par:64 * par + 64], Cn[pl].t[:], mk.t[:, par:par + 1], ALU.mult, [Cn[pl], mk], [Cpad])
            for ct in range(8):
                pb = gps()
                kb.tr(pb.t[:, 0:128], Cpad.t[:, ct, :], identf.t[:], [Cpad, identf], [pb])
                kb.cp(act, CT[pl].t[:, 4 * ct:4 * ct + 4, :].rearrange("p j c -> p (j c)"), pb.t[:, 0:128], [pb], [CT[pl]])
        Gbd = kb.sb("Gbd", [128, 32, 8, 2, 32], BF16, st)
        for m in range(1, 9):
            kb.tt(dve, w1.t[:], CT[0].t[:], bc(Pre.t[:, m, :], 32), ALU.mult, [CT[0], Pre], [w1])
            kb.tt(dve, w2.t[:], CT[1].t[:], bc(Pim.t[:, m, :], 32), ALU.mult, [CT[1], Pim], [w2])
            kb.tt(dve, Gbd.t[:, :, m - 1, 0, :], w1.t[:], w2.t[:], ALU.subtract, [w1, w2], [Gbd])
            kb.tt(dve, w1.t[:], CT[1].t[:], bc(Pre.t[:, m, :], 32), ALU.mult, [CT[1], Pre], [w1])
            kb.tt(dve, w2.t[:], CT[0].t[:], bc(Pim.t[:, m, :], 32), ALU.mult, [CT[0], Pim], [w2])
            kb.tt(dve, w1.t[:], w1.t[:], w2.t[:], ALU.add, [w1, w2], [w1])
            kb.ts(dve, Gbd.t[:, :, m - 1, 1, :], w1.t[:], -1.0, ALU.mult, [w1], [Gbd])
        kb.ts(dve, CT[1].t[:], CT[1].t[:], -1.0, ALU.mult, [CT[1]], [CT[1]])

        Qb = kb.sb("Qb", [128, 8, 2, 128], BF16, st)
        CTb = [kb.sb("CTbre", [128, 32, 32], BF16, st), kb.sb("CTbim", [128, 32, 32], BF16, st)]
        for pl in range(2):
            kb.cp(dve, CTb[pl].t[:], CT[pl].t[:], [CT[pl]], [CTb[pl]])
        Fst = [kb.sb(f"Fst{i}", [128, 2048], BF16, st) for i in range(2)]
        Kst = [kb.sb(f"Kst{i}", [128, 8, 128], BF16, st) for i in range(2)]
        for ct in range(8):
            fs = Fst[ct % 2]; ks = Kst[ct % 2]
            for half in range(4):
                pb = gps()
                for k4 in range(4):
                    idx = half * 4 + k4
                    tau, plane = idx // 2, idx % 2
                    kb.tr(pb.t[:, k4 * 128:(k4 + 1) * 128], Qbd.t[:, ct, 7 - tau, plane, :], identf.t[:], [Qbd, identf], [pb], inc=(k4 == 3))
                kb.cp(act if half % 2 else dve, fs.t[:, half * 512:(half + 1) * 512], pb.t[:, :], [pb], [fs])
            kb.dma(sp, W["s5f"].t[ct], fs.t[:], W["s5f"], fs)
            kb.cp(dve, Qb.t[:], Qbd.t[:, ct, :, :, :], [Qbd], [Qb])
            pb = gps()
            pv = pb.t[:, 0:256].rearrange("p (d c) -> p d c", c=32)
            for j in range(4):
                for dl in range(8):
                    for pl in range(2):
                        kb.mm(pv[32 * j:32 * j + 32, dl, :], Qb.t[:, dl, pl, 32 * j:32 * j + 32], CTb[pl].t[:, 4 * ct + j, :],
                              [Qb, CTb[pl]], [pb], start=(pl == 0), stop=(pl == 1), inc=(j == 3 and dl == 7 and pl == 1),
                              tile_position=(0, 32 * j))
            kb.memset(dve, ks.t[:], 0.0, [ks])
            for j in range(4):
                kb.cp(dve, ks.t[32 * j:32 * j + 32, :, 32 * j:32 * j + 32], pv[32 * j:32 * j + 32, :, :], [pb], [ks])
            kb.dma(sp, W["s5g"].t[ct].rearrange("p (j r) -> p j r", j=4),
                   Gbd.t[:, 4 * ct:4 * ct + 4, :, :, :].rearrange("p j m a c -> p j (m a c)"), W["s5g"], Gbd)
            kb.dma(sp, W["s5k"].t[ct], ks.t[:].rearrange("p d c -> p (d c)"), W["s5k"], ks)

    if stop == 3:
        kb.barrier(); return kb
    xT = kb.sb("xT", [128, 8, TB], F32)
    hT = kb.sb("hT", [128, 8, TB], BF16)
    hTs = [Buf(f"hT{c_}", hT.t) for c_ in range(8)]
    Sp = kb.sb("Sp", [128, 4, 2, 512], F32)
    s5c = [kb.sb("s5c_re", [128, 32], F32), kb.sb("s5c_im", [128, 32], F32)]
    convc = [kb.sb(f"convc{i}", [128, NFT, 1, 2], F32) for i in range(2)]
    convs = [kb.sb(f"convs{i}", [128, NFT, NS, 2], F32) for i in range(2)]
    kTm = [kb.sb(f"kTm{i}", [128, 8, 256], BF16) for i in range(2)]
    vm = [kb.sb(f"vm{i}", [128, 2, D], BF16) for i in range(2)]
    kb.memset(dve, Sp.t[:], 0.0, [Sp])
    for b_ in s5c:
        kb.memset(dve, b_.t[:], 0.0, [b_])
    for i in range(2):
        kb.memset(dve, convc[i].t[:], 0.0, [convc[i]])

    epsb = kb.sb("epsb", [128, 2], F32)
    kb.memset(dve, epsb.t[:, 0:1], EPS, [epsb]); kb.memset(dve, epsb.t[:, 1:2], GN_EPS, [epsb])
    ws = WStream(kb, 4, 4096)

    def sched_block():
        l = [(W["s5f"], ct, 2048) for ct in range(8)]
        l += [(W["s5k"], ct, 1024) for ct in range(8)] + [(W["s5g"], ct, 2048) for ct in range(8)]
        l += [(W["glu"], c, 4096) for c in range(2)]
        l += [(W["mq0"], c, 4096) for c in range(2)] + [(W["mo0"], c, 4096) for c in range(2)]
        l += [(W["up0"], c, 4096) for c in range(11)] + [(W["down0"], c, 4096 if c < 5 else 2048) for c in range(6)]
        for h in range(4):
            l += [(W["rq"], h, 2048), (W["rk"], h, 2048), (W["rv"], h, 4096), (W["rg"], h, 4096)]
        l += [(W["ro"], c, 4096) for c in range(4)]
        l += [(W["mq1"], c, 4096) for c in range(2)] + [(W["mo1"], c, 4096) for c in range(2)]
        l += [(W["up1"], c, 4096) for c in range(11)] + [(W["down1"], c, 4096 if c < 5 else 2048) for c in range(6)]
        return l

    rn_sq = kb.sb("rn_sq", [128, 4, TB], BF16)
    rn_sq2 = kb.sb("rn_sq2", [128, 4, TB], BF16)
    rn_rs = kb.sb("rn_rs", [128, TB], F32)

    def sq_tile(ot, N):
        buf = rn_sq if ot < 4 else rn_sq2
        kb.actf(buf.t[:, ot % 4, 0:N], xT.t[:, ot, 0:N], AF.Square, [xT], [buf])

    def rmsnorm(N, gain, out_buf, presq=False):
        sq = rn_sq; rs = rn_rs
        if not presq:
            kb.actf(sq.t[:, 0:4, 0:N], xT.t[:, 0:4, 0:N], AF.Square, [xT], [sq])
            kb.tt(pool, rn_sq2.t[:, :, 0:N], xT.t[:, 4:8, 0:N], xT.t[:, 4:8, 0:N], ALU.mult, [xT], [rn_sq2])
        pb = gps()
        for ct in range(8):
            sqb = sq if ct < 4 else rn_sq2
            kb.mm(pb.t[:, 0:N], onesb.t[:], sqb.t[:, ct % 4, 0:N], [onesb, sqb], [pb], start=(ct == 0), stop=(ct == 7), inc=(ct == 7))
        kb.actf(rs.t[:, 0:N], pb.t[:, 0:N], AF.Sqrt, [pb, epsb], [rs], scale=1.0 / D, bias=epsb.t[:, 0:1])
        kb.recip(rs.t[:, 0:N], rs.t[:, 0:N], [rs], [rs])
        for ct in range(8):
            ob = out_buf[ct] if isinstance(out_buf, list) else out_buf
            kb.stt(ob.t[:, ct, 0:N], xT.t[:, ct, 0:N], gain.t[:, ct:ct + 1], rs.t[:, 0:N], ALU.mult, ALU.mult, [xT, gain, rs], [ob])

    def gelu(out_ap, out_buf, x_ap, x_buf, t_ap, t_buf):
        kb.actf(t_ap, x_ap, AF.Square, [x_buf], [t_buf], scale=math.sqrt(0.044715))
        kb.stt(t_ap, t_ap, 1.0, x_ap, ALU.add, ALU.mult, [t_buf, x_buf], [t_buf])
        kb.actf(t_ap, t_ap, AF.Sigmoid, [t_buf], [t_buf], scale=GELU_C)
        kb.tt(pool, out_ap, t_ap, x_ap, ALU.mult, [t_buf, x_buf], [out_buf])

    def linear_fm(wname, nch, N, rhs_fn, rhs_bufs, consume):
        scr = W[wname]
        kt_n, oc = scr.kt, scr.oc
        for c in range(nch):
            slot = ws.get(scr, c)
            wv = slot.t[:, 0:kt_n * oc].rearrange("p (k o) -> p k o", o=oc)
            for o_ in range(oc // 128):
                pb = gps()
                for kt in range(kt_n):
                    rb = rhs_bufs(kt) if callable(rhs_bufs) else rhs_bufs
                    kb.mm(pb.t[:, 0:N], wv[:, kt, o_ * 128:(o_ + 1) * 128], rhs_fn(kt), [slot] + rb, [pb],
                          start=(kt == 0), stop=(kt == kt_n - 1), inc=(kt == kt_n - 1))
                consume(c * (oc // 128) + o_, pb)

    def add_to_x(N, presq=False):
        def f(ot, pb):
            kb.tt(dve, xT.t[:, ot, 0:N], xT.t[:, ot, 0:N], pb.t[:, 0:N], ALU.add, [xT, pb], [xT])
            if presq:
                sq_tile(ot, N)
        return f

    def load_x(src, row0, N):
        with kb.scope() as st:
            stg = [kb.sb(f"xstg{i}", [128, D], F32, st) for i in range(2)]
            for tt_ in range(N // 128):
                sg = stg[tt_ % 2]
                kb.dma(sp, sg.t[:], src.t[row0 + tt_ * 128:row0 + (tt_ + 1) * 128, :], sg, wsrc)
                for half in range(2):
                    pb = gps()
                    for k4 in range(4):
                        ct = half * 4 + k4
                        kb.tr(pb.t[:, k4 * 128:(k4 + 1) * 128], sg.t[:, ct * 128:(ct + 1) * 128], identf.t[:], [sg, identf], [pb], inc=(k4 == 3))
                    kb.cp(act if half else dve, xT.t[:, half * 4:half * 4 + 4, tt_ * 128:(tt_ + 1) * 128],
                          pb.t[:, :].rearrange("p (a b) -> p a b", b=128), [pb], [xT])
                    sqb = rn_sq2 if half else rn_sq
                    kb.actf(sqb.t[:, :, tt_ * 128:(tt_ + 1) * 128], xT.t[:, half * 4:half * 4 + 4, tt_ * 128:(tt_ + 1) * 128], AF.Square, [xT], [sqb])

    def store_y(dst, row0, N):
        with kb.scope() as st:
            yF = kb.sb("yF", [128, 8, N], F32, st)
            rmsnorm(N, g_fin, yF)
            stg = [kb.sb(f"ystg{i}", [128, D], F32, st) for i in range(2)]
            for tt_ in range(N // 128):
                sg = stg[tt_ % 2]
                for half in range(2):
                    pb = gps()
                    for k4 in range(4):
                        ct = half * 4 + k4
                        kb.tr(pb.t[:, k4 * 128:(k4 + 1) * 128], yF.t[:, ct, tt_ * 128:(tt_ + 1) * 128], identf.t[:], [yF, identf], [pb], inc=(k4 == 3))
                    kb.cp(act if half else dve, sg.t[:, half * 512:(half + 1) * 512], pb.t[:, :], [pb], [sg])
                kb.dma(sp, dst.t[row0 + tt_ * 128:row0 + (tt_ + 1) * 128, :], sg.t[:], dst, sg)

    def dump_dbg(N):
        kb.dma(sp, dbg_out.t[:, :, 0:N], xT.t[:, :, 0:N], dbg_out, xT)

    def s5_block(N, nseg, is_prompt):
        NK = N // 8
        Lc = NK // nseg
        Ltop = min(8, Lc)
        nl = int(math.log2(Ltop))
        rmsnorm(N, g_mix[0], hTs, presq=True)
        with kb.scope() as st:
            V = [[kb.sb(f"V{l}_{pl}", [128, 32, NK >> l], F32, st) for pl in range(2)] for l in range(nl + 1)]
            S = [kb.sb(f"S_{pl}", [128, 32, nseg, Lc + 1], F32, st) for pl in range(2)]
            Sb = [kb.sb(f"Sb_{pl}", [128, 32, NK], BF16, st) for pl in range(2)]
            T1 = kb.sb("T1", [128, 32, max(NK // 2, 4)], F32, st); T2 = kb.sb("T2", [128, 32, max(NK // 2, 4)], F32, st)
            gT = kb.sb("gT", [128, 8, N], BF16, st)
            g32 = gT.t[:].rearrange("p a n -> p (a n)").bitcast(F32)
            tw = max(NK // 2, 4)
            T3v = g32[:, 0:32 * tw].rearrange("p (g k) -> p g k", k=tw)
            T4v = g32[:, 32 * tw:64 * tw].rearrange("p (g k) -> p g k", k=tw)
            T3 = gT; T4 = gT
            yy = [kb.sb(f"yy{k}", [128, N], F32, st) for k in range(2)]
            tq = [kb.sb(f"tq{k}", [128, N], F32, st) for k in range(2)]
            if is_prompt:
                for pl in range(2):
                    kb.cp(pool, S[pl].t[:, :, 0, 0], s5c[pl].t[:], [s5c[pl]], [S[pl]])
            else:
                natS = kb.sb("natS", [32, 2, NS, 128], F32, st)
                kb.dma(sp, natS.t[:, 0, :, :], sre.t.rearrange("s g q -> g s q"), natS, wsrc)
                kb.dma(sp, natS.t[:, 1, :, :], sim.t.rearrange("s g q -> g s q"), natS, wsrc)
                for pl in range(2):
                    for s_ in range(NS):
                        tr_f32(S[pl].t[:, :, s_, 0], S[pl], natS.t[:, pl, s_, :], natS, 32)
            hv_all = hT.t[:, :, 0:N].rearrange("p c (k t) -> p c k t", t=8)
            if stop == 6:
                return
            import os as _os
            _nct = int(_os.environ.get("P1CT", "99")); _noev = _os.environ.get("P1NOEV") == "1"
            for half in range(2):
                for c4 in range(4):
                    ct = half * 4 + c4
                    if ct >= _nct:
                        continue
                    slot = ws.get(W["s5f"], ct)
                    Fv = slot.t[:, 0:2048].rearrange("p (t a q) -> p t a q", a=2, q=128)
                    for j in range(4):
                        r = slice(32 * j, 32 * j + 32)
                        vb = P[4 + j]
                        vbv = vb.t[:, 0:8 * NK].rearrange("p (c a k) -> p c a k", a=2, k=NK)
                        for pl in range(2):
                            for tau in range(8):
                                kb.mm(vbv[:, c4, pl, :], Fv[r, tau, pl, :], hv_all[r, ct, :, tau], [slot, hTs[ct]], [vb],
                                      start=(tau == 0), stop=(tau == 7), inc=(tau == 7 and j == 3 and pl == 1), tile_position=(32 * j, 0))
                for j in range(4):
                    if _noev or _nct < 99:
                        continue
                    vb = P[4 + j]
                    vbv = vb.t[:, 0:8 * NK].rearrange("p (c a k) -> p c a k", a=2, k=NK)
                    g0 = 16 * half + j
                    kb.cp(act, V[0][0].t[:, g0:g0 + 13:4, :], vbv[:, :, 0, :], [vb], [V[0][0]])
                    kb.cp(dve, V[0][1].t[:, g0:g0 + 13:4, :], vbv[:, :, 1, :], [vb], [V[0][1]])
            if stop == 7:
                return

            def cmadd(dre, dim_, dbufs, lv, sre_, sim_, sbufs, vre, vim, vbufs, shape):
                Are, Aim = Apow[lv]
                nfree = 1
                for x in shape[2:]:
                    nfree *= x
                if len(shape) == 3:
                    t1 = T1.t[:, :, 0:nfree]; t2 = T2.t[:, :, 0:nfree]; t3 = T3v[:, :, 0:nfree]; t4 = T4v[:, :, 0:nfree]
                    bre = Are.t[:].unsqueeze(2).to_broadcast(shape); bim = Aim.t[:].unsqueeze(2).to_broadcast(shape)
                else:
                    t1 = T1.t[:, :, 0:nfree].rearrange("p g (s k) -> p g s k", s=shape[2])
                    t2 = T2.t[:, :, 0:nfree].rearrange("p g (s k) -> p g s k", s=shape[2])
                    t3 = T3v[:, :, 0:nfree].rearrange("p g (s k) -> p g s k", s=shape[2])
                    t4 = T4v[:, :, 0:nfree].rearrange("p g (s k) -> p g s k", s=shape[2])
                    bre = Are.t[:].unsqueeze(2).unsqueeze(3).to_broadcast(shape); bim = Aim.t[:].unsqueeze(2).unsqueeze(3).to_broadcast(shape)
                kb.tt(pool, t1, sre_, bre, ALU.mult, sbufs + [Are], [T1])
                kb.tt(pool, t2, sim_, bim, ALU.mult, sbufs + [Aim], [T2])
                kb.tt(pool, t1, t1, t2, ALU.subtract, [T1, T2], [T1])
                kb.tt(dve, t3, sim_, bre, ALU.mult, sbufs + [Are], [T3])
                kb.tt(dve, t4, sre_, bim, ALU.mult, sbufs + [Aim], [T4])
                kb.tt(dve, t3, t3, t4, ALU.add, [T3, T4], [T3])
                kb.tt(pool, dre, t1, vre, ALU.add, [T1] + vbufs, [dbufs[0]])
                kb.tt(dve, dim_, t3, vim, ALU.add, [T3] + vbufs, [dbufs[1]])

            ybanks = [P[6], P[7], P[0], P[1], P[2], P[3], P[4], P[5]]

            def toep(ct, yb):
                slot = ws.get(W["s5k"], ct)
                Kv = slot.t[:, 0:1024].rearrange("p (d c) -> p d c", c=128)
                yv = yb.t[:, 0:N].rearrange("p (k t) -> p k t", t=8)
                kb.mm(yb.t[:, 0:N], Kv[:, 0, :], hT.t[:, ct, 0:N], [slot, hTs[ct]], [yb], start=True, stop=False, inc=False)
                for dl in range(1, 8):
                    for tau in range(dl, 8):
                        kb.mm(yv[:, :, tau], Kv[:, dl, :], hv_all[:, ct, :, tau - dl], [slot, hTs[ct]], [yb], start=False, stop=False,
                              inc=(dl == 7))

            def gpart(ct, yb):
                slot = ws.get(W["s5g"], ct)
                Gv = slot.t[:, 0:2048].rearrange("p (j t a c) -> p j t a c", j=4, t=8, a=2)
                yv = yb.t[:, 0:N].rearrange("p (k t) -> p k t", t=8)
                for j in range(4):
                    for pl in range(2):
                        for tau in range(8):
                            last = (j == 3 and pl == 1 and tau == 7)
                            kb.mm(yv[32 * j:32 * j + 32, :, tau], Gv[:, j, tau, pl, :], Sb[pl].t[:, 4 * ct + j, :], [slot, Sb[pl]], [yb],
                                  start=False, stop=(pl == 1 and tau == 7), inc=last, tile_position=(0, 32 * j))
                y_ = yy[ct % 2]; t_ = tq[ct % 2]
                kb.stt(y_.t[:], hT.t[:, ct, 0:N], v_d.t[:, ct:ct + 1], yb.t[:, 0:N], ALU.mult, ALU.add, [hTs[ct], v_d, yb], [y_])
                gelu(gT.t[:, ct, :], gT, y_.t[:], y_, t_.t[:], t_)

            for ct in range(8):
                toep(ct, ybanks[ct])

            for l in range(nl):
                n2 = NK >> (l + 1)
                src = [V[l][pl].t[:].rearrange("p g (k two) -> p g k two", two=2) for pl in range(2)]
                cmadd(V[l + 1][0].t[:], V[l + 1][1].t[:], V[l + 1], 8 << l, src[0][:, :, :, 0], src[1][:, :, :, 0], V[l],
                      src[0][:, :, :, 1], src[1][:, :, :, 1], V[l], [128, 32, n2])
            nt = Lc // Ltop
            vtop = [V[nl][pl].t[:].rearrange("p g (s u) -> p g s u", s=nseg) for pl in range(2)]
            for u in range(nt):
                cmadd(S[0].t[:, :, :, (u + 1) * Ltop], S[1].t[:, :, :, (u + 1) * Ltop], S, 8 << nl,
                      S[0].t[:, :, :, u * Ltop], S[1].t[:, :, :, u * Ltop], S, vtop[0][:, :, :, u], vtop[1][:, :, :, u], V[nl], [128, 32, nseg])
            for l in range(nl - 1, -1, -1):
                stp = 1 << l
                cnt = Lc // (2 * stp)
                vv = [V[l][pl].t[:].rearrange("p g (s k two) -> p g s k two", s=nseg, two=2) for pl in range(2)]
                dsl = slice(stp, stp + 2 * stp * (cnt - 1) + 1, 2 * stp)
                ssl = slice(0, 2 * stp * (cnt - 1) + 1, 2 * stp)
                cmadd(S[0].t[:, :, :, dsl], S[1].t[:, :, :, dsl], S, 8 << l, S[0].t[:, :, :, ssl], S[1].t[:, :, :, ssl], S,
                      vv[0][:, :, :, :, 0], vv[1][:, :, :, :, 0], V[l], [128, 32, nseg, cnt])
            if stop == 8:
                return
            for pl in range(2):
                kb.cp(act if pl else dve, Sb[pl].t[:].rearrange("p g (s k) -> p g s k", s=nseg), S[pl].t[:, :, :, 0:Lc], [S[pl]], [Sb[pl]])
            if stop == 9:
                return
            for ct in range(8):
                gpart(ct, ybanks[ct])
            if is_prompt:
                for pl in range(2):
                    kb.cp(pool, s5c[pl].t[:], S[pl].t[:, :, 0, Lc], [S[pl]], [s5c[pl]])
            else:
                fin = kb.sb("fin", [128, 2, NS, 32], F32, st)
                fo = kb.sb("fo", [32, 2, NS, 128], F32, st)
                for pl in range(2):
                    kb.cp(pool, fin.t[:, pl, :, :], S[pl].t[:, :, :, Lc].rearrange("p g s -> p s g"), [S[pl]], [fin])
                    for s_ in range(NS):
                        pb = gps()
                        kb.tr(pb.t[0:32, 0:128], fin.t[:, pl, s_, :], identf.t[:], [fin, identf], [pb])
                        kb.cp(dve, fo.t[:, pl, s_, :], pb.t[0:32, 0:128], [pb], [fo])
                kb.dma(sp, o_sre_s.t.rearrange("s g q -> g s q"), fo.t[:, 0, :, :], o_sre_s, fo)
                kb.dma(sp, o_sim_s.t.rearrange("s g q -> g s q"), fo.t[:, 1, :, :], o_sim_s, fo)
            if stages < 1 or stop == 10:
                return
            def glu_c(ot, pb):
                t_ = tq[ot % 2]
                kb.actf(t_.t[:], pb.t[:, 0:N], AF.Sigmoid, [pb, v_bglu], [t_], bias=v_bglu.t[:, ot:ot + 1])
                if ot > 0:
                    sq_tile(ot - 1, N)
                kb.tt(dve, t_.t[:], t_.t[:], gT.t[:, ot, :], ALU.mult, [t_, gT], [t_])
                kb.tt(dve, xT.t[:, ot, 0:N], xT.t[:, ot, 0:N], t_.t[:], ALU.add, [xT, t_], [xT])
            linear_fm("glu", 2, N, lambda kt: gT.t[:, kt, :], [gT], glu_c)
            sq_tile(7, N)

    def memattn(i, N, is_prompt):
        rmsnorm(N, g_mq[i], hTs, presq=True)
        with kb.scope() as st:
            qT = kb.sb("qT", [128, 8, N], BF16, st); oT = kb.sb("oT", [128, 8, N], BF16, st)
            E = [kb.sb(f"E{k}", [128, 2, N], BF16, st) for k in range(2)]
            rds = [kb.sb(f"rd{k}", [128, N], F32, st) for k in range(2)]
            if is_prompt:
                segs = [(0, N, kTm[i], vm[i])]
            else:
                segs = []
                Kf = [kb.sb(f"Kf{k}", [128, 2, D], F32, st) for k in range(2)]
                for s_ in range(NS):
                    kTs = kb.sb(f"kTs{s_}", [128, 8, 256], BF16, st); vs_ = kb.sb(f"vs{s_}", [128, 2, D], BF16, st)
                    kf = Kf[0]; vf = Kf[1]
                    kb.dma(sp, kf.t[:], cmk.t[i, s_].rearrange("(mt p) d -> p mt d", p=128), kf, wsrc)
                    kb.dma(sp, vf.t[:], cmv.t[i, s_].rearrange("(mt p) d -> p mt d", p=128), vf, wsrc)
                    kb.cp(pool, vs_.t[:], vf.t[:], [vf], [vs_])
                    for mt in range(2):
                        for half in range(2):
                            pb = gps()
                            for k4 in range(4):
                                c = half * 4 + k4
                                kb.tr(pb.t[:, k4 * 128:(k4 + 1) * 128], kf.t[:, mt, c * 128:(c + 1) * 128], identf.t[:], [kf, identf], [pb], inc=(k4 == 3))
                            kb.cp(act if half else dve, kTs.t[:, half * 4:half * 4 + 4, mt * 128:(mt + 1) * 128],
                                  pb.t[:, :].rearrange("p (a b) -> p a b", b=128), [pb], [kTs])
                    segs.append((LS * s_, LS, kTs, vs_))
            linear_fm(f"mq{i}", 2, N, lambda kt: hT.t[:, kt, 0:N], lambda kt: [hTs[kt]],
                      lambda ot, pb: kb.actf(qT.t[:, ot, :], pb.t[:, 0:N], AF.Copy, [pb], [qT], scale=1.0 / 16.0))
            abank_i = [0]

            def abank():
                bnk = P[abank_i[0] % 8]
                abank_i[0] += 1
                return bnk

            pss_h = {}

            def scores(h):
                pss = [abank(), abank()]
                pss_h[h] = pss
                for (c0, n, kT_b, v_b) in segs:
                    for mt in range(2):
                        for dt_ in range(2):
                            kb.mm(pss[mt].t[:, c0:c0 + n], kT_b.t[:, 2 * h + dt_, mt * 128:(mt + 1) * 128], qT.t[:, 2 * h + dt_, c0:c0 + n],
                                  [kT_b, qT], [pss[mt]], start=(dt_ == 0), stop=(dt_ == 1), inc=(dt_ == 1))
                Eh = E[h % 2]
                for mt in range(2):
                    kb.actf(Eh.t[:, mt, :], pss[mt].t[:, 0:N], AF.Exp, [pss[mt]], [Eh])

            def pv(h):
                Eh = E[h % 2]
                rdh = rds[h % 2]
                pd = abank()
                for mt in range(2):
                    kb.mm(pd.t[:, 0:N], onesb.t[:], Eh.t[:, mt, :], [onesb, Eh], [pd], start=(mt == 0), stop=(mt == 1), inc=(mt == 1))
                kb.recip(rdh.t[:], pd.t[:, 0:N], [pd], [rdh])
                for dt_ in range(2):
                    po = abank()
                    for (c0, n, kT_b, v_b) in segs:
                        for mt in range(2):
                            kb.mm(po.t[:, c0:c0 + n], v_b.t[:, mt, (2 * h + dt_) * 128:(2 * h + dt_ + 1) * 128], Eh.t[:, mt, c0:c0 + n],
                                  [v_b, Eh], [po], start=(mt == 0), stop=(mt == 1), inc=(mt == 1))
                    kb.tt(dve, oT.t[:, 2 * h + dt_, :], po.t[:, 0:N], rdh.t[:], ALU.mult, [po, rdh], [oT])

            scores(0)
            for h in range(4):
                if h + 1 < 4:
                    scores(h + 1)
                pv(h)
            linear_fm(f"mo{i}", 2, N, lambda kt: oT.t[:, kt, :], [oT], add_to_x(N, presq=True))

    def ffn(i, N, nseg, carry):
        L = N // nseg
        NB = 4
        rmsnorm(N, g_ffn[i], hTs, presq=True)
        with kb.scope() as st:
            actTs = [kb.sb(f"actT{c}", [128, 4, N], BF16, st) for c in range(6)]
            aext = [kb.sb(f"aext{k}", [128, nseg, L + 2], F32, st) for k in range(NB)]
            tcs = [kb.sb(f"tc{k}", [128, nseg, L], F32, st) for k in range(NB)]
            tgs = [kb.sb(f"tg{k}", [128, nseg, L], F32, st) for k in range(NB)]
            gss = [kb.sb(f"gs{k}", [128, N], BF16, st) for k in range(NB)]
            scr = W[f"up{i}"]
            cur = {}
            bank_i = [0]

            def bank():
                bnk = P[bank_i[0] % 8]
                bank_i[0] += 1
                return bnk

            def stage_a(ft):
                c, u = ft // 2, ft % 2
                if u == 0:
                    cur["slot"] = ws.get(scr, c)
                slot = cur["slot"]
                wv = slot.t[:, 0:4096].rearrange("p (k o) -> p k o", o=512)
                pa = bank(); pg = bank()
                for kt in range(8):
                    kb.mm(pa.t[:, 0:N], wv[:, kt, u * 128:(u + 1) * 128], hT.t[:, kt, 0:N], [slot, hTs[kt]], [pa], start=(kt == 0), stop=(kt == 7), inc=(kt == 7))
                for kt in range(8):
                    kb.mm(pg.t[:, 0:N], wv[:, kt, 256 + u * 128:256 + (u + 1) * 128], hT.t[:, kt, 0:N], [slot, hTs[kt]], [pg], start=(kt == 0), stop=(kt == 7), inc=(kt == 7))
                ae = aext[ft % NB]; tcb = tcs[ft % NB]; gs = gss[ft % NB]
                kb.cp(act, ae.t[:, :, 0:2], carry.t[:, ft, :, :], [carry], [ae])
                kb.actf(ae.t[:, :, 2:L + 2], pa.t[:, 0:N].rearrange("p (s l) -> p s l", l=L), AF.Copy, [pa], [ae])
                kb.cp(act, gs.t[:], pg.t[:, 0:N], [pg], [gs])
                kb.cp(act, carry.t[:, ft, :, :], ae.t[:, :, L:L + 2], [ae], [carry])
                kb.actf(tcb.t[:], ae.t[:, :, 2:L + 2], AF.Identity, [ae, cw[i][2], cb[i]], [tcb], scale=cw[i][2].t[:, ft:ft + 1], bias=cb[i].t[:, ft:ft + 1])
                kb.stt(tcb.t[:], ae.t[:, :, 1:L + 1], cw[i][1].t[:, ft:ft + 1], tcb.t[:], ALU.mult, ALU.add, [ae, cw[i][1], tcb], [tcb])
                kb.stt(tcb.t[:], ae.t[:, :, 0:L], cw[i][0].t[:, ft:ft + 1], tcb.t[:], ALU.mult, ALU.add, [ae, cw[i][0], tcb], [tcb])

            def stage_b(ft):
                tcb = tcs[ft % NB]; tgb = tgs[ft % NB]
                kb.actf(tgb.t[:], tcb.t[:], AF.Square, [tcb], [tgb], scale=math.sqrt(0.044715))
                kb.stt(tgb.t[:], tgb.t[:], 1.0, tcb.t[:], ALU.add, ALU.mult, [tgb, tcb], [tgb])

            def stage_c(ft):
                tcb = tcs[ft % NB]; tgb = tgs[ft % NB]; gs = gss[ft % NB]
                kb.actf(tgb.t[:], tgb.t[:], AF.Sigmoid, [tgb], [tgb], scale=GELU_C)
                kb.tt(pool, tgb.t[:], tgb.t[:], tcb.t[:], ALU.mult, [tgb, tcb], [tgb])
                kb.tt(pool, actTs[ft // 4].t[:, ft % 4, :].rearrange("p (s l) -> p s l", l=L), tgb.t[:], gs.t[:].rearrange("p (s l) -> p s l", l=L),
                      ALU.mult, [tgb, gs], [actTs[ft // 4]])

            dscr = W[f"down{i}"]

            def down_chunk(c):
                slot = ws.get(dscr, c)
                wv = slot.t[:, 0:4096].rearrange("p (k o) -> p k o", o=1024)
                nk = 4 if c < 5 else 2
                for ot in range(8):
                    for k in range(nk):
                        ft = 4 * c + k
                        kb.mm(P[ot].t[:, 0:N], wv[:, k, ot * 128:(ot + 1) * 128], actTs[c].t[:, k, :], [slot, actTs[c]], [P[ot]],
                              start=(ft == 0), stop=(ft == NFT - 1), inc=(k == nk - 1 and (ot == 7 or c == 5)))

            for step in range(NFT + 2):
                if step < NFT:
                    stage_a(step)
                if 0 <= step - 1 < NFT:
                    stage_b(step - 1)
                if 0 <= step - 2 < NFT:
                    stage_c(step - 2)
                if step == NFT - 1:
                    for c in range(5):
                        down_chunk(c)
            down_chunk(5)
            for ot in range(8):
                add_to_x(N)(ot, P[ot])

    def retention(N, is_prompt, blk):
        nmt = N // 128
        rmsnorm(N, g_mix[1], hTs)
        with kb.scope() as st:
            Cb = kb.sb("Cb", [128, N], F32, st); Sbt = kb.sb("Sbt", [128, N], F32, st)
            Ck = kb.sb("Ck", [128, N], F32, st); Sk = kb.sb("Sk", [128, N], F32, st)
            tA = kb.sb("tA", [128, N], F32, st); tB = kb.sb("tB", [128, N], F32, st)
            tC = kb.sb("tC", [128, N], F32, st); tD = kb.sb("tD", [128, N], F32, st)
            tE = kb.sb("tE", [128, N], F32, st); tF = kb.sb("tF", [128, N], F32, st)
            ygT = kb.sb("ygT", [128, 16, N], BF16, st)
            qTh = [kb.sb(f"qTh{k}", [128, 2, N], BF16, st) for k in range(2)]
            kTh = [kb.sb(f"kTh{k}", [128, 2, N], BF16, st) for k in range(2)]
            kTf = [kb.sb("kTf0", [128, 2, N], F32, st)] * 2
            vTok = [kb.sb(f"vTok{k}", [128, nmt, 512], BF16, st) for k in range(2)]
            kTok = [kb.sb(f"kTok{k}", [128, nmt, 256], BF16, st) for k in range(2)]
            PT = [kb.sb(f"PT{k}", [128, nmt, N], BF16, st) for k in range(2)]
            of32 = [kb.sb(f"of32_{k}", [128, 4, N], F32, st) for k in range(1)]; osq = kb.sb("osq", [128, 4, N], BF16, st)
            mean = kb.sb("mean", [128, N], F32, st); var = kb.sb("var", [128, N], F32, st)
            nseg = 1 if is_prompt else NS
            Lg = N // nseg
            S16 = [kb.sb(f"S16_{s_}", [128, 2, 512], BF16, st) for s_ in range(nseg)]
            Ssf = None if is_prompt else [kb.sb(f"Ssf_{s_}", [128, 2, 512], F32, st) for s_ in range(NS)]
            col = blk if is_prompt else 8
            if is_prompt:
                c0v = C0.t[:, 0:N]; s0v = S0.t[:, 0:N]
                vw = lambda b: b.t[:]
            else:
                c0v = C0.t[:, 0:LS].unsqueeze(1).to_broadcast([128, NS, LS]); s0v = S0.t[:, 0:LS].unsqueeze(1).to_broadcast([128, NS, LS])
                vw = lambda b: b.t[:].rearrange("p (s l) -> p s l", l=LS)
            kb.ts(dve, vw(tA), s0v, dS.t[:, col:col + 1], ALU.mult, [S0, dS], [tA])
            kb.stt(vw(Cb), c0v, dC.t[:, col:col + 1], vw(tA), ALU.mult, ALU.subtract, [C0, dC, tA], [Cb])
            kb.ts(dve, vw(tB), c0v, dS.t[:, col:col + 1], ALU.mult, [C0, dS], [tB])
            kb.stt(vw(Sbt), s0v, dC.t[:, col:col + 1], vw(tB), ALU.mult, ALU.add, [S0, dC, tB], [Sbt])
            kb.ts(dve, Ck.t[:], Cb.t[:], 1.0 / 16.0, ALU.mult, [Cb], [Ck])
            kb.ts(dve, Sk.t[:], Sbt.t[:], 1.0 / 16.0, ALU.mult, [Sbt], [Sk])
            gpow = [math.exp(LG[h] * Lg) for h in range(4)]
            def proj(h):
                qh = qTh[h % 2]; kh = kTh[h % 2]; kf = kTf[h % 2]; vt = vTok[h % 2]; kt_ = kTok[h % 2]; pt = PT[h % 2]
                inner = None
                inner = innerp.t[:, h, 0:N] if is_prompt else inners.t[:, h, :]
                slot = ws.get(W["rq"], h)
                wv = slot.t[:, 0:2048].rearrange("p (k o) -> p k o", o=256)
                p1 = gps(); p2 = gps()
                for kt in range(8):
                    kb.mm(p1.t[:, 0:N], wv[:, kt, 0:128], hT.t[:, kt, 0:N], [slot, hTs[kt]], [p1], start=(kt == 0), stop=(kt == 7), inc=(kt == 7))
                for kt in range(8):
                    kb.mm(p2.t[:, 0:N], wv[:, kt, 128:256], hT.t[:, kt, 0:N], [slot, hTs[kt]], [p2], start=(kt == 0), stop=(kt == 7), inc=(kt == 7))
                kb.tt(dve, tA.t[:], p1.t[:, 0:N], Cb.t[:], ALU.mult, [p1, Cb], [tA])
                kb.tt(dve, tB.t[:], p2.t[:, 0:N], Sbt.t[:], ALU.mult, [p2, Sbt], [tB])
                kb.tt(dve, tE.t[:], p2.t[:, 0:N], Cb.t[:], ALU.mult, [p2, Cb], [tE])
                kb.tt(dve, tF.t[:], p1.t[:, 0:N], Sbt.t[:], ALU.mult, [p1, Sbt], [tF])
                kb.tt(pool, tA.t[:], tA.t[:], tB.t[:], ALU.subtract, [tA, tB], [tA])
                kb.tt(pool, qh.t[:, 0, :], tA.t[:], inner, ALU.mult, [tA, innerp, inners], [qh])
                kb.tt(pool, tE.t[:], tE.t[:], tF.t[:], ALU.add, [tE, tF], [tE])
                kb.tt(pool, qh.t[:, 1, :], tE.t[:], inner, ALU.mult, [tE, innerp, inners], [qh])
                slot = ws.get(W["rk"], h)
                wv = slot.t[:, 0:2048].rearrange("p (k o) -> p k o", o=256)
                p1 = gps(); p2 = gps()
                for kt in range(8):
                    kb.mm(p1.t[:, 0:N], wv[:, kt, 0:128], hT.t[:, kt, 0:N], [slot, hTs[kt]], [p1], start=(kt == 0), stop=(kt == 7), inc=(kt == 7))
                for kt in range(8):
                    kb.mm(p2.t[:, 0:N], wv[:, kt, 128:256], hT.t[:, kt, 0:N], [slot, hTs[kt]], [p2], start=(kt == 0), stop=(kt == 7), inc=(kt == 7))
                kb.tt(dve, tA.t[:], p1.t[:, 0:N], Ck.t[:], ALU.mult, [p1, Ck], [tA])
                kb.tt(dve, tB.t[:], p2.t[:, 0:N], Sk.t[:], ALU.mult, [p2, Sk], [tB])
                kb.tt(dve, tE.t[:], p2.t[:, 0:N], Ck.t[:], ALU.mult, [p2, Ck], [tE])
                kb.tt(dve, tF.t[:], p1.t[:, 0:N], Sk.t[:], ALU.mult, [p1, Sk], [tF])
                kb.tt(pool, kf.t[:, 0, :], tA.t[:], tB.t[:], ALU.subtract, [tA, tB], [kf])
                kb.tt(pool, kf.t[:, 1, :], tE.t[:], tF.t[:], ALU.add, [tE, tF], [kf])
                kb.cp(act, kh.t[:], kf.t[:], [kf], [kh])
                slot = ws.get(W["rv"], h)
                wv = slot.t[:, 0:4096].rearrange("p (k o) -> p k o", o=512)
                for mt in range(nmt):
                    pb = gps()
                    for kt in range(8):
                        kb.mm(pb.t[:, 0:512], hT.t[:, kt, mt * 128:(mt + 1) * 128], wv[:, kt, :], [slot, hTs[kt]], [pb], start=(kt == 0), stop=(kt == 7), inc=(kt == 7))
                    kb.cp(act, vt.t[:, mt, :], pb.t[:, 0:512], [pb], [vt])
                slot = ws.get(W["rg"], h)
                wv = slot.t[:, 0:4096].rearrange("p (k o) -> p k o", o=512)
                for et in range(4):
                    pb = gps()
                    for kt in range(8):
                        kb.mm(pb.t[:, 0:N], wv[:, kt, et * 128:(et + 1) * 128], hT.t[:, kt, 0:N], [slot, hTs[kt]], [pb], start=(kt == 0), stop=(kt == 7), inc=(kt == 7))
                    kb.actf(ygT.t[:, 4 * h + et, :], pb.t[:, 0:N], AF.Silu, [pb], [ygT])
                for mt in range(nmt):
                    pb = gps()
                    for dt_ in range(2):
                        kb.tr(pb.t[:, dt_ * 128:(dt_ + 1) * 128], kf.t[:, dt_, mt * 128:(mt + 1) * 128], identf.t[:], [kf, identf], [pb], inc=(dt_ == 1))
                    tl = tailp.t[:, h, mt:mt + 1] if is_prompt else tails.t[:, h:h + 1]
                    kb.actf(kt_.t[:, mt, :], pb.t[:, 0:256], AF.Identity, [pb, tailp, tails], [kt_], scale=tl)
                for mt in range(nmt):
                    lo = 128 * mt if is_prompt else 0
                    pb = gps()
                    for dt_ in range(2):
                        kb.mm(pb.t[:, 0:N - lo], kh.t[:, dt_, mt * 128:(mt + 1) * 128], qh.t[:, dt_, lo:N], [kh, qh], [pb],
                              start=(dt_ == 0), stop=(dt_ == 1), inc=(dt_ == 1))
                    if is_prompt:
                        kb.stt(pt.t[:, mt, lo:N], pb.t[:, 0:N - lo], tinvp.t[:, h, mt:mt + 1], M01.t[:, 0:N - lo], ALU.mult, ALU.mult, [pb, tinvp, M01], [pt])
                    else:
                        kb.stt(pt.t[:, mt, :], pb.t[:, 0:N], tinvs.t[:, h:h + 1], M01s.t[:], ALU.mult, ALU.mult, [pb, tinvs, M01s], [pt])
            def attn_a(h):
                qh = qTh[h % 2]; kh = kTh[h % 2]; kf = kTf[h % 2]; vt = vTok[h % 2]; kt_ = kTok[h % 2]; pt = PT[h % 2]
                if is_prompt:
                    kb.cp(act, S16[0].t[:], Sp.t[:, h, :, :], [Sp], [S16[0]])
                else:
                    for s_ in range(NS):
                        kb.dma(sp, Ssf[s_].t[:], sret.t[s_, h].rearrange("(dt p) e -> p dt e", p=128), Ssf[s_], wsrc)
                        kb.cp(act, S16[s_].t[:], Ssf[s_].t[:], [Ssf[s_]], [S16[s_]])
                for et in range(4):
                    ob = P[4 + et]
                    for mt in range(nmt):
                        lo = 128 * mt if is_prompt else 0
                        kb.mm(ob.t[:, lo:N], vt.t[:, mt, et * 128:(et + 1) * 128], pt.t[:, mt, lo:N], [vt, pt], [ob], start=(mt == 0), stop=False, inc=False)
                    for s_ in range(nseg):
                        for dt_ in range(2):
                            last = (s_ == nseg - 1 and dt_ == 1)
                            kb.mm(ob.t[:, s_ * Lg:(s_ + 1) * Lg], S16[s_].t[:, dt_, et * 128:(et + 1) * 128], qh.t[:, dt_, s_ * Lg:(s_ + 1) * Lg],
                                  [S16[s_], qh], [ob], start=False, stop=last, inc=last)
                of = of32[0]
                for et in range(4):
                    kb.cp(act, of.t[:, et, :], P[4 + et].t[:, 0:N], [P[4 + et]], [of])
                    kb.actf(osq.t[:, et, :], P[4 + et].t[:, 0:N], AF.Square, [P[4 + et]], [osq])
                for s_ in range(nseg):
                    for dt_ in range(2):
                        pb = gps()
                        if is_prompt:
                            for mt in range(nmt):
                                kb.mm(pb.t[:, 0:512], kt_.t[:, mt, dt_ * 128:(dt_ + 1) * 128], vt.t[:, mt, :], [kt_, vt], [pb],
                                      start=(mt == 0), stop=(mt == nmt - 1), inc=(mt == nmt - 1))
                            sdst = Sp.t[:, h, dt_, :]; sbuf_ = Sp
                        else:
                            r = slice(32 * s_, 32 * s_ + 32)
                            kb.mm(pb.t[:, 0:512], kt_.t[r, 0, dt_ * 128:(dt_ + 1) * 128], vt.t[r, 0, :], [kt_, vt], [pb], start=True, stop=True,
                                  tile_position=(32 * s_, 0))
                            sdst = Ssf[s_].t[:, dt_, :]; sbuf_ = Ssf[s_]
                        kb.stt(sdst, sdst, gpow[h], pb.t[:, 0:512], ALU.mult, ALU.add, [sbuf_, pb], [sbuf_])
                    if not is_prompt:
                        kb.dma(sp, o_ret_s.t[s_, h].rearrange("(dt p) e -> p dt e", p=128), Ssf[s_].t[:], o_ret_s, Ssf[s_])
            def attn_b(h):
                qh = qTh[h % 2]; kh = kTh[h % 2]; kf = kTf[h % 2]; vt = vTok[h % 2]; kt_ = kTok[h % 2]; pt = PT[h % 2]
                of = of32[0]
                psm = gps(); psq = gps()
                for et in range(4):
                    kb.mm(psm.t[:, 0:N], onesf.t[:], of.t[:, et, :], [onesf, of], [psm], start=(et == 0), stop=(et == 3), inc=(et == 3))
                for et in range(4):
                    kb.mm(psq.t[:, 0:N], onesb.t[:], osq.t[:, et, :], [onesb, osq], [psq], start=(et == 0), stop=(et == 3), inc=(et == 3))
                kb.actf(mean.t[:], psm.t[:, 0:N], AF.Copy, [psm], [mean], scale=1.0 / 512.0)
                kb.tt(dve, tD.t[:], mean.t[:], mean.t[:], ALU.mult, [mean], [tD])
                kb.stt(var.t[:], psq.t[:, 0:N], 1.0 / 512.0, tD.t[:], ALU.mult, ALU.subtract, [psq, tD], [var])
                kb.actf(var.t[:], var.t[:], AF.Sqrt, [var, epsb], [var], bias=epsb.t[:, 1:2])
                kb.recip(var.t[:], var.t[:], [var], [var])
                for et in range(4):
                    tcx = tC
                    kb.tt(dve, tcx.t[:], of.t[:, et, :], mean.t[:], ALU.subtract, [of, mean], [tcx])
                    kb.tt(dve, tcx.t[:], tcx.t[:], var.t[:], ALU.mult, [tcx, var], [tcx])
                    kb.tt(pool, ygT.t[:, 4 * h + et, :], tcx.t[:], ygT.t[:, 4 * h + et, :], ALU.mult, [tcx, ygT], [ygT])
            proj(0)
            proj(1)
            attn_a(0)
            proj(2)
            attn_b(0)
            attn_a(1)
            proj(3)
            attn_b(1)
            attn_a(2)
            attn_b(2)
            attn_a(3)
            attn_b(3)
            linear_fm("ro", 4, N, lambda kt: ygT.t[:, kt, :], [ygT], add_to_x(N, presq=True))

    with kb.scope() as st:
        mtk = kb.sb("memt", [128, 2, D], F32, st)
        kb.dma(sp, mtk.t[:], mem.t.rearrange("(mt p) d -> p mt d", p=128), mtk, wsrc)
        ssq = kb.sb("ssq", [128, 2], F32, st); junk = kb.sb("junk", [128, D], BF16, st)
        for mt in range(2):
            kb.actf(junk.t[:], mtk.t[:, mt, :], AF.Square, [mtk], [junk, ssq], accum_out=ssq.t[:, mt:mt + 1])
        kb.actf(ssq.t[:], ssq.t[:], AF.Sqrt, [ssq, epsb], [ssq], scale=1.0 / D, bias=epsb.t[:, 0:1])
        kb.recip(ssq.t[:], ssq.t[:], [ssq], [ssq])
        for mt in range(2):
            kb.ts(dve, mtk.t[:, mt, :], mtk.t[:, mt, :], ssq.t[:, mt:mt + 1], ALU.mult, [mtk, ssq], [mtk])
        memnT = kb.sb("memnT", [128, 8, 256], F32, st)
        for mt in range(2):
            for half in range(2):
                pb = gps()
                for k4 in range(4):
                    ct = half * 4 + k4
                    kb.tr(pb.t[:, k4 * 128:(k4 + 1) * 128], mtk.t[:, mt, ct * 128:(ct + 1) * 128], identf.t[:], [mtk, identf], [pb], inc=(k4 == 3))
                kb.cp(act if half else dve, memnT.t[:, half * 4:half * 4 + 4, mt * 128:(mt + 1) * 128],
                      pb.t[:, :].rearrange("p (a b) -> p a b", b=128), [pb], [memnT])
        memhT = kb.sb("memhT", [128, 8, 256], BF16, st)
        wsl = [kb.sb(f"wsl{k}", [128, 4096], BF16, st) for k in range(2)]
        kof = [kb.sb(f"kof{k}", [128, 512], F32, st) for k in range(2)]
        nko = 0
        for i in range(2):
            for ct in range(8):
                kb.ts(dve, memhT.t[:, ct, :], memnT.t[:, ct, :], g_mkv[i].t[:, ct:ct + 1], ALU.mult, [memnT, g_mkv[i]], [memhT])
            scr = W[f"mkv{i}"]
            for c in range(4):
                sl = wsl[c % 2]
                scr.load(kb, sl, c, 4096)
                wv = sl.t[:].rearrange("p (k o) -> p k o", o=512)
                if c < 2:
                    for o_ in range(4):
                        pb = gps()
                        for kt in range(8):
                            kb.mm(pb.t[:, 0:256], wv[:, kt, o_ * 128:(o_ + 1) * 128], memhT.t[:, kt, :], [sl, memhT], [pb], start=(kt == 0), stop=(kt == 7), inc=(kt == 7))
                        kb.cp(act, kTm[i].t[:, c * 4 + o_, :], pb.t[:, 0:256], [pb], [kTm[i]])
                for mt in range(2):
                    pb = gps()
                    for kt in range(8):
                        kb.mm(pb.t[:, 0:512], memhT.t[:, kt, mt * 128:(mt + 1) * 128], wv[:, kt, :], [sl, memhT], [pb], start=(kt == 0), stop=(kt == 7), inc=(kt == 7))
                    ko = kof[nko % 2]; nko += 1
                    kb.cp(dve, ko.t[:], pb.t[:, 0:512], [pb], [ko])
                    dsto = o_mk if c < 2 else o_mv
                    kb.dma(sp, dsto.t[i, mt * 128:(mt + 1) * 128, (c % 2) * 512:(c % 2 + 1) * 512], ko.t[:], dsto, ko)
                    if c >= 2:
                        kb.cp(act, vm[i].t[:, mt, (c - 2) * 512:(c - 1) * 512], pb.t[:, 0:512], [pb], [vm[i]])

    if stop == 4:
        kb.barrier(); return kb
    with kb.scope() as st:
        cvn = [kb.sb(f"cvn{k}", [88, 128], F32, st) for k in range(2)]
        for i in range(2):
            for hh in range(2):
                cn = cvn[hh]
                kb.dma(sp, cn.t[:], cconv.t[i][2 * hh:2 * hh + 2].rearrange("s r (ft p) -> (s r ft) p", p=128), cn, wsrc)
                pb = gps()
                kb.tr(pb.t[:, 0:88], cn.t[:], identf.t[0:88, 0:88], [cn, identf], [pb])
                kb.cp(dve, convs[i].t[:, :, 2 * hh:2 * hh + 2, :], pb.t[:, 0:88].rearrange("p (s r ft) -> p ft s r", s=2, r=2), [pb], [convs[i]])
    cast_done = [False]
    def run_block(N, is_prompt, blk):
        nseg = 1 if is_prompt else NS
        if is_prompt:
            load_x(xp, blk * TB, N)
        else:
            load_x(xs, 0, N)
        if stop == 5:
            return
        s5_block(N, nseg, is_prompt)
        if not cast_done[0]:
            cast_done[0] = True
            cast_group(2)
        if 6 <= stop <= 10:
            return
        if dbg == "mix0" or stages < 1:
            return
        memattn(0, N, is_prompt)
        if dbg == "att0" or stages < 2:
            return
        ffn(0, N, nseg, convc[0] if is_prompt else convs[0])
        if dbg == "ffn0" or stages < 3:
            return
        retention(N, is_prompt, blk)
        if dbg == "mix1" or stages < 4:
            return
        memattn(1, N, is_prompt)
        if dbg == "att1" or stages < 5:
            return
        ffn(1, N, nseg, convc[1] if is_prompt else convs[1])

    order = [("p", 0)] + ([("s", 0)] if do_sample else []) + [("p", b) for b in range(1, nblk)]
    if stages >= 6:
        for _ in order:
            ws.plan(sched_block())
    for kind, b in order:
        if stages < 6:
            full = sched_block()
            cut = {"mix0": 22, "att0": 26, "ffn0": 43, "mix1": 63, "att1": 67}.get(dbg, len(full))
            ws.plan(full[:cut])
        if kind == "p":
            run_block(TB, True, b)
            if dbg is not None and dbg_blk == ("p", b):
                dump_dbg(TB)
            if dbg is None:
                store_y(yp, b * TB, TB)
        else:
            run_block(NS * LS, False, 0)
            if dbg is not None and dbg_blk == ("s", 0):
                dump_dbg(NS * LS)
            if dbg is None:
                store_y(ys, 0, NS * LS)
                with kb.scope() as st:
                    ctmp = [kb.sb(f"ctmp{k}", [128, 88], F32, st) for k in range(2)]
                    cout = [kb.sb(f"cout{k}", [88, 128], F32, st) for k in range(2)]
                    for i in range(2):
                        for hh in range(2):
                            k = (2 * i + hh) % 2
                            kb.cp(dve, ctmp[k].t[:].rearrange("p (s r ft) -> p ft s r", s=2, r=2), convs[i].t[:, :, 2 * hh:2 * hh + 2, :], [convs[i]], [ctmp[k]])
                            pb = gps()
                            kb.tr(pb.t[0:88, 0:128], ctmp[k].t[:], identf.t[:], [ctmp[k], identf], [pb])
                            kb.cp(dve, cout[k].t[:], pb.t[0:88, 0:128], [pb], [cout[k]])
                            kb.dma(sp, o_conv_s.t[i][2 * hh:2 * hh + 2].rearrange("s r (ft p) -> (s r ft) p", p=128), cout[k].t[:], o_conv_s, cout[k])

    with kb.scope() as st:
        fo = kb.sb("fo_p", [32, 2, 128], F32, st)
        for pl in range(2):
            pb = gps()
            kb.tr(pb.t[0:32, 0:128], s5c[pl].t[:], identf.t[:], [s5c[pl], identf], [pb])
            kb.cp(dve, fo.t[:, pl, :], pb.t[0:32, 0:128], [pb], [fo])
        kb.dma(sp, o_sre_p.t[:, :], fo.t[:, 0, :], o_sre_p, fo)
        kb.dma(sp, o_sim_p.t[:, :], fo.t[:, 1, :], o_sim_p, fo)
        for h in range(4):
            kb.dma(sp, o_ret_p.t[h].rearrange("(dt p) e -> p dt e", p=128), Sp.t[:, h, :, :], o_ret_p, Sp)
        for i in range(2):
            ctp = kb.sb(f"ctp{i}", [128, 44], F32, st)
            cop = kb.sb(f"cop{i}", [44, 128], F32, st)
            kb.cp(dve, ctp.t[:].rearrange("p (r ft) -> p ft r", r=2), convc[i].t[:, :, 0, :], [convc[i]], [ctp])
            pb = gps()
            kb.tr(pb.t[0:44, 0:128], ctp.t[:], identf.t[:], [ctp, identf], [pb])
            kb.cp(dve, cop.t[:], pb.t[0:44, 0:128], [pb], [cop])
            kb.dma(sp, o_conv_p.t[i].rearrange("r (ft p) -> (r ft) p", p=128), cop.t[:], o_conv_p, cop)
    kb.barrier(final=True)
    return kb


_W_KEYS = ["norm_mix", "norm_mem_q", "norm_mem_kv", "norm_ffn", "norm_final", "mem_w_q", "mem_w_kv", "mem_w_o",
           "ffn_w_up", "ffn_conv_w", "ffn_conv_b", "ffn_w_down"]
_W0_KEYS = ["ssm_log_dt", "ssm_b_re", "ssm_b_im", "ssm_c_re", "ssm_c_im", "ssm_d", "ssm_w_glu", "ssm_b_glu", "ret_w_qkvg", "ret_w_o"]


def make_in_maps(inputs, cores):
    f = lambda a: np.ascontiguousarray(np.asarray(a, dtype=np.float32))
    shared = {k: f(inputs[k]) for k in _W_KEYS}
    for k in _W0_KEYS:
        shared[k] = f(inputs[k][0])
    shared["ssm_a_re"] = f(inputs["ssm_a_re"][0]).reshape(32, 128)
    shared["ssm_a_im"] = f(inputs["ssm_a_im"][0]).reshape(32, 128)
    maps = []
    for c in cores:
        b = c % 4
        s = slice(NS * c, NS * c + NS)
        m = dict(shared)
        m["xp"] = f(inputs["x_prompt"][b])
        m["xs"] = f(inputs["x_sample"][s]).reshape(NS * LS, D)
        m["mem"] = f(inputs["mem_prompt"][b])
        m["sre"] = f(inputs["state_ssm_re"][0, s]).reshape(NS, 32, 128)
        m["sim"] = f(inputs["state_ssm_im"][0, s]).reshape(NS, 32, 128)
        m["sret"] = f(inputs["state_ret"][0, s])
        m["cmk"] = f(inputs["cache_mem_k"][:, s]).reshape(2, NS, 256, D)
        m["cmv"] = f(inputs["cache_mem_v"][:, s]).reshape(2, NS, 256, D)
        m["cconv"] = f(inputs["cache_conv"][:, s])
        maps.append(m)
    return maps


def kernel(**inputs):
    nc = bass.Bass("TRN2", target_bir_lowering=False)
    build(nc)
    cores = list(range(8))
    res = run_bass_kernel_spmd(nc, make_in_maps(inputs, cores), core_ids=cores)
    R = res.results
    y_prompt = np.stack([R[b]["yp"] for b in range(4)])
    y_sample = np.concatenate([R[c]["ys"].reshape(NS, LS, D) for c in cores])
    re_p = np.stack([R[b]["o_sre_p"].reshape(64, 64) for b in range(4)])[None]
    im_p = np.stack([R[b]["o_sim_p"].reshape(64, 64) for b in range(4)])[None]
    re_s = np.concatenate([R[c]["o_sre_s"].reshape(NS, 64, 64) for c in cores])[None]
    im_s = np.concatenate([R[c]["o_sim_s"].reshape(NS, 64, 64) for c in cores])[None]
    ret_p = np.stack([R[b]["o_ret_p"] for b in range(4)])[None]
    ret_s = np.concatenate([R[c]["o_ret_s"] for c in cores])[None]
    mk_p = np.stack([R[b]["o_mk"].reshape(2, 256, 4, 256) for b in range(4)], axis=1)
    mv_p = np.stack([R[b]["o_mv"].reshape(2, 256, 4, 256) for b in range(4)], axis=1)
    conv_p = np.stack([R[b]["o_conv_p"] for b in range(4)], axis=1)
    conv_s = np.concatenate([R[c]["o_conv_s"] for c in cores], axis=1)
    outs = (y_prompt, y_sample, re_p, im_p, re_s, im_s, ret_p, ret_s, mk_p, mv_p, conv_p, conv_s)
    return tuple(np.ascontiguousarray(o, dtype=np.float32) for o in outs)
```
